# Optimizing a Trainium2 kernel written in Bass

```python
import math
import jax
import jax.numpy as jnp
from jax import lax
import numpy as np

D_MODEL = 2048
BATCH = 4
SEQ = 4096
DEPTH = 4

HEAD_DIM = 128
GRID_W = 64
ROPE_THETA = 10000.0
EPS = 1e-6
NEG_INF = -1e30

A_HEADS = 6
A_PATTERNS = ((128, 1), (512, 4), (2048, 16))
A_BAND_BLOCK = 64
B_Q_HEADS = 8
B_KV_HEADS = 2
B_Q_BLOCK = 128
C_HEADS = 8
C_QK_DIM = 64
C_V_DIM = 128
C_CHUNK = 64
C_CONV = 5
D_FF = 4 * D_MODEL
N_BRANCHES = 3

A_WIDTH = A_HEADS * HEAD_DIM
B_Q_WIDTH = B_Q_HEADS * HEAD_DIM
B_KV_WIDTH = B_KV_HEADS * HEAD_DIM
C_QK_WIDTH = C_HEADS * C_QK_DIM
C_V_WIDTH = C_HEADS * C_V_DIM
C_GATES = 4 * C_HEADS
IN_SPLITS = (A_WIDTH, A_WIDTH, A_WIDTH, B_Q_WIDTH, B_KV_WIDTH, B_KV_WIDTH, 2 * C_QK_WIDTH, C_V_WIDTH, C_V_WIDTH, C_GATES, N_BRANCHES * D_MODEL)
IN_COLS = sum(IN_SPLITS)

kernel_name = 'hybrid_dilated_gqa_mlstm_encoder'


def rms_norm(x, g):
    xf = x.astype(jnp.float32)
    y = xf * lax.rsqrt(jnp.mean(xf * xf, axis=-1, keepdims=True) + EPS)
    return (y * g.astype(jnp.float32)).astype(x.dtype)


def rope_tables(pos, dim, dtype):
    inv_freq = ROPE_THETA ** (-jnp.arange(0, dim, 2, dtype=jnp.float32) / dim)
    ang = pos.astype(jnp.float32)[:, None] * inv_freq[None, :]
    return jnp.cos(ang).astype(dtype), jnp.sin(ang).astype(dtype)


def apply_rope(x, cos, sin):
    half = x.shape[-1] // 2
    x1, x2 = x[..., :half], x[..., half:]
    c, s = cos[None, :, None, :], sin[None, :, None, :]
    return jnp.concatenate([x1 * c - x2 * s, x2 * c + x1 * s], axis=-1)


def dilated_band_attention(q, k, v, dilation, reach):
    bsz, seq, heads, hd = q.shape
    L = seq // dilation
    blk = A_BAND_BLOCK
    nb = -(-L // blk)
    Lp = nb * blk

    def by_residue(t):
        return t.reshape(bsz, L, dilation, heads, hd).transpose(0, 2, 3, 1, 4)

    qr, kr, vr = by_residue(q), by_residue(k), by_residue(v)
    qb = jnp.pad(qr, ((0, 0), (0, 0), (0, 0), (0, Lp - L), (0, 0))).reshape(bsz, dilation, heads, nb, blk, hd)
    pad_k = ((0, 0), (0, 0), (0, 0), (blk, Lp - L + blk), (0, 0))

    def band(t):
        tb = jnp.pad(t, pad_k).reshape(bsz, dilation, heads, nb + 2, blk, hd)
        return jnp.concatenate([tb[:, :, :, :-2], tb[:, :, :, 1:-1], tb[:, :, :, 2:]], axis=4)

    kw, vw = band(kr), band(vr)
    qpos = jnp.arange(nb)[:, None, None] * blk + jnp.arange(blk)[None, :, None]
    kpos = jnp.arange(nb)[:, None, None] * blk + jnp.arange(3 * blk)[None, None, :] - blk
    valid = (jnp.abs(kpos - qpos) <= reach) & (kpos >= 0) & (kpos < L)
    s = jnp.einsum('bdhiqc,bdhikc->bdhiqk', qb, kw).astype(jnp.float32) * (hd ** -0.5)
    s = jnp.where(valid, s, NEG_INF)
    m = jnp.max(s, axis=-1, keepdims=True)
    p = jnp.exp(s - m)
    z = jnp.sum(p, axis=-1, keepdims=True)
    o = jnp.einsum('bdhiqk,bdhikc->bdhiqc', (p / z).astype(v.dtype), vw)
    lse = (m + jnp.log(z))[..., 0]
    o = o.reshape(bsz, dilation, heads, Lp, hd)[:, :, :, :L].transpose(0, 3, 1, 2, 4).reshape(bsz, seq, heads, hd)
    lse = lse.reshape(bsz, dilation, heads, Lp)[:, :, :, :L].transpose(0, 3, 1, 2).reshape(bsz, seq, heads)
    return o, lse


def dilated_mixer(q, k, v, q_gain, k_gain, rope_1d):
    bsz, seq, _ = q.shape
    shp = (bsz, seq, A_HEADS, HEAD_DIM)
    cos, sin = rope_1d
    q = apply_rope(rms_norm(q.reshape(shp), q_gain), cos, sin)
    k = apply_rope(rms_norm(k.reshape(shp), k_gain), cos, sin)
    v = v.reshape(shp)
    outs, lses = [], []
    for window, dilation in A_PATTERNS:
        o, lse = dilated_band_attention(q, k, v, dilation, window // (2 * dilation))
        outs.append(o)
        lses.append(lse)
    w = jax.nn.softmax(jnp.stack(lses), axis=0)
    o = jnp.einsum('gbsh,gbshc->bshc', w, jnp.stack(outs).astype(jnp.float32))
    return o.astype(v.dtype).reshape(bsz, seq, A_WIDTH)


def gqa_axial_mixer(q, k, v, q_gain, k_gain, rope_row, rope_col):
    bsz, seq, _ = q.shape
    half = HEAD_DIM // 2
    group = B_Q_HEADS // B_KV_HEADS

    def axial(t):
        return jnp.concatenate([apply_rope(t[..., :half], *rope_row), apply_rope(t[..., half:], *rope_col)], axis=-1)

    q = axial(rms_norm(q.reshape(bsz, seq, B_Q_HEADS, HEAD_DIM), q_gain))
    k = axial(rms_norm(k.reshape(bsz, seq, B_KV_HEADS, HEAD_DIM), k_gain))
    v = v.reshape(bsz, seq, B_KV_HEADS, HEAD_DIM)
    nb = seq // B_Q_BLOCK
    qb = q.reshape(bsz, nb, B_Q_BLOCK, B_KV_HEADS, group, HEAD_DIM).transpose(1, 0, 3, 4, 2, 5)
    scale = HEAD_DIM ** -0.5

    def attend(q_blk):
        s = jnp.einsum('bhgqc,bkhc->bhgqk', q_blk, k).astype(jnp.float32) * scale
        p = jax.nn.softmax(s, axis=-1).astype(v.dtype)
        return jnp.einsum('bhgqk,bkhc->bhgqc', p, v)

    o = lax.map(attend, qb)
    return o.transpose(1, 0, 4, 2, 3, 5).reshape(bsz, seq, B_Q_WIDTH)


def centred_depthwise_conv(x, w, b):
    chans = x.shape[-1]
    pad = C_CONV // 2
    y = lax.conv_general_dilated(x, w[:, None, :], window_strides=(1,), padding=[(pad, pad)],
                                 dimension_numbers=('NWC', 'WIO', 'NWC'), feature_group_count=chans)
    return y + b


def mlstm_chunkwise(q, k, v, log_f, log_i):
    bsz, heads, seq, dqk = q.shape
    dv = v.shape[-1]
    L = C_CHUNK
    nc = seq // L

    def chunks(t):
        return jnp.moveaxis(t.reshape(bsz, heads, nc, L, *t.shape[3:]), 2, 0)

    tril = jnp.tril(jnp.ones((L, L), dtype=bool))

    def step(carry, inp):
        C, n, m = carry
        qc, kc, vc, lf, ic = inp
        b = jnp.cumsum(lf, axis=-1)
        D = jnp.where(tril, b[..., :, None] - b[..., None, :] + ic[..., None, :], NEG_INF)
        m_inter = b + m[..., None]
        m_t = jnp.maximum(m_inter, jnp.max(D, axis=-1))
        A = jnp.exp(D - m_t[..., None]) * jnp.einsum('bhtd,bhsd->bhts', qc, kc)
        inter = jnp.exp(m_inter - m_t)
        num = jnp.einsum('bhts,bhsv->bhtv', A, vc) + inter[..., None] * jnp.einsum('bhvd,bhtd->bhtv', C, qc)
        den = jnp.sum(A, axis=-1) + inter * jnp.einsum('bhd,bhtd->bht', n, qc)
        h = num / jnp.maximum(jnp.abs(den), jnp.exp(-m_t))[..., None]
        b_last = b[..., -1]
        g = b_last[..., None] - b + ic
        m_new = jnp.maximum(b_last + m, jnp.max(g, axis=-1))
        decay = jnp.exp(b_last + m - m_new)
        wk = jnp.exp(g - m_new[..., None])
        C_new = decay[..., None, None] * C + jnp.einsum('bhsv,bhsd->bhvd', vc * wk[..., None], kc)
        n_new = decay[..., None] * n + jnp.einsum('bhs,bhsd->bhd', wk, kc)
        return (C_new, n_new, m_new), h

    init = (jnp.zeros((bsz, heads, dv, dqk), jnp.float32), jnp.zeros((bsz, heads, dqk), jnp.float32),
            jnp.zeros((bsz, heads), jnp.float32))
    _, h = lax.scan(step, init, (chunks(q), chunks(k), chunks(v), chunks(log_f), chunks(log_i)))
    return jnp.moveaxis(h, 0, 2).reshape(bsz, heads, seq, dv)


def mlstm_mixer(qk_pre, v, o_pre, gate_pre, conv_w, conv_b, gate_b, out_g):
    bsz, seq, _ = qk_pre.shape
    qk = jax.nn.silu(centred_depthwise_conv(qk_pre, conv_w, conv_b))
    to_heads = lambda t, d: t.reshape(bsz, seq, C_HEADS, d).transpose(0, 2, 1, 3).astype(jnp.float32)
    q = to_heads(qk[..., :C_QK_WIDTH], C_QK_DIM)
    k = to_heads(qk[..., C_QK_WIDTH:], C_QK_DIM) * (C_QK_DIM ** -0.5)
    vh = to_heads(v, C_V_DIM)
    gates = (gate_pre + gate_b).astype(jnp.float32).reshape(bsz, seq, 4, C_HEADS).transpose(2, 0, 3, 1)
    i_fw, f_fw, i_bw, f_bw = gates[0], gates[1], gates[2], gates[3]
    h_fw = mlstm_chunkwise(q, k, vh, jax.nn.log_sigmoid(f_fw), i_fw)
    flip = lambda t: jnp.flip(t, axis=2)
    h_bw = flip(mlstm_chunkwise(flip(q), flip(k), flip(vh), jax.nn.log_sigmoid(flip(f_bw)), flip(i_bw)))
    h = (h_fw + h_bw).transpose(0, 2, 1, 3)
    h = h * lax.rsqrt(jnp.mean(h * h, axis=-1, keepdims=True) + EPS)
    h = h.reshape(bsz, seq, C_V_WIDTH) * out_g.astype(jnp.float32)
    return (h * jax.nn.sigmoid(o_pre.astype(jnp.float32))).astype(qk_pre.dtype)


def hybrid_layer(x, rope_1d, rope_row, rope_col, norm1_g, w_in, a_q_gain, a_k_gain, b_q_gain, b_k_gain,
                 c_conv_w, c_conv_b, c_gate_b, c_out_g, w_branch_a, w_branch_b, w_branch_c, w_out,
                 norm2_g, w_up, w_down):
    bsz, seq, _ = x.shape
    xn = rms_norm(x, norm1_g)
    split_points = [int(p) for p in np.cumsum(IN_SPLITS)[:-1]]
    (a_q, a_k, a_v, b_q, b_k, b_v, c_qk, c_v, c_o, c_gate, g) = jnp.split(xn @ w_in, split_points, axis=-1)
    ya = dilated_mixer(a_q, a_k, a_v, a_q_gain, a_k_gain, rope_1d)
    yb = gqa_axial_mixer(b_q, b_k, b_v, b_q_gain, b_k_gain, rope_row, rope_col)
    yc = mlstm_mixer(c_qk, c_v, c_o, c_gate, c_conv_w, c_conv_b, c_gate_b, c_out_g)
    gates = jax.nn.sigmoid(g.astype(jnp.float32)).astype(x.dtype).reshape(bsz, seq, N_BRANCHES, D_MODEL)
    merged = (gates[:, :, 0] * (ya @ w_branch_a) + gates[:, :, 1] * (yb @ w_branch_b)
              + gates[:, :, 2] * (yc @ w_branch_c))
    h = x + merged @ w_out
    u = jnp.maximum(rms_norm(h, norm2_g) @ w_up, 0)
    return h + (u * u) @ w_down


def setup_inputs(seed: int = 0) -> dict:
    key = jax.random.key(seed)
    ks = jax.random.split(key, 20)
    f32 = jnp.float32
    nrm = lambda k, shape, fan_in: jax.random.normal(k, shape, f32) * (fan_in ** -0.5)
    gain = lambda k, shape: 1.0 + 0.02 * jax.random.normal(k, shape, f32)
    f_base = jnp.linspace(3.0, 6.0, C_HEADS, dtype=f32)
    i_base = jnp.zeros((C_HEADS,), f32)
    gate_base = jnp.concatenate([i_base, f_base, i_base, f_base])
    return {
        'x': jax.random.normal(ks[0], (BATCH, SEQ, D_MODEL), f32),
        'norm1_g': gain(ks[1], (DEPTH, D_MODEL)),
        'w_in': nrm(ks[2], (DEPTH, D_MODEL, IN_COLS), D_MODEL),
        'a_q_gain': gain(ks[3], (DEPTH, HEAD_DIM)),
        'a_k_gain': gain(ks[4], (DEPTH, HEAD_DIM)),
        'b_q_gain': gain(ks[5], (DEPTH, HEAD_DIM)),
        'b_k_gain': gain(ks[6], (DEPTH, HEAD_DIM)),
        'c_conv_w': nrm(ks[7], (DEPTH, C_CONV, 2 * C_QK_WIDTH), C_CONV),
        'c_conv_b': 0.02 * jax.random.normal(ks[8], (DEPTH, 2 * C_QK_WIDTH), f32),
        'c_gate_b': gate_base[None, :] + 0.1 * jax.random.normal(ks[9], (DEPTH, C_GATES), f32),
        'c_out_g': gain(ks[10], (DEPTH, C_V_WIDTH)),
        'w_branch_a': nrm(ks[11], (DEPTH, A_WIDTH, D_MODEL), A_WIDTH),
        'w_branch_b': nrm(ks[12], (DEPTH, B_Q_WIDTH, D_MODEL), B_Q_WIDTH),
        'w_branch_c': nrm(ks[13], (DEPTH, C_V_WIDTH, D_MODEL), C_V_WIDTH),
        'w_out': nrm(ks[14], (DEPTH, D_MODEL, D_MODEL), D_MODEL),
        'norm2_g': gain(ks[15], (DEPTH, D_MODEL)),
        'w_up': nrm(ks[16], (DEPTH, D_MODEL, D_FF), D_MODEL),
        'w_down': nrm(ks[17], (DEPTH, D_FF, D_MODEL), D_FF),
    }


def reference(x, norm1_g, w_in, a_q_gain, a_k_gain, b_q_gain, b_k_gain, c_conv_w, c_conv_b, c_gate_b,
              c_out_g, w_branch_a, w_branch_b, w_branch_c, w_out, norm2_g, w_up, w_down):
    seq = x.shape[1]
    rows = seq // GRID_W
    pos = jnp.arange(seq)
    row = jnp.repeat(jnp.arange(rows), GRID_W)
    col = jnp.tile(jnp.arange(GRID_W), rows)
    rope_1d = rope_tables(pos, HEAD_DIM, x.dtype)
    rope_row = rope_tables(row, HEAD_DIM // 2, x.dtype)
    rope_col = rope_tables(col, HEAD_DIM // 2, x.dtype)
    for l in range(DEPTH):
        x = hybrid_layer(x, rope_1d, rope_row, rope_col, norm1_g[l], w_in[l], a_q_gain[l], a_k_gain[l],
                         b_q_gain[l], b_k_gain[l], c_conv_w[l], c_conv_b[l], c_gate_b[l], c_out_g[l],
                         w_branch_a[l], w_branch_b[l], w_branch_c[l], w_out[l], norm2_g[l], w_up[l], w_down[l])
    return x
```

```python
import contextlib
import numpy as np
import concourse.bass as bass
import concourse.mybir as mybir
from concourse.bass_utils import run_bass_kernel_spmd

F32 = mybir.dt.float32
BF16 = mybir.dt.bfloat16
AF = mybir.ActivationFunctionType
ALU = mybir.AluOpType
AX = mybir.AxisListType

S = 4096
D = 2048
DEPTH = 4
NCORES = 8
IN_COLS = 13088
DFF = 8192
EPS = 1e-6
O_AQ, O_AK, O_AV, O_BQ, O_BK, O_BV, O_CQK, O_CV, O_CO, O_CG, O_G = 0, 768, 1536, 2304, 3328, 3584, 3840, 4864, 5888, 6912, 6944


class T:
    def __init__(self, kb, h, name):
        self.h = h
        self.name = name
        self.last_w = None
        self.readers = {}
        self.sem = None
        self.dcnt = 0
        self.kb = kb

    def ap(self):
        return self.h.ap()

    def __getitem__(self, idx):
        return self.h.__getitem__(idx)

    def get_sem(self):
        if self.sem is None:
            self.sem, self.dcnt = self.kb.take_sem(self.name)
        return self.sem


class KB:
    def __init__(self, nc):
        self.nc = nc
        self.engs = {"pe": nc.tensor, "act": nc.scalar, "dve": nc.vector, "pool": nc.gpsimd, "sp": nc.sync}
        self.sem = {e: nc.alloc_semaphore(name="s_" + e) for e in self.engs}
        self.cnt = {e: 0 for e in self.engs}
        self.waited = {e: {} for e in self.engs}
        self.sem_pool = []
        self.live_dma = {}
        self.n_ins = 0
        self.n_wait = 0
        self.uid = 0
        self.stack = None

    def take_sem(self, name):
        if self.sem_pool:
            return self.sem_pool.pop()
        self.uid += 1
        return self.nc.alloc_semaphore(name="d%d" % self.uid), 0

    def sb(self, name, shape, dt):
        self.uid += 1
        h = self.stack.enter_context(self.nc.sbuf_tensor("%s_%d" % (name, self.uid), list(shape), dt))
        t = T(self, h, name)
        self.phase_tiles.append(t)
        return t

    def ps(self, name, shape, dt=F32):
        self.uid += 1
        h = self.stack.enter_context(self.nc.psum_tensor("%s_%d" % (name, self.uid), list(shape), dt))
        t = T(self, h, name)
        self.phase_tiles.append(t)
        return t

    def dram(self, name, shape, dt, kind="Internal"):
        return T(self, self.nc.dram_tensor(name, list(shape), dt, kind=kind), name)

    @contextlib.contextmanager
    def phase(self):
        old = self.stack
        old_tiles = getattr(self, "phase_tiles", None)
        self.phase_tiles = []
        with contextlib.ExitStack() as st:
            self.stack = st
            yield
            self.barrier()
            for t in self.phase_tiles:
                if t.sem is not None:
                    self.sem_pool.append((t.sem, t.dcnt))
                    t.sem = None
        self.stack = old
        self.phase_tiles = old_tiles

    def _wait(self, eng, evs):
        w = self.waited[eng]
        need = {}
        for ev in evs:
            if ev is None:
                continue
            s, v = ev
            if eng == "pe" and s is self.sem["pe"]:
                continue
            if w.get(id(s), 0) >= v:
                continue
            if id(s) not in need or need[id(s)][1] < v:
                need[id(s)] = (s, v)
        for s, v in need.values():
            self.engs[eng].wait_ge(s, v)
            w[id(s)] = v
            self.n_wait += 1

    def _deps(self, reads, writes):
        evs = []
        for t in reads:
            evs.append(t.last_w)
        for t in writes:
            evs.append(t.last_w)
            evs.extend(t.readers.values())
        return evs

    def _add_reader(self, t, ev):
        s, v = ev
        cur = t.readers.get(id(s))
        if cur is None or cur[1] < v:
            t.readers[id(s)] = ev

    def op(self, eng, fn, reads=(), writes=(), track=True):
        self._wait(eng, self._deps(reads, writes))
        ins = fn(self.engs[eng])
        self.n_ins += 1
        if track:
            self.cnt[eng] += 1
            ins.then_inc(self.sem[eng], 1)
            ev = (self.sem[eng], self.cnt[eng])
        else:
            ev = (self.sem[eng], self.cnt[eng] + 1)
        for t in reads:
            self._add_reader(t, ev)
        for t in writes:
            t.last_w = ev
            t.readers = {}
        return ins

    def dma(self, q, out, in_, reads=(), writes=(), waw=True, **kw):
        evs = self._deps(reads, writes)
        if not waw:
            evs = [t.last_w for t in reads]
            for t in writes:
                evs.extend(t.readers.values())
        self._wait(q, evs)
        owner = writes[0]
        sem = owner.get_sem()
        ins = self.engs[q].dma_start(out=out, in_=in_, **kw)
        owner.dcnt += 16
        ins.then_inc(sem, 16)
        self.n_ins += 1
        ev = (sem, owner.dcnt)
        self.live_dma[id(sem)] = ev
        for t in reads:
            self._add_reader(t, ev)
        for t in writes:
            t.last_w = ev
            t.readers = {}
        return ins

    def wait_tile(self, eng, t):
        self._wait(eng, [t.last_w])

    def barrier(self):
        evs = [(self.sem[e], self.cnt[e]) for e in self.engs if self.cnt[e] > 0]
        evs += list(self.live_dma.values())
        for e in self.engs:
            self._wait(e, evs)


def bcast_mid(ap2d, n, inner):
    return ap2d.rearrange("p (o i) -> p o i", o=1).broadcast_to([ap2d.shape[0], n, inner])


def _rope_tables_np(pos, dim):
    inv = (np.float32(10000.0) ** (-(np.arange(0, dim, 2, dtype=np.float32)) / np.float32(dim))).astype(np.float32)
    ang = pos.astype(np.float32)[:, None] * inv[None, :]
    return np.cos(ang).astype(np.float32), np.sin(ang).astype(np.float32)


def make_consts():
    c = {}
    pos = np.arange(S)
    cos1, sin1 = _rope_tables_np(pos, 128)
    ropeA = np.zeros((2, 128, S), np.float32)
    ropeA[0] = np.concatenate([cos1, cos1], 1).T
    ropeA[1] = np.concatenate([sin1, sin1], 1).T
    cr, sr = _rope_tables_np(pos // 64, 64)
    cc, sc = _rope_tables_np(pos % 64, 64)
    ropeB = np.zeros((2, 128, S), np.float32)
    ropeB[0] = np.concatenate([cr, cr, cc, cc], 1).T
    ropeB[1] = np.concatenate([sr, sr, sc, sc], 1).T
    c["ropeA"] = ropeA
    c["ropeB"] = ropeB
    RA = np.zeros((128, 128), np.float32)
    for dp in range(128):
        if dp < 64:
            RA[dp + 64, dp] = -1.0
        else:
            RA[dp - 64, dp] = 1.0
    RB = np.zeros((128, 128), np.float32)
    for dp in range(128):
        if dp % 64 < 32:
            RB[dp + 32, dp] = -1.0
        else:
            RB[dp - 32, dp] = 1.0
    c["rmats"] = np.stack([RA, RB])
    i = np.arange(128)[:, None]
    j = np.arange(512)[None, :]
    am = np.zeros((20, 128, 512), np.float32)
    for ri, r in enumerate(range(-8, 12)):
        delta = r * 128 + i - j
        m = np.zeros_like(delta, dtype=np.float32)
        for dil in (1, 4, 16):
            m += ((delta % dil == 0) & (np.abs(delta) <= 64 * dil)).astype(np.float32)
        am[ri] = m
    c["amask"] = am
    c["ident"] = np.eye(128, dtype=np.float32)
    u = np.arange(64)[:, None]
    t = np.arange(64)[None, :]
    c["cmats"] = np.stack([(u <= t), (u >= t), np.ones((64, 64), bool)]).astype(np.float32)
    return c


def prep_params(inp):
    p = {}
    f = lambda a: np.ascontiguousarray(a, dtype=np.float32)
    p["n1g"] = f(inp["norm1_g"].reshape(DEPTH, 16, 128).transpose(0, 2, 1))
    p["n2g"] = f(inp["norm2_g"].reshape(DEPTH, 16, 128).transpose(0, 2, 1))
    p["gains"] = f(np.stack([inp["a_q_gain"], inp["a_k_gain"], inp["b_q_gain"], inp["b_k_gain"]], -1))
    p["convw"] = f(inp["c_conv_w"].reshape(DEPTH, 5, 8, 128).transpose(0, 3, 2, 1))
    p["convb"] = f(inp["c_conv_b"].reshape(DEPTH, 8, 128).transpose(0, 2, 1))
    p["gateb"] = f(inp["c_gate_b"].reshape(DEPTH, 1, 32))
    p["outg"] = f(inp["c_out_g"].reshape(DEPTH, 1, 1024))
    return p


WEIGHTS = [("w_in", D, IN_COLS), ("w_branch_a", 768, D), ("w_branch_b", 1024, D), ("w_branch_c", 1024, D),
           ("w_out", D, D), ("w_up", D, DFF), ("w_down", DFF, D)]


class Prog:
    def __init__(self, n_layers=DEPTH, debug=False, stop_after=None):
        self.nc = nc = bass.Bass("TRN2", target_bir_lowering=False)
        self.kb = kb = KB(nc)
        self.n_layers = n_layers
        self.debug = debug
        self.stop_after = stop_after
        ext = "ExternalInput"
        self.xT = kb.dram("xT", [D, S], F32, kind=ext)
        self.W = {n: kb.dram(n, [n_layers, r, c], F32, kind=ext) for n, r, c in WEIGHTS}
        self.P = {}
        for n, shp in (("n1g", [DEPTH, 128, 16]), ("n2g", [DEPTH, 128, 16]), ("gains", [DEPTH, 128, 4]),
                       ("convw", [DEPTH, 128, 8, 5]), ("convb", [DEPTH, 128, 8]), ("gateb", [DEPTH, 1, 32]),
                       ("outg", [DEPTH, 1, 1024])):
            self.P[n] = kb.dram(n, shp, F32, kind=ext)
        self.C = {}
        for n, shp in (("ropeA", [2, 128, S]), ("ropeB", [2, 128, S]), ("rmats", [2, 128, 128]),
                       ("amask", [20, 128, 512]), ("ident", [128, 128]), ("cmats", [3, 64, 64])):
            self.C[n] = kb.dram(n, shp, F32, kind=ext)
        self.outT = kb.dram("outT", [D, S], F32, kind="ExternalOutput")
        sk = "ExternalOutput" if debug else "Internal"
        self.WB = [{n: kb.dram("wb%d_%s" % (par, n), [r, c], BF16) for n, r, c in WEIGHTS} for par in range(2)]
        self.sc = {}
        for n, shp, dt in (("aqT", [768, S], BF16), ("akT", [768, S], BF16), ("bqT", [1024, S], BF16),
                           ("bkT", [256, S], BF16), ("cqkT", [1024, S], F32), ("gT", [3 * D, S], BF16),
                           ("av", [S, 768], BF16), ("bv", [S, 256], BF16), ("cv", [S, 1024], F32),
                           ("co", [S, 1024], F32), ("cg", [S, 32], F32), ("yT", [2816, S], BF16),
                           ("hT", [D, S], F32), ("cqT", [512, S], BF16), ("ckT", [512, S], BF16),
                           ("ck", [S, 512], BF16), ("hfw", [S, 1024], F32)):
            self.sc[n] = kb.dram("sc_" + n, shp, dt, kind=sk)
        self.xbuf = [kb.dram("xbuf%d" % i, [D, S], F32) for i in range(2)]

    def cast_weights(self, L):
        kb = self.kb
        for n, r, c in WEIGHTS:
            dst = self.WB[L % 2][n]
            src = self.W[n]
            step = 256
            for r0 in range(0, r, step):
                kb.dma("pool", dst.ap()[r0:r0 + step, :], src.ap()[L, r0:r0 + step, :], writes=[dst], waw=False)

    def rmsnorm_T(self, src_ap_fn, ntok, gt, dstT, dst_tok0, xts, sqs, rs, psN, ones_f, sub=128):
        kb = self.kb
        for i in range(ntok // sub):
            xt = xts[i % len(xts)]
            src = src_ap_fn(i * sub, sub)
            kb.dma("sp", xt[:, 0:8, :], src[:, 0:8, :], writes=[xt])
            kb.dma("sp", xt[:, 8:16, :], src[:, 8:16, :], writes=[xt])
            for kc in range(16):
                sq = sqs[kc % len(sqs)]
                kb.op("act", lambda e: e.activation(out=sq[:, 0:sub], in_=xt[:, kc, :], func=AF.Square), reads=[xt], writes=[sq])
                kb.op("pe", lambda e: e.matmul(psN[:, 0:sub], lhsT=ones_f[:], rhs=sq[:, 0:sub], start=(kc == 0), stop=(kc == 15)),
                      reads=[ones_f, sq], writes=[psN])
            kb.op("act", lambda e: e.activation(out=rs[:, 0:sub], in_=psN[:, 0:sub], func=AF.Ln, scale=1.0 / D, bias=self.eps_t[:, 0:1]),
                  reads=[psN, self.eps_t], writes=[rs])
            kb.op("act", lambda e: e.activation(out=rs[:, 0:sub], in_=rs[:, 0:sub], func=AF.Exp, scale=-0.5), reads=[rs], writes=[rs])
            c0 = dst_tok0 + i * sub
            for kc in range(16):
                kb.op("dve", lambda e: e.scalar_tensor_tensor(out=dstT[:, kc, c0:c0 + sub], in0=xt[:, kc, :], scalar=gt[:, kc:kc + 1],
                                                              in1=rs[:, 0:sub], op0=ALU.mult, op1=ALU.mult),
                      reads=[xt, gt, rs], writes=[dstT])

    def consts_phase(self):
        kb = self.kb
        self.eps_t = kb.sb("eps", [128, 1], F32)
        kb.op("dve", lambda e: e.memset(self.eps_t[:], EPS), writes=[self.eps_t])
        self.ones_f = kb.sb("ones_f", [128, 128], F32)
        kb.op("dve", lambda e: e.memset(self.ones_f[:], 1.0), writes=[self.ones_f])
        self.ones_b = kb.sb("ones_b", [128, 128], BF16)
        kb.op("dve", lambda e: e.memset(self.ones_b[:], 1.0), writes=[self.ones_b])
        self.ident_b = kb.sb("ident_b", [128, 128], BF16)
        kb.dma("pool", self.ident_b[:], self.C["ident"].ap()[:, :], writes=[self.ident_b])
        self.rmat = kb.sb("rmat", [128, 2, 128], BF16)
        kb.dma("pool", self.rmat[:], self.C["rmats"].ap().rearrange("r p c -> p r c"), writes=[self.rmat])


    def _qk_stages(self, ps, k, sg_, tsl, dsl, dst, sqs3, rss3, qnbs, t1s, t2s, obs, psRa, psRb, gains):
        kb = self.kb
        sq, rs, qnb = sqs3[k % 3], rss3[k % 3], qnbs[k % 3]
        t1, t2, ob = t1s[k % 2], t2s[k % 2], obs[k % 3]
        pa, pb = psRa[k % 2], psRb[k % 2]
        rope = sg_["rope"]
        gi = sg_["gi"]
        rm = sg_["rm"]

        def e1():
            kb.op("act", lambda e: e.activation(out=sq[:], in_=ps[:], func=AF.Square), reads=[ps], writes=[sq])

        def e2():
            kb.op("pe", lambda e: e.matmul(pa[:], lhsT=self.ones_f[:], rhs=sq[:], start=True, stop=True), reads=[self.ones_f, sq], writes=[pa])
            kb.op("act", lambda e: e.activation(out=rs[:], in_=pa[:], func=AF.Ln, scale=1.0 / 128, bias=self.eps_t[:, 0:1]), reads=[pa, self.eps_t], writes=[rs])
            kb.op("act", lambda e: e.activation(out=rs[:], in_=rs[:], func=AF.Exp, scale=-0.5), reads=[rs], writes=[rs])
            kb.op("dve", lambda e: e.scalar_tensor_tensor(out=qnb[:], in0=ps[:], scalar=gains[:, gi:gi + 1], in1=rs[:], op0=ALU.mult, op1=ALU.mult),
                  reads=[ps, gains, rs], writes=[qnb])

        def e3():
            kb.op("pe", lambda e: e.matmul(pb[:], lhsT=self.rmat[:, rm, :], rhs=qnb[:], start=True, stop=True), reads=[self.rmat, qnb], writes=[pb])
            kb.op("dve", lambda e: e.tensor_tensor(out=t1[:], in0=qnb[:], in1=rope[:, 0, tsl], op=ALU.mult), reads=[qnb, rope], writes=[t1])
            kb.op("dve", lambda e: e.tensor_tensor(out=t2[:], in0=pb[:], in1=rope[:, 1, tsl], op=ALU.mult), reads=[pb, rope], writes=[t2])
            kb.op("pool", lambda e: e.tensor_tensor(out=ob[:], in0=t1[:], in1=t2[:], op=ALU.add), reads=[t1, t2], writes=[ob])
            kb.dma("pool", dsl, ob[:], reads=[ob], writes=[dst], waw=False)

        return [e1, e2, e3]

    def p1(self, L, xin, sg):
        kb, sc = self.kb, self.sc
        NT = 2048
        T0 = sg * NT
        WB = self.WB[L % 2]["w_in"]
        with kb.phase():
            xnT = kb.sb("xnT", [128, 16, NT], BF16)
            xts = [kb.sb("xt%d" % i, [128, 16, 128], F32) for i in range(2)]
            sqs = [kb.sb("sq%d" % i, [128, 512], F32) for i in range(2)]
            rss = [kb.sb("rs%d" % i, [128, 512], F32) for i in range(2)]
            wts = [kb.sb("wt%d" % i, [128, 16, 512], BF16) for i in range(2)]
            g1 = kb.sb("g1", [128, 16], F32)
            gains = kb.sb("gains", [128, 4], F32)
            ropeA = kb.sb("ropeA", [128, 2, NT], F32)
            ropeB = kb.sb("ropeB", [128, 2, NT], F32)
            qnbs = [kb.sb("qnb%d" % i, [128, 512], BF16) for i in range(3)]
            sqs3 = sqs + [kb.sb("sq2", [128, 512], F32)]
            rss3 = rss + [kb.sb("rs2", [128, 512], F32)]
            t1s = [kb.sb("t1%d" % i, [128, 512], F32) for i in range(2)]
            t2s = [kb.sb("t2%d" % i, [128, 512], F32) for i in range(2)]
            obs = [kb.sb("ob%d" % i, [128, 512], BF16) for i in range(3)]
            ofs = [kb.sb("of%d" % i, [128, 512], F32) for i in range(3)]
            psX = [kb.ps("psX%d" % i, [128, 512]) for i in range(3)]
            psRa = [kb.ps("psRa%d" % i, [128, 512]) for i in range(2)]
            psRb = [kb.ps("psRb%d" % i, [128, 512]) for i in range(2)]
            psN = kb.ps("psN", [128, 512])
            pend = []
            qcnt = 0
            kb.dma("sp", g1[:], self.P["n1g"].ap()[L], writes=[g1])
            kb.dma("sp", gains[:], self.P["gains"].ap()[L], writes=[gains])
            for j in range(2):
                kb.dma("sp", ropeA[:, j, :], self.C["ropeA"].ap()[j, :, T0:T0 + NT], writes=[ropeA])
                kb.dma("sp", ropeB[:, j, :], self.C["ropeB"].ap()[j, :, T0:T0 + NT], writes=[ropeB])
            xv = xin.ap().rearrange("(kc p) t -> p kc t", p=128)
            self.rmsnorm_T(lambda t0, n: xv[:, :, T0 + t0:T0 + t0 + n], NT, g1, xnT, 0, xts, sqs, rss[0], psN, self.ones_f)

            segs = []
            def add(col0, n, kind, dst, drow0=0, **kw):
                c = 0
                while c < n:
                    w = min(512, n - c)
                    segs.append(dict(col0=col0 + c, n=w, kind=kind, dst=dst, d0=drow0 + c, **kw))
                    c += w
            add(O_AQ, 768, "qk", sc["aqT"], gi=0, rope=ropeA, rm=0)
            add(O_AK, 768, "qk", sc["akT"], gi=1, rope=ropeA, rm=0)
            add(O_BQ, 1024, "qk", sc["bqT"], gi=2, rope=ropeB, rm=1)
            add(O_BK, 256, "qk", sc["bkT"], gi=3, rope=ropeB, rm=1)
            add(O_CQK, 1024, "copyT", sc["cqkT"])
            add(O_G, 3 * D, "sigT", sc["gT"])
            add(O_AV, 768, "tok_bf", sc["av"])
            add(O_BV, 256, "tok_bf", sc["bv"])
            add(O_CV, 1024, "tok_f", sc["cv"])
            add(O_CO, 1024, "tok_sig", sc["co"])
            add(O_CG, 32, "tok_f", sc["cg"])
            wv = WB.ap().rearrange("(kc p) c -> p kc c", p=128)
            cnt = 0
            ocnt = 0
            for si, sg_ in enumerate(segs):
                wt = wts[si % 2]
                n = sg_["n"]
                c0 = sg_["col0"]
                kb.dma("sp", wt[:, 0:8, 0:n], wv[:, 0:8, c0:c0 + n], reads=[WB], writes=[wt])
                kb.dma("sp", wt[:, 8:16, 0:n], wv[:, 8:16, c0:c0 + n], reads=[WB], writes=[wt])
                kind = sg_["kind"]
                dst = sg_["dst"]
                if kind in ("qk", "copyT", "sigT"):
                    for cc in range(n // 128):
                        r0 = sg_["d0"] + cc * 128
                        for tg in range(NT // 512):
                            ps = psX[cnt % 3]
                            cnt += 1
                            for kc in range(16):
                                kb.op("pe", lambda e: e.matmul(ps[:, :], lhsT=wt[:, kc, cc * 128:(cc + 1) * 128], rhs=xnT[:, kc, tg * 512:(tg + 1) * 512],
                                                               start=(kc == 0), stop=(kc == 15)), reads=[wt, xnT], writes=[ps], track=(kc == 15))
                            tsl = slice(tg * 512, (tg + 1) * 512)
                            dsl = dst.ap()[r0:r0 + 128, T0 + tg * 512:T0 + (tg + 1) * 512]
                            if kind == "sigT":
                                ob = obs[ocnt % 3]
                                ocnt += 1
                                kb.op("act", lambda e: e.activation(out=ob[:], in_=ps[:], func=AF.Sigmoid), reads=[ps], writes=[ob])
                                kb.dma("pool", dsl, ob[:], reads=[ob], writes=[dst], waw=False)
                            elif kind == "copyT":
                                of = ofs[ocnt % 3]
                                ocnt += 1
                                kb.op("dve", lambda e: e.tensor_copy(out=of[:], in_=ps[:]), reads=[ps], writes=[of])
                                kb.dma("pool", dsl, of[:], reads=[of], writes=[dst], waw=False)
                            else:
                                pend.append(self._qk_stages(ps, qcnt, sg_, tsl, dsl, dst, sqs3, rss3, qnbs, t1s, t2s, obs, psRa, psRb, gains))
                                qcnt += 1
                            for st in list(pend):
                                st.pop(0)()
                                if not st:
                                    pend.remove(st)
                    while pend:
                        for st in list(pend):
                            st.pop(0)()
                            if not st:
                                pend.remove(st)
                else:
                    for tt in range(NT // 128):
                        ps = psX[cnt % 3]
                        cnt += 1
                        for kc in range(16):
                            kb.op("pe", lambda e: e.matmul(ps[:, 0:n], lhsT=xnT[:, kc, tt * 128:(tt + 1) * 128], rhs=wt[:, kc, 0:n],
                                                           start=(kc == 0), stop=(kc == 15)), reads=[wt, xnT], writes=[ps], track=(kc == 15))
                        dsl = dst.ap()[T0 + tt * 128:T0 + (tt + 1) * 128, sg_["d0"]:sg_["d0"] + n]
                        if kind == "tok_bf":
                            ob = obs[ocnt % 3]
                            ocnt += 1
                            kb.op("dve", lambda e: e.tensor_copy(out=ob[:, 0:n], in_=ps[:, 0:n]), reads=[ps], writes=[ob])
                            kb.dma("pool", dsl, ob[:, 0:n], reads=[ob], writes=[dst], waw=False)
                        else:
                            of = ofs[ocnt % 3]
                            ocnt += 1
                            if kind == "tok_sig":
                                kb.op("act", lambda e: e.activation(out=of[:, 0:n], in_=ps[:, 0:n], func=AF.Sigmoid), reads=[ps], writes=[of])
                            else:
                                kb.op("dve", lambda e: e.tensor_copy(out=of[:, 0:n], in_=ps[:, 0:n]), reads=[ps], writes=[of])
                            kb.dma("pool", dsl, of[:, 0:n], reads=[of], writes=[dst], waw=False)

    def p3a(self, L, xin):
        kb, sc = self.kb, self.sc
        WBs = self.WB[L % 2]
        br = [("w_branch_a", 0, 6), ("w_branch_b", 6, 8), ("w_branch_c", 14, 8)]
        with kb.phase():
            yT = kb.sb("yT", [128, 22, 512], BF16)
            xt = kb.sb("xt", [128, 16, 512], F32)
            mT = kb.sb("mT", [128, 16, 512], BF16)
            wbs = [kb.sb("wb%d" % i, [128, 22, 256], BF16) for i in range(2)]
            wos = [kb.sb("wo%d" % i, [128, 16, 256], BF16) for i in range(2)]
            gts = [kb.sb("gt%d" % i, [128, 3, 512], BF16) for i in range(2)]
            ms = [kb.sb("m%d" % i, [128, 512], F32) for i in range(2)]
            tmps = [kb.sb("tmp%d" % i, [128, 512], F32) for i in range(2)]
            psX = [kb.ps("psX%d" % i, [128, 512]) for i in range(4)]
            yv = sc["yT"].ap().rearrange("(kc p) t -> p kc t", p=128)
            xv = xin.ap().rearrange("(kc p) t -> p kc t", p=128)
            hv = sc["hT"].ap().rearrange("(kc p) t -> p kc t", p=128)
            gv = sc["gT"].ap().rearrange("(b oc p) t -> p b oc t", p=128, b=3)
            wvs = [WBs[n].ap().rearrange("(kc p) c -> p kc c", p=128) for n, _, _ in br]
            wov = WBs["w_out"].ap().rearrange("(kc p) c -> p kc c", p=128)
            cnt = 0
            for tg in range(S // 512):
                tsl = slice(tg * 512, (tg + 1) * 512)
                kb.dma("sp", yT[:, 0:11, :], yv[:, 0:11, tsl], reads=[sc["yT"]], writes=[yT])
                kb.dma("sp", yT[:, 11:22, :], yv[:, 11:22, tsl], reads=[sc["yT"]], writes=[yT])
                kb.dma("sp", xt[:, 0:8, :], xv[:, 0:8, tsl], reads=[xin], writes=[xt])
                kb.dma("sp", xt[:, 8:16, :], xv[:, 8:16, tsl], reads=[xin], writes=[xt])
                for o2 in range(8):
                    wb = wbs[o2 % 2]
                    for (n, k0, nk), wv in zip(br, wvs):
                        kb.dma("sp", wb[:, k0:k0 + nk, :], wv[:, :, o2 * 256:(o2 + 1) * 256], reads=[WBs[n]], writes=[wb])
                    for oi in range(2):
                        oc = o2 * 2 + oi
                        gt = gts[oc % 2]
                        kb.dma("sp", gt[:], gv[:, :, oc, tsl], reads=[sc["gT"]], writes=[gt])
                        m = ms[oc % 2]
                        for bi, (n, k0, nk) in enumerate(br):
                            ps = psX[cnt % 4]
                            cnt += 1
                            for kc in range(nk):
                                kb.op("pe", lambda e: e.matmul(ps[:], lhsT=wb[:, k0 + kc, oi * 128:(oi + 1) * 128], rhs=yT[:, k0 + kc, :],
                                                               start=(kc == 0), stop=(kc == nk - 1)), reads=[wb, yT], writes=[ps], track=(kc == nk - 1))
                            if bi == 0:
                                kb.op("dve", lambda e: e.tensor_tensor(out=m[:], in0=ps[:], in1=gt[:, 0, :], op=ALU.mult), reads=[ps, gt], writes=[m])
                            else:
                                tmp = tmps[bi % 2]
                                kb.op("dve", lambda e: e.tensor_tensor(out=tmp[:], in0=ps[:], in1=gt[:, bi, :], op=ALU.mult), reads=[ps, gt], writes=[tmp])
                                if bi == 1:
                                    kb.op("pool", lambda e: e.tensor_tensor(out=m[:], in0=m[:], in1=tmp[:], op=ALU.add), reads=[m, tmp], writes=[m])
                                else:
                                    kb.op("pool", lambda e: e.tensor_tensor(out=mT[:, oc, :], in0=m[:], in1=tmp[:], op=ALU.add), reads=[m, tmp], writes=[mT])
                for o2 in range(8):
                    wo = wos[o2 % 2]
                    kb.dma("sp", wo[:], wov[:, :, o2 * 256:(o2 + 1) * 256], reads=[WBs["w_out"]], writes=[wo])
                    for oi in range(2):
                        oc = o2 * 2 + oi
                        ps = psX[cnt % 4]
                        cnt += 1
                        for kc in range(16):
                            kb.op("pe", lambda e: e.matmul(ps[:], lhsT=wo[:, kc, oi * 128:(oi + 1) * 128], rhs=mT[:, kc, :],
                                                           start=(kc == 0), stop=(kc == 15)), reads=[wo, mT], writes=[ps], track=(kc == 15))
                        kb.op("dve", lambda e: e.tensor_tensor(out=xt[:, oc, :], in0=ps[:], in1=xt[:, oc, :], op=ALU.add), reads=[ps, xt], writes=[xt])
                kb.dma("pool", hv[:, 0:8, tsl], xt[:, 0:8, :], reads=[xt], writes=[sc["hT"]], waw=False)
                kb.dma("pool", hv[:, 8:16, tsl], xt[:, 8:16, :], reads=[xt], writes=[sc["hT"]], waw=False)

    def p3b(self, L, xout):
        kb, sc = self.kb, self.sc
        WBs = self.WB[L % 2]
        with kb.phase():
            hT = kb.sb("hT", [128, 16, 512], F32)
            hnT = kb.sb("hnT", [128, 16, 512], BF16)
            uT = kb.sb("uT", [128, 64, 512], BF16)
            sqs = [kb.sb("sq%d" % i, [128, 512], F32) for i in range(2)]
            rs = kb.sb("rs", [128, 512], F32)
            g2 = kb.sb("g2", [128, 16], F32)
            wus = [kb.sb("wu%d" % i, [128, 16, 256], BF16) for i in range(2)]
            wds = [kb.sb("wd%d" % i, [128, 64, 128], BF16) for i in range(2)]
            rls = [kb.sb("rl%d" % i, [128, 512], F32) for i in range(2)]
            psX = [kb.ps("psX%d" % i, [128, 512]) for i in range(4)]
            psN = kb.ps("psN", [128, 512])
            kb.dma("sp", g2[:], self.P["n2g"].ap()[L], writes=[g2])
            hv = sc["hT"].ap().rearrange("(kc p) t -> p kc t", p=128)
            ov = xout.ap().rearrange("(kc p) t -> p kc t", p=128)
            wuv = WBs["w_up"].ap().rearrange("(kc p) c -> p kc c", p=128)
            wdv = WBs["w_down"].ap().rearrange("(fc p) c -> p fc c", p=128)
            cnt = 0
            for tg in range(S // 512):
                tsl = slice(tg * 512, (tg + 1) * 512)
                kb.dma("sp", hT[:, 0:8, :], hv[:, 0:8, tsl], reads=[sc["hT"]], writes=[hT])
                kb.dma("sp", hT[:, 8:16, :], hv[:, 8:16, tsl], reads=[sc["hT"]], writes=[hT])
                for kc in range(16):
                    sq = sqs[kc % 2]
                    kb.op("act", lambda e: e.activation(out=sq[:], in_=hT[:, kc, :], func=AF.Square), reads=[hT], writes=[sq])
                    kb.op("pe", lambda e: e.matmul(psN[:], lhsT=self.ones_f[:], rhs=sq[:], start=(kc == 0), stop=(kc == 15)),
                          reads=[self.ones_f, sq], writes=[psN])
                kb.op("act", lambda e: e.activation(out=rs[:], in_=psN[:], func=AF.Ln, scale=1.0 / D, bias=self.eps_t[:, 0:1]), reads=[psN, self.eps_t], writes=[rs])
                kb.op("act", lambda e: e.activation(out=rs[:], in_=rs[:], func=AF.Exp, scale=-0.5), reads=[rs], writes=[rs])
                for kc in range(16):
                    kb.op("dve", lambda e: e.scalar_tensor_tensor(out=hnT[:, kc, :], in0=hT[:, kc, :], scalar=g2[:, kc:kc + 1], in1=rs[:],
                                                                  op0=ALU.mult, op1=ALU.mult), reads=[hT, g2, rs], writes=[hnT])
                for f2 in range(32):
                    wu = wus[f2 % 2]
                    kb.dma("sp", wu[:], wuv[:, :, f2 * 256:(f2 + 1) * 256], reads=[WBs["w_up"]], writes=[wu])
                    for fi in range(2):
                        fc = f2 * 2 + fi
                        ps = psX[cnt % 4]
                        cnt += 1
                        for kc in range(16):
                            kb.op("pe", lambda e: e.matmul(ps[:], lhsT=wu[:, kc, fi * 128:(fi + 1) * 128], rhs=hnT[:, kc, :],
                                                           start=(kc == 0), stop=(kc == 15)), reads=[wu, hnT], writes=[ps], track=(kc == 15))
                        rl = rls[fc % 2]
                        kb.op("act", lambda e: e.activation(out=rl[:], in_=ps[:], func=AF.Relu), reads=[ps], writes=[rl])
                        kb.op("pool" if fc % 2 else "dve", lambda e: e.tensor_tensor(out=uT[:, fc, :], in0=rl[:], in1=rl[:], op=ALU.mult), reads=[rl], writes=[uT])
                for oc in range(16):
                    wd = wds[oc % 2]
                    kb.dma("sp", wd[:, 0:32, :], wdv[:, 0:32, oc * 128:(oc + 1) * 128], reads=[WBs["w_down"]], writes=[wd])
                    kb.dma("sp", wd[:, 32:64, :], wdv[:, 32:64, oc * 128:(oc + 1) * 128], reads=[WBs["w_down"]], writes=[wd])
                    ps = psX[cnt % 4]
                    cnt += 1
                    for fc in range(64):
                        kb.op("pe", lambda e: e.matmul(ps[:], lhsT=wd[:, fc, :], rhs=uT[:, fc, :], start=(fc == 0), stop=(fc == 63)),
                              reads=[wd, uT], writes=[ps], track=(fc == 63))
                    kb.op("dve", lambda e: e.tensor_tensor(out=hT[:, oc, :], in0=ps[:], in1=hT[:, oc, :], op=ALU.add), reads=[ps, hT], writes=[hT])
                kb.dma("pool", ov[:, 0:8, tsl], hT[:, 0:8, :], reads=[hT], writes=[xout], waw=False)
                kb.dma("pool", ov[:, 8:16, tsl], hT[:, 8:16, :], reads=[hT], writes=[xout], waw=False)

    def attn_gen(self, L, tl):
        kb, sc = self.kb, self.sc
        SKEW = 2
        scale = float(1.0 / np.sqrt(128.0))
        KTs, Vs, QTs, am, PTs, yos, rzs, zas, zbs, psS, psO = tl
        jobs = []
        for h in range(6):
            jobs.append(("A", sc["aqT"], h * 128, sc["akT"], h * 128, sc["av"], h * 128, h * 128, ("A", h)))
        for h in range(8):
            kvh = h // 4
            jobs.append(("B", sc["bqT"], h * 128, sc["bkT"], kvh * 128, sc["bv"], kvh * 128, 768 + h * 128, ("B", kvh)))
        cur_kv = None
        nkv = 0
        g = 0
        sidx = 0
        for ji, (kind, qsrc, q0, ksrc, k0, vsrc, v0, y0, kvid) in enumerate(jobs):
            QT = QTs[ji % 2]
            kb.dma("sp", QT[:], qsrc.ap()[q0:q0 + 128, :], reads=[qsrc], writes=[QT])
            if kvid != cur_kv:
                cur_kv = kvid
                KT = KTs[nkv % 2]
                V = Vs[nkv % 2]
                nkv += 1
                kb.dma("sp", KT[:], ksrc.ap()[k0:k0 + 128, :], reads=[ksrc], writes=[KT])
                vv = vsrc.ap()[:, v0:v0 + 128].rearrange("(kc p) d -> p kc d", p=128)
                for c0 in range(0, 32, 8):
                    kb.dma("sp", V[:, c0:c0 + 8, :], vv[:, c0:c0 + 8, :], reads=[vsrc], writes=[V])
            for qg in range(8):
                kcs = list(range(32)) if kind == "B" else list(range(max(0, 4 * qg - 8), min(32, 4 * qg + 12)))
                pso = psO
                za, zb = zas[g % 2], zbs[g % 2]
                n = len(kcs)
                for i in range(n + SKEW):
                    if i < n:
                        kc = kcs[i]
                        ps = psS[sidx % 2]
                        sidx += 1
                        PT = PTs[i % 4]
                        kb.op("pe", lambda e: e.matmul(ps[:], lhsT=KT[:, kc * 128:(kc + 1) * 128], rhs=QT[:, qg * 512:(qg + 1) * 512], start=True, stop=True),
                              reads=[KT, QT], writes=[ps])
                        kb.op("act", lambda e: e.activation(out=PT[:], in_=ps[:], func=AF.Exp, scale=scale), reads=[ps], writes=[PT])
                        if kind == "A":
                            r = kc - 4 * qg + 8
                            kb.op("dve", lambda e: e.tensor_tensor(out=PT[:], in0=PT[:], in1=am[:, r, :], op=ALU.mult), reads=[PT, am], writes=[PT])
                        eng, z = ("dve", za) if i % 2 == 0 else ("pool", zb)
                        if i < 2:
                            kb.op(eng, lambda e: e.tensor_copy(out=z[:], in_=PT[:]), reads=[PT], writes=[z])
                        else:
                            kb.op(eng, lambda e: e.tensor_tensor(out=z[:], in0=z[:], in1=PT[:], op=ALU.add), reads=[z, PT], writes=[z])
                    if i >= SKEW:
                        j = i - SKEW
                        kc = kcs[j]
                        PT = PTs[j % 4]
                        kb.op("pe", lambda e: e.matmul(pso[:], lhsT=V[:, kc, :], rhs=PT[:], start=(j == 0), stop=(j == n - 1)),
                              reads=[V, PT], writes=[pso], track=True)
                    yield
                rz, yo = rzs[g % 2], yos[g % 2]
                kb.op("dve", lambda e: e.tensor_tensor(out=za[:], in0=za[:], in1=zb[:], op=ALU.add), reads=[za, zb], writes=[za])
                ps = psS[sidx % 2]
                sidx += 1
                kb.op("pe", lambda e: e.matmul(ps[:], lhsT=self.ones_f[:], rhs=za[:], start=True, stop=True), reads=[self.ones_f, za], writes=[ps])
                kb.op("dve", lambda e: e.reciprocal(out=rz[:], in_=ps[:]), reads=[ps], writes=[rz])
                kb.op("dve", lambda e: e.tensor_tensor(out=yo[:], in0=pso[:], in1=rz[:], op=ALU.mult), reads=[pso, rz], writes=[yo])
                kb.dma("pool", sc["yT"].ap()[y0:y0 + 128, qg * 512:(qg + 1) * 512], yo[:], reads=[yo], writes=[sc["yT"]], waw=False)
                g += 1
                yield

    def mlstm_gen(self, L, tl):
        kb, sc = self.kb, self.sc
        NCH = S // 64
        Wt, EBt, EBTt, W2t, Wb, W2b, cm, trib, outg = tl
        with kb.phase():
            G = kb.sb("G", [64, NCH, 32], F32)
            gb = kb.sb("gb", [64, 32], F32)
            nlf = kb.sb("nlf", [64, 2, NCH, 8], F32)
            tmp = kb.sb("gtmp", [64, 2, NCH, 8], F32)
            psA = [kb.ps("psg%d" % i, [128, 512]) for i in range(2)]
            psB = [kb.ps("psh%d" % i, [128, 512]) for i in range(2)]
            cgv = sc["cg"].ap().rearrange("(c t) g -> t c g", t=64)
            for c0 in range(0, NCH, 16):
                kb.dma("sp", G[:, c0:c0 + 16, :], cgv[:, c0:c0 + 16, :], reads=[sc["cg"]], writes=[G])
            kb.dma("sp", gb[:], self.P["gateb"].ap()[L].partition_broadcast(64), writes=[gb])
            kb.op("dve", lambda e: e.tensor_tensor(out=G[:], in0=G[:], in1=gb[:, :].unsqueeze(1).broadcast_to([64, NCH, 32]), op=ALU.add),
                  reads=[G, gb], writes=[G])
            for d in range(2):
                fcol = 8 + 16 * d
                kb.op("act", lambda e: e.activation(out=nlf[:, d], in_=G[:, :, fcol:fcol + 8], func=AF.Exp, scale=-1.0), reads=[G], writes=[nlf])
                kb.op("act", lambda e: e.activation(out=nlf[:, d], in_=nlf[:, d], func=AF.Ln, bias=1.0), reads=[nlf], writes=[nlf])
            yield
            for d in range(2):
                icol = 16 * d
                rhs = nlf[:, d].rearrange("p c h -> p (c h)")
                kb.op("pe", lambda e: e.matmul(psA[d][0:64, :], lhsT=cm[:, d, :], rhs=rhs, start=True, stop=True), reads=[cm, nlf], writes=[psA[d]])
                kb.op("pe", lambda e: e.matmul(psB[d][0:64, :], lhsT=cm[:, 2, :], rhs=rhs, start=True, stop=True), reads=[cm, nlf], writes=[psB[d]])
                csv = psA[d][0:64, :].rearrange("p (c h) -> p c h", h=8)
                kb.op("dve", lambda e: e.tensor_tensor(out=tmp[:, d], in0=csv, in1=G[:, :, icol:icol + 8], op=ALU.add), reads=[psA[d], G], writes=[tmp])
                kb.op("act", lambda e: e.activation(out=Wt[:, d], in_=tmp[:, d], func=AF.Exp), reads=[tmp], writes=[Wt])
                kb.op("act", lambda e: e.activation(out=EBt[:, d], in_=csv, func=AF.Exp, scale=-1.0), reads=[psA[d]], writes=[EBt])
                kb.op("act", lambda e: e.activation(out=EBTt[:, d], in_=psB[d][0:64, :].rearrange("p (c h) -> p c h", h=8), func=AF.Exp, scale=-1.0),
                      reads=[psB[d]], writes=[EBTt])
                kb.op("dve", lambda e: e.tensor_tensor(out=W2t[:, d], in0=Wt[:, d], in1=EBTt[:, d], op=ALU.mult), reads=[Wt, EBTt], writes=[W2t])
                yield
            kb.op("dve", lambda e: e.tensor_copy(out=Wb[:], in_=Wt[:]), reads=[Wt], writes=[Wb])
            kb.op("dve", lambda e: e.tensor_copy(out=W2b[:], in_=W2t[:]), reads=[W2t], writes=[W2b])
        yield
        with kb.phase():
            xs = [kb.sb("cx%d" % i, [128, 8, 516], F32) for i in range(2)]
            cw = kb.sb("cw", [128, 8, 5], F32)
            cb = kb.sb("cb", [128, 8], F32)
            accs = [kb.sb("acc%d" % i, [128, 512], F32) for i in range(2)]
            qbs = [kb.sb("qb%d" % i, [128, 512], BF16) for i in range(3)]
            kfs = [kb.sb("kf%d" % i, [128, 512], F32) for i in range(2)]
            kfm = kb.sb("kfm", [128, 4, 512], BF16)
            kts = [kb.sb("kt%d" % i, [128, 512], BF16) for i in range(2)]
            psT = [kb.ps("psT%d" % i, [128, 1024], BF16) for i in range(2)]
            kb.dma("sp", cw[:], self.P["convw"].ap()[L], writes=[cw])
            kb.dma("sp", cb[:], self.P["convb"].ap()[L], writes=[cb])
            cv_ = sc["cqkT"].ap().rearrange("(c p) t -> p c t", p=128)
            for tg in range(8):
                x = xs[tg % 2]
                t0 = tg * 512
                lo, hi = max(0, t0 - 2), min(S, t0 + 514)
                if tg == 0:
                    kb.op("dve", lambda e: e.memset(x[:, :, 0:2], 0.0), writes=[x])
                if tg == 7:
                    kb.op("dve", lambda e: e.memset(x[:, :, 514:516], 0.0), writes=[x])
                b0 = lo - (t0 - 2)
                kb.dma("sp", x[:, 0:4, b0:b0 + (hi - lo)], cv_[:, 0:4, lo:hi], reads=[sc["cqkT"]], writes=[x])
                kb.dma("sp", x[:, 4:8, b0:b0 + (hi - lo)], cv_[:, 4:8, lo:hi], reads=[sc["cqkT"]], writes=[x])
                for c in range(8):
                    acc = accs[c % 2]
                    kb.op("dve", lambda e: e.tensor_scalar(out=acc[:], in0=x[:, c, 0:512], scalar1=cw[:, c, 0:1], scalar2=None, op0=ALU.mult),
                          reads=[x, cw], writes=[acc])
                    for j in range(1, 5):
                        kb.op("dve", lambda e: e.scalar_tensor_tensor(out=acc[:], in0=x[:, c, j:j + 512], scalar=cw[:, c, j:j + 1], in1=acc[:],
                                                                      op0=ALU.mult, op1=ALU.add), reads=[x, cw, acc], writes=[acc])
                    if c < 4:
                        qb = qbs[c % 3]
                        kb.op("act", lambda e: e.activation(out=qb[:], in_=acc[:], func=AF.Silu, bias=cb[:, c:c + 1]), reads=[acc, cb], writes=[qb])
                        kb.dma("pool", sc["cqT"].ap()[c * 128:(c + 1) * 128, t0:t0 + 512], qb[:], reads=[qb], writes=[sc["cqT"]], waw=False)
                    else:
                        kf = kfs[c % 2]
                        kb.op("act", lambda e: e.activation(out=kf[:], in_=acc[:], func=AF.Silu, bias=cb[:, c:c + 1]), reads=[acc, cb], writes=[kf])
                        kb.op("pool", lambda e: e.tensor_scalar(out=kfm[:, c - 4, :], in0=kf[:], scalar1=0.125, scalar2=None, op0=ALU.mult),
                              reads=[kf], writes=[kfm])
                    yield
                kb.dma("pool", sc["ckT"].ap().rearrange("(c p) t -> p c t", p=128)[:, :, t0:t0 + 512], kfm[:], reads=[kfm], writes=[sc["ckT"]], waw=False)
                for tt in range(4):
                    pt = psT[tt % 2]
                    for c in range(4):
                        kb.op("pe", lambda e: e.transpose(out=pt[:, c * 128:(c + 1) * 128], in_=kfm[:, c, tt * 128:(tt + 1) * 128], identity=self.ident_b[:]),
                              reads=[kfm, self.ident_b], writes=[pt])
                    kt = kts[tt % 2]
                    kb.op("act", lambda e: e.activation(out=kt[:], in_=pt[:, 0:512], func=AF.Copy), reads=[pt], writes=[kt])
                    kb.dma("pool", sc["ck"].ap()[t0 + tt * 128:t0 + (tt + 1) * 128, :], kt[:], reads=[kt], writes=[sc["ck"]], waw=False)
                    yield
        yield
        for d in range(2):
            with kb.phase():
                QTc = [kb.sb("QTc%d" % i, [64, 8, 64], BF16) for i in range(2)]
                KTc = [kb.sb("KTc%d" % i, [64, 8, 64], BF16) for i in range(2)]
                Ktk = [kb.sb("Ktk%d" % i, [64, 512], BF16) for i in range(2)]
                vcs = [kb.sb("vc%d" % i, [64, 8, 128], F32) for i in range(2)]
                ATs = [kb.sb("AT%d" % i, [64, 8, 64], BF16) for i in range(2)]
                v1s = [kb.sb("v1%d" % i, [64, 8, 128], BF16) for i in range(2)]
                v2s = [kb.sb("v2%d" % i, [64, 8, 128], BF16) for i in range(2)]
                Cst = kb.sb("Cst", [64, 8, 128], F32)
                nst = kb.sb("nst", [64, 8], F32)
                Cbfs = [kb.sb("Cbf%d" % i, [64, 8, 128], BF16) for i in range(2)]
                nbfs = [kb.sb("nbf%d" % i, [64, 8], BF16) for i in range(2)]
                sm = [kb.sb("sm%d" % i, [64, 8], F32) for i in range(4)]
                hds = [kb.sb("hd%d" % i, [64, 8, 128], F32) for i in range(2)]
                psS = kb.ps("mpsS", [128, 512])
                psN = kb.ps("mpsN", [128, 512])
                psU = kb.ps("mpsU", [128, 512])
                psD = kb.ps("mpsD", [128, 512])
                if d == 1:
                    hfs = [kb.sb("hf%d" % i, [64, 1024], F32) for i in range(2)]
                    cos_ = [kb.sb("co%d" % i, [64, 1024], F32) for i in range(2)]
                    junk = kb.sb("junk", [64, 1024], F32)
                    ybs = [kb.sb("yb%d" % i, [64, 1024], BF16) for i in range(2)]
                    yts = [kb.sb("yt%d" % i, [128, 8, 64], BF16) for i in range(2)]
                    psT = kb.ps("mpsT", [128, 1024], BF16)
                kb.op("dve", lambda e: e.memset(Cst[:], 0.0), writes=[Cst])
                kb.op("dve", lambda e: e.memset(nst[:], 0.0), writes=[nst])
                kb.op("dve", lambda e: e.memset(Cbfs[0][:], 0.0), writes=[Cbfs[0]])
                kb.op("dve", lambda e: e.memset(nbfs[0][:], 0.0), writes=[nbfs[0]])
                qv = sc["cqT"].ap().rearrange("(h dd) t -> dd h t", dd=64)
                kv = sc["ckT"].ap().rearrange("(h dd) t -> dd h t", dd=64)
                yv = sc["yT"].ap()[1792:2816, :].rearrange("(h p) t -> p h t", p=128)
                order = list(range(NCH)) if d == 0 else list(range(NCH - 1, -1, -1))

                def loads(it):
                    c = order[it]
                    b = it % 2
                    tsl = slice(c * 64, (c + 1) * 64)
                    kb.dma("sp", QTc[b][:], qv[:, :, tsl], reads=[sc["cqT"]], writes=[QTc[b]])
                    kb.dma("sp", KTc[b][:], kv[:, :, tsl], reads=[sc["ckT"]], writes=[KTc[b]])
                    kb.dma("sp", Ktk[b][:], sc["ck"].ap()[tsl, :], reads=[sc["ck"]], writes=[Ktk[b]])
                    kb.dma("sp", vcs[b][:], sc["cv"].ap()[tsl, :].rearrange("t (h v) -> t h v", h=8), reads=[sc["cv"]], writes=[vcs[b]])
                    if d == 1:
                        kb.dma("sp", hfs[b][:], sc["hfw"].ap()[tsl, :], reads=[sc["hfw"]], writes=[hfs[b]])
                        kb.dma("sp", cos_[b][:], sc["co"].ap()[tsl, :], reads=[sc["co"]], writes=[cos_[b]])

                loads(0)
                for it, c in enumerate(order):
                    b = it % 2
                    tsl = slice(c * 64, (c + 1) * 64)
                    Q, K, Kt, vc, AT, v1, v2 = QTc[b], KTc[b], Ktk[b], vcs[b], ATs[b], v1s[b], v2s[b]
                    Cbf, nbf, Cbn, nbn = Cbfs[b], nbfs[b], Cbfs[1 - b], nbfs[1 - b]
                    if d == 1:
                        hf, co = hfs[b], cos_[b]
                    for h in range(8):
                        kb.op("pe", lambda e: e.matmul(psS[0:64, h * 64:(h + 1) * 64], lhsT=K[:, h, :], rhs=Q[:, h, :], start=True, stop=True),
                              reads=[K, Q], writes=[psS], track=(h == 7))
                    if it + 1 < NCH:
                        loads(it + 1)
                    yield
                    kb.op("dve", lambda e: e.tensor_tensor(out=AT[:], in0=psS[0:64, :].rearrange("p (h t) -> p h t", h=8),
                                                           in1=trib[:, d, :].unsqueeze(1).broadcast_to([64, 8, 64]), op=ALU.mult), reads=[psS, trib], writes=[AT])
                    kb.op("pool", lambda e: e.tensor_tensor(out=v1[:], in0=vc[:], in1=Wt[:, d, c, :].unsqueeze(2).broadcast_to([64, 8, 128]), op=ALU.mult),
                          reads=[vc, Wt], writes=[v1])
                    kb.op("pool", lambda e: e.tensor_tensor(out=v2[:], in0=vc[:], in1=W2t[:, d, c, :].unsqueeze(2).broadcast_to([64, 8, 128]), op=ALU.mult),
                          reads=[vc, W2t], writes=[v2])
                    yield
                    eb = EBt[:, d, c, :]
                    ebt = EBTt[:, d, c, :]
                    hd = hds[b]
                    s0, s1, s2, s3 = sm
                    for hf_ in range(2):
                        h0 = hf_ * 4
                        hs = slice(h0, h0 + 4)
                        for hh in range(4):
                            h = h0 + hh
                            kb.op("pe", lambda e: e.matmul(psN[0:64, hh * 128:(hh + 1) * 128], lhsT=AT[:, h, :], rhs=v1[:, h, :], start=True, stop=False),
                                  reads=[AT, v1], writes=[psN], track=False)
                            kb.op("pe", lambda e: e.matmul(psN[0:64, hh * 128:(hh + 1) * 128], lhsT=Q[:, h, :], rhs=Cbf[:, h, :], start=False, stop=True),
                                  reads=[Q, Cbf], writes=[psN], track=False)
                            kb.op("pe", lambda e: e.matmul(psD[0:64, h:h + 1], lhsT=AT[:, h, :], rhs=Wb[:, d, c, h:h + 1], start=True, stop=False),
                                  reads=[AT, Wb], writes=[psD], track=False)
                            kb.op("pe", lambda e: e.matmul(psD[0:64, h:h + 1], lhsT=Q[:, h, :], rhs=nbf[:, h:h + 1], start=False, stop=True),
                                  reads=[Q, nbf], writes=[psD], track=(hh == 3))
                        for hh in range(4):
                            h = h0 + hh
                            kb.op("pe", lambda e: e.matmul(psU[0:64, hh * 128:(hh + 1) * 128], lhsT=Kt[:, h * 64:(h + 1) * 64], rhs=v2[:, h, :], start=True, stop=True),
                                  reads=[Kt, v2], writes=[psU], track=False)
                            kb.op("pe", lambda e: e.matmul(psD[0:64, 8 + h:9 + h], lhsT=Kt[:, h * 64:(h + 1) * 64], rhs=W2b[:, d, c, h:h + 1], start=True, stop=True),
                                  reads=[Kt, W2b], writes=[psD], track=(hh == 3))
                        yield
                        kb.op("dve", lambda e: e.tensor_tensor(out=Cst[:, hs, :], in0=Cst[:, hs, :], in1=ebt[:, hs].unsqueeze(2).broadcast_to([64, 4, 128]), op=ALU.mult),
                              reads=[Cst, EBTt], writes=[Cst])
                        kb.op("dve", lambda e: e.tensor_tensor(out=Cst[:, hs, :], in0=Cst[:, hs, :], in1=psU[0:64, :].rearrange("p (h v) -> p h v", h=4), op=ALU.add),
                              reads=[Cst, psU], writes=[Cst])
                        kb.op("dve", lambda e: e.tensor_tensor(out=nst[:, hs], in0=nst[:, hs], in1=ebt[:, hs], op=ALU.mult), reads=[nst, EBTt], writes=[nst])
                        kb.op("dve", lambda e: e.tensor_tensor(out=nst[:, hs], in0=nst[:, hs], in1=psD[0:64, 8 + h0:12 + h0], op=ALU.add), reads=[nst, psD], writes=[nst])
                        kb.op("act", lambda e: e.activation(out=Cbn[:, hs, :], in_=Cst[:, hs, :], func=AF.Copy), reads=[Cst], writes=[Cbn])
                        kb.op("act", lambda e: e.activation(out=nbn[:, hs], in_=nst[:, hs], func=AF.Copy), reads=[nst], writes=[nbn])
                        kb.op("dve", lambda e: e.tensor_tensor(out=s0[:, hs], in0=psD[0:64, hs], in1=eb[:, hs], op=ALU.mult), reads=[psD, EBt], writes=[s0])
                        kb.op("dve", lambda e: e.scalar_tensor_tensor(out=s1[:, hs], in0=s0[:, hs], scalar=-1.0, in1=s0[:, hs], op0=ALU.mult, op1=ALU.max), reads=[s0], writes=[s1])
                        kb.op("dve", lambda e: e.tensor_scalar(out=s1[:, hs], in0=s1[:, hs], scalar1=1.0, scalar2=None, op0=ALU.max), reads=[s1], writes=[s1])
                        kb.op("dve", lambda e: e.reciprocal(out=s2[:, hs], in_=s1[:, hs]), reads=[s1], writes=[s2])
                        kb.op("dve", lambda e: e.tensor_tensor(out=s3[:, hs], in0=s2[:, hs], in1=eb[:, hs], op=ALU.mult), reads=[s2, EBt], writes=[s3])
                        kb.op("dve", lambda e: e.tensor_tensor(out=hd[:, hs, :], in0=psN[0:64, :].rearrange("p (h v) -> p h v", h=4),
                                                               in1=s3[:, hs].unsqueeze(2).broadcast_to([64, 4, 128]), op=ALU.mult), reads=[psN, s3], writes=[hd])
                        yield
                    if d == 0:
                        kb.dma("pool", sc["hfw"].ap()[tsl, :], hd[:].rearrange("p h v -> p (h v)"), reads=[hd], writes=[sc["hfw"]], waw=False)
                    else:
                        yb, yt = ybs[b], yts[b]
                        hdf = hd[:].rearrange("p h v -> p (h v)")
                        kb.op("pool", lambda e: e.tensor_tensor(out=hdf, in0=hdf, in1=hf[:], op=ALU.add), reads=[hd, hf], writes=[hd])
                        kb.op("act", lambda e: e.activation(out=junk[:], in_=hdf, func=AF.Square), reads=[hd], writes=[junk])
                        kb.op("dve", lambda e: e.tensor_reduce(out=s0[:], in_=junk[:].rearrange("p (h v) -> p h v", h=8), axis=AX.X, op=ALU.add), reads=[junk], writes=[s0])
                        kb.op("act", lambda e: e.activation(out=s1[:], in_=s0[:], func=AF.Ln, scale=1.0 / 128, bias=self.eps_t[0:64, 0:1]), reads=[s0, self.eps_t], writes=[s1])
                        kb.op("act", lambda e: e.activation(out=s1[:], in_=s1[:], func=AF.Exp, scale=-0.5), reads=[s1], writes=[s1])
                        kb.op("pool", lambda e: e.tensor_tensor(out=co[:], in0=co[:], in1=outg[:], op=ALU.mult), reads=[co, outg], writes=[co])
                        yield
                        kb.op("dve", lambda e: e.tensor_tensor(out=hd[:], in0=hd[:], in1=s1[:, :].unsqueeze(2).broadcast_to([64, 8, 128]), op=ALU.mult), reads=[hd, s1], writes=[hd])
                        kb.op("pool", lambda e: e.tensor_tensor(out=yb[:], in0=hdf, in1=co[:], op=ALU.mult), reads=[hd, co], writes=[yb])
                        yield
                        for h in range(8):
                            kb.op("pe", lambda e: e.transpose(out=psT[:, h * 64:(h + 1) * 64], in_=yb[:, h * 128:(h + 1) * 128], identity=self.ident_b[0:64, 0:64]),
                                  reads=[yb, self.ident_b], writes=[psT], track=(h == 7))
                        kb.op("act", lambda e: e.activation(out=yt[:], in_=psT[:, 0:512].rearrange("p (h t) -> p h t", h=8), func=AF.Copy), reads=[psT], writes=[yt])
                        kb.dma("pool", yv[:, :, tsl], yt[:], reads=[yt], writes=[sc["yT"]], waw=False)
                    yield
            yield

    def mix(self, L):
        kb, sc = self.kb, self.sc
        NCH = S // 64
        with kb.phase():
            KTs = [kb.sb("KT%d" % i, [128, S], BF16) for i in range(2)]
            Vs = [kb.sb("V%d" % i, [128, 32, 128], BF16) for i in range(2)]
            QTs = [kb.sb("QT%d" % i, [128, S], BF16) for i in range(2)]
            am = kb.sb("am", [128, 20, 512], BF16)
            PTs = [kb.sb("PT%d" % i, [128, 512], BF16) for i in range(4)]
            yos = [kb.sb("yo%d" % i, [128, 512], BF16) for i in range(2)]
            rzs = [kb.sb("rz%d" % i, [128, 512], F32) for i in range(2)]
            zas = [kb.sb("za%d" % i, [128, 512], F32) for i in range(2)]
            zbs = [kb.sb("zb%d" % i, [128, 512], F32) for i in range(2)]
            psS = [kb.ps("psS%d" % i, [128, 512]) for i in range(2)]
            psO = kb.ps("psO", [128, 512])
            amv = self.C["amask"].ap().rearrange("r p c -> p r c")
            for r0 in range(0, 20, 5):
                kb.dma("pool", am[:, r0:r0 + 5, :], amv[:, r0:r0 + 5, :], writes=[am])
            Wt = kb.sb("Wt", [64, 2, NCH, 8], F32)
            EBt = kb.sb("EBt", [64, 2, NCH, 8], F32)
            EBTt = kb.sb("EBTt", [64, 2, NCH, 8], F32)
            W2t = kb.sb("W2t", [64, 2, NCH, 8], F32)
            Wb = kb.sb("Wb", [64, 2, NCH, 8], BF16)
            W2b = kb.sb("W2b", [64, 2, NCH, 8], BF16)
            cm = kb.sb("cm", [64, 3, 64], F32)
            trib = kb.sb("trib", [64, 2, 64], BF16)
            outg = kb.sb("outg", [64, 1024], F32)
            kb.dma("sp", cm[:], self.C["cmats"].ap().rearrange("r p c -> p r c"), writes=[cm])
            kb.dma("pool", trib[:], self.C["cmats"].ap()[0:2].rearrange("r p c -> p r c"), writes=[trib])
            kb.dma("sp", outg[:], self.P["outg"].ap()[L].partition_broadcast(64), writes=[outg])
            ga = self.attn_gen(L, (KTs, Vs, QTs, am, PTs, yos, rzs, zas, zbs, psS, psO))
            gm = self.mlstm_gen(L, (Wt, EBt, EBTt, W2t, Wb, W2b, cm, trib, outg))
            RATIO = 3
            a_live = m_live = True
            while a_live or m_live:
                for _ in range(RATIO):
                    if a_live:
                        try:
                            next(ga)
                        except StopIteration:
                            a_live = False
                if m_live:
                    try:
                        next(gm)
                    except StopIteration:
                        m_live = False

    def build(self, phases=("p1", "mix", "p3a", "p3b")):
        kb = self.kb
        with kb.phase():
            self.consts_phase()
            self.cast_weights(0)
            for L in range(self.n_layers):
                xin = self.xT if L == 0 else self.xbuf[(L - 1) % 2]
                xout = self.outT if L == self.n_layers - 1 else self.xbuf[L % 2]
                if "p1" in phases:
                    self.p1(L, xin, 0)
                    self.p1(L, xin, 1)
                if L + 1 < self.n_layers:
                    for n, _, _ in WEIGHTS:
                        kb.wait_tile("pool", self.WB[L % 2][n])
                    self.cast_weights(L + 1)
                if "mix" in phases:
                    self.mix(L)
                if "p3a" in phases:
                    self.p3a(L, xin)
                if "p3b" in phases:
                    self.p3b(L, xout)
            kb.barrier()
        return self.nc


_PROG_CACHE = {}


def kernel(**inputs):
    x = np.asarray(inputs["x"], dtype=np.float32)
    B = x.shape[0]
    consts = make_consts()
    params = prep_params(inputs)
    if "nc" not in _PROG_CACHE:
        _PROG_CACHE["nc"] = Prog().build()
    nc = _PROG_CACHE["nc"]
    shared = {}
    for n, _, _ in WEIGHTS:
        shared[n] = np.ascontiguousarray(inputs[n], dtype=np.float32)
    shared.update(params)
    shared.update(consts)
    in_maps = []
    active = [0, 1, 4, 5]
    zero_x = np.zeros((D, S), np.float32)
    for c in range(NCORES):
        m = dict(shared)
        m["xT"] = np.ascontiguousarray(x[active.index(c)].T) if c in active else zero_x
        in_maps.append(m)
    res = run_bass_kernel_spmd(nc, in_maps, core_ids=list(range(NCORES)))
    out = np.stack([np.ascontiguousarray(res.results[active[b]]["outT"].T) for b in range(B)], axis=0)
    return out.astype(np.float32)
```

```python
import contextlib
import numpy as np
import concourse.bass as bass
import concourse.mybir as mybir
from concourse.bass_utils import run_bass_kernel_spmd

F32 = mybir.dt.float32
BF16 = mybir.dt.bfloat16
AF = mybir.ActivationFunctionType
ALU = mybir.AluOpType
AX = mybir.AxisListType

S = 4096
D = 2048
DEPTH = 4
NCORES = 8
IN_COLS = 13088
DFF = 8192
EPS = 1e-6
O_AQ, O_AK, O_AV, O_BQ, O_BK, O_BV, O_CQK, O_CV, O_CO, O_CG, O_G = 0, 768, 1536, 2304, 3328, 3584, 3840, 4864, 5888, 6912, 6944


class T:
    def __init__(self, kb, h, name):
        self.h = h
        self.name = name
        self.last_w = None
        self.readers = {}
        self.sem = None
        self.dcnt = 0
        self.kb = kb

    def ap(self):
        return self.h.ap()

    def __getitem__(self, idx):
        return self.h.__getitem__(idx)

    def get_sem(self):
        if self.sem is None:
            self.sem, self.dcnt = self.kb.take_sem(self.name)
        return self.sem


class KB:
    def __init__(self, nc):
        self.nc = nc
        self.engs = {"pe": nc.tensor, "act": nc.scalar, "dve": nc.vector, "pool": nc.gpsimd, "sp": nc.sync}
        self.sem = {e: nc.alloc_semaphore(name="s_" + e) for e in self.engs}
        self.cnt = {e: 0 for e in self.engs}
        self.waited = {e: {} for e in self.engs}
        self.sem_pool = []
        self.live_dma = {}
        self.n_ins = 0
        self.n_wait = 0
        self.uid = 0
        self.stack = None

    def take_sem(self, name):
        if self.sem_pool:
            return self.sem_pool.pop()
        self.uid += 1
        return self.nc.alloc_semaphore(name="d%d" % self.uid), 0

    def sb(self, name, shape, dt):
        self.uid += 1
        h = self.stack.enter_context(self.nc.sbuf_tensor("%s_%d" % (name, self.uid), list(shape), dt))
        t = T(self, h, name)
        self.phase_tiles.append(t)
        return t

    def ps(self, name, shape, dt=F32):
        self.uid += 1
        h = self.stack.enter_context(self.nc.psum_tensor("%s_%d" % (name, self.uid), list(shape), dt))
        t = T(self, h, name)
        self.phase_tiles.append(t)
        return t

    def dram(self, name, shape, dt, kind="Internal"):
        return T(self, self.nc.dram_tensor(name, list(shape), dt, kind=kind), name)

    @contextlib.contextmanager
    def phase(self):
        old = self.stack
        old_tiles = getattr(self, "phase_tiles", None)
        self.phase_tiles = []
        with contextlib.ExitStack() as st:
            self.stack = st
            yield
            self.barrier()
            for t in self.phase_tiles:
                if t.sem is not None:
                    self.sem_pool.append((t.sem, t.dcnt))
                    t.sem = None
        self.stack = old
        self.phase_tiles = old_tiles

    def _wait(self, eng, evs):
        w = self.waited[eng]
        need = {}
        for ev in evs:
            if ev is None:
                continue
            s, v = ev
            if eng == "pe" and s is self.sem["pe"]:
                continue
            if w.get(id(s), 0) >= v:
                continue
            if id(s) not in need or need[id(s)][1] < v:
                need[id(s)] = (s, v)
        for s, v in need.values():
            self.engs[eng].wait_ge(s, v)
            w[id(s)] = v
            self.n_wait += 1

    def _deps(self, reads, writes):
        evs = []
        for t in reads:
            evs.append(t.last_w)
        for t in writes:
            evs.append(t.last_w)
            evs.extend(t.readers.values())
        return evs

    def _add_reader(self, t, ev):
        s, v = ev
        cur = t.readers.get(id(s))
        if cur is None or cur[1] < v:
            t.readers[id(s)] = ev

    def op(self, eng, fn, reads=(), writes=(), track=True):
        self._wait(eng, self._deps(reads, writes))
        ins = fn(self.engs[eng])
        self.n_ins += 1
        if track:
            self.cnt[eng] += 1
            ins.then_inc(self.sem[eng], 1)
            ev = (self.sem[eng], self.cnt[eng])
        else:
            ev = (self.sem[eng], self.cnt[eng] + 1)
        for t in reads:
            self._add_reader(t, ev)
        for t in writes:
            t.last_w = ev
            t.readers = {}
        return ins

    def dma(self, q, out, in_, reads=(), writes=(), waw=True, **kw):
        evs = self._deps(reads, writes)
        if not waw:
            evs = [t.last_w for t in reads]
            for t in writes:
                evs.extend(t.readers.values())
        self._wait(q, evs)
        owner = writes[0]
        sem = owner.get_sem()
        ins = self.engs[q].dma_start(out=out, in_=in_, **kw)
        owner.dcnt += 16
        ins.then_inc(sem, 16)
        self.n_ins += 1
        ev = (sem, owner.dcnt)
        self.live_dma[id(sem)] = ev
        for t in reads:
            self._add_reader(t, ev)
        for t in writes:
            t.last_w = ev
            t.readers = {}
        return ins

    def dma_multi(self, items, reads=(), writes=()):
        evs = self._deps(reads, writes)
        owner = writes[0]
        sem = owner.get_sem()
        for q in dict.fromkeys(q for q, _, _ in items):
            self._wait(q, evs)
        for q, o, i in items:
            ins = self.engs[q].dma_start(out=o, in_=i)
            owner.dcnt += 16
            ins.then_inc(sem, 16)
            self.n_ins += 1
        ev = (sem, owner.dcnt)
        self.live_dma[id(sem)] = ev
        for t in reads:
            self._add_reader(t, ev)
        for t in writes:
            t.last_w = ev
            t.readers = {}

    def wait_tile(self, eng, t):
        self._wait(eng, [t.last_w])

    def barrier(self):
        evs = [(self.sem[e], self.cnt[e]) for e in self.engs if self.cnt[e] > 0]
        evs += list(self.live_dma.values())
        for e in self.engs:
            self._wait(e, evs)


def bcast_mid(ap2d, n, inner):
    return ap2d.rearrange("p (o i) -> p o i", o=1).broadcast_to([ap2d.shape[0], n, inner])


def _rope_tables_np(pos, dim):
    inv = (np.float32(10000.0) ** (-(np.arange(0, dim, 2, dtype=np.float32)) / np.float32(dim))).astype(np.float32)
    ang = pos.astype(np.float32)[:, None] * inv[None, :]
    return np.cos(ang).astype(np.float32), np.sin(ang).astype(np.float32)


def make_consts():
    c = {}
    pos = np.arange(S)
    cos1, sin1 = _rope_tables_np(pos, 128)
    ropeA = np.zeros((2, 128, S), np.float32)
    ropeA[0] = np.concatenate([cos1, cos1], 1).T
    ropeA[1] = np.concatenate([sin1, sin1], 1).T
    cr, sr = _rope_tables_np(pos // 64, 64)
    cc, sc = _rope_tables_np(pos % 64, 64)
    ropeB = np.zeros((2, 128, S), np.float32)
    ropeB[0] = np.concatenate([cr, cr, cc, cc], 1).T
    ropeB[1] = np.concatenate([sr, sr, sc, sc], 1).T
    c["ropeA"] = ropeA
    c["ropeB"] = ropeB
    RA = np.zeros((128, 128), np.float32)
    for dp in range(128):
        if dp < 64:
            RA[dp + 64, dp] = -1.0
        else:
            RA[dp - 64, dp] = 1.0
    RB = np.zeros((128, 128), np.float32)
    for dp in range(128):
        if dp % 64 < 32:
            RB[dp + 32, dp] = -1.0
        else:
            RB[dp - 32, dp] = 1.0
    c["rmats"] = np.stack([RA, RB])
    i = np.arange(128)[:, None]
    j = np.arange(512)[None, :]
    am = np.zeros((20, 128, 512), np.float32)
    for ri, r in enumerate(range(-8, 12)):
        delta = r * 128 + i - j
        m = np.zeros_like(delta, dtype=np.float32)
        for dil in (1, 4, 16):
            m += ((delta % dil == 0) & (np.abs(delta) <= 64 * dil)).astype(np.float32)
        am[ri] = m
    c["amask"] = am
    c["ident"] = np.eye(128, dtype=np.float32)
    u = np.arange(64)[:, None]
    t = np.arange(64)[None, :]
    c["cmats"] = np.stack([(u <= t), (u >= t), np.ones((64, 64), bool)]).astype(np.float32)
    return c


def prep_params(inp):
    p = {}
    f = lambda a: np.ascontiguousarray(a, dtype=np.float32)
    p["n1g"] = f(inp["norm1_g"].reshape(DEPTH, 16, 128).transpose(0, 2, 1))
    p["n2g"] = f(inp["norm2_g"].reshape(DEPTH, 16, 128).transpose(0, 2, 1))
    p["gains"] = f(np.stack([inp["a_q_gain"], inp["a_k_gain"], inp["b_q_gain"], inp["b_k_gain"]], -1))
    p["convw"] = f(inp["c_conv_w"].reshape(DEPTH, 5, 8, 128).transpose(0, 3, 2, 1))
    p["convb"] = f(inp["c_conv_b"].reshape(DEPTH, 8, 128).transpose(0, 2, 1))
    p["gateb"] = f(inp["c_gate_b"].reshape(DEPTH, 1, 32))
    p["outg"] = f(inp["c_out_g"].reshape(DEPTH, 1, 1024))
    return p


WEIGHTS = [("w_in", D, IN_COLS), ("w_branch_a", 768, D), ("w_branch_b", 1024, D), ("w_branch_c", 1024, D),
           ("w_out", D, D), ("w_up", D, DFF), ("w_down", DFF, D)]


class Prog:
    def __init__(self, n_layers=DEPTH, debug=False, stop_after=None):
        self.nc = nc = bass.Bass("TRN2", target_bir_lowering=False)
        self.kb = kb = KB(nc)
        self.n_layers = n_layers
        self.debug = debug
        self.stop_after = stop_after
        ext = "ExternalInput"
        self.xT = kb.dram("xT", [D, S], F32, kind=ext)
        self.W = {n: kb.dram(n, [n_layers, r, c], F32, kind=ext) for n, r, c in WEIGHTS}
        self.P = {}
        for n, shp in (("n1g", [DEPTH, 128, 16]), ("n2g", [DEPTH, 128, 16]), ("gains", [DEPTH, 128, 4]),
                       ("convw", [DEPTH, 128, 8, 5]), ("convb", [DEPTH, 128, 8]), ("gateb", [DEPTH, 1, 32]),
                       ("outg", [DEPTH, 1, 1024])):
            self.P[n] = kb.dram(n, shp, F32, kind=ext)
        self.C = {}
        for n, shp in (("ropeA", [2, 128, S]), ("ropeB", [2, 128, S]), ("rmats", [2, 128, 128]),
                       ("amask", [20, 128, 512]), ("ident", [128, 128]), ("cmats", [3, 64, 64])):
            self.C[n] = kb.dram(n, shp, F32, kind=ext)
        self.outT = kb.dram("outT", [D, S], F32, kind="ExternalOutput")
        sk = "ExternalOutput" if debug else "Internal"
        self.WB = [{n: kb.dram("wb%d_%s" % (par, n), [r, c], BF16) for n, r, c in WEIGHTS} for par in range(2)]
        self.sc = {}
        for n, shp, dt in (("aqT", [768, S], BF16), ("akT", [768, S], BF16), ("bqT", [1024, S], BF16),
                           ("bkT", [256, S], BF16), ("cqkT", [1024, S], F32), ("gT", [3 * D, S], BF16),
                           ("av", [S, 768], BF16), ("bv", [S, 256], BF16), ("cv", [S, 1024], F32),
                           ("co", [S, 1024], F32), ("cg", [S, 32], F32), ("yT", [2816, S], BF16),
                           ("hT", [D, S], F32), ("cqT", [512, S], BF16), ("ckT", [512, S], BF16),
                           ("ck", [S, 512], BF16), ("hfw", [S, 1024], F32), ("hbw", [S, 1024], F32)):
            self.sc[n] = kb.dram("sc_" + n, shp, dt, kind=sk)
        self.xbuf = [kb.dram("xbuf%d" % i, [D, S], F32) for i in range(2)]

    def cast_weights(self, L):
        kb = self.kb
        for n, r, c in WEIGHTS:
            dst = self.WB[L % 2][n]
            src = self.W[n]
            step = 256
            for r0 in range(0, r, step):
                kb.dma("pool", dst.ap()[r0:r0 + step, :], src.ap()[L, r0:r0 + step, :], writes=[dst], waw=False)

    def rmsnorm_T(self, src_ap_fn, ntok, gt, dstT, dst_tok0, xts, sqs, rs, psN, ones_f, sub=128):
        kb = self.kb
        for i in range(ntok // sub):
            xt = xts[i % len(xts)]
            src = src_ap_fn(i * sub, sub)
            kb.dma_multi([("sp", xt[:, 0:8, :], src[:, 0:8, :]), ("pool", xt[:, 8:16, :], src[:, 8:16, :])], writes=[xt])
            for kc in range(16):
                sq = sqs[kc % len(sqs)]
                kb.op("act", lambda e: e.activation(out=sq[:, 0:sub], in_=xt[:, kc, :], func=AF.Square), reads=[xt], writes=[sq])
                kb.op("pe", lambda e: e.matmul(psN[:, 0:sub], lhsT=ones_f[:], rhs=sq[:, 0:sub], start=(kc == 0), stop=(kc == 15)),
                      reads=[ones_f, sq], writes=[psN])
            kb.op("act", lambda e: e.activation(out=rs[:, 0:sub], in_=psN[:, 0:sub], func=AF.Ln, scale=1.0 / D, bias=self.eps_t[:, 0:1]),
                  reads=[psN, self.eps_t], writes=[rs])
            kb.op("act", lambda e: e.activation(out=rs[:, 0:sub], in_=rs[:, 0:sub], func=AF.Exp, scale=-0.5), reads=[rs], writes=[rs])
            c0 = dst_tok0 + i * sub
            for kc in range(16):
                kb.op("dve", lambda e: e.scalar_tensor_tensor(out=dstT[:, kc, c0:c0 + sub], in0=xt[:, kc, :], scalar=gt[:, kc:kc + 1],
                                                              in1=rs[:, 0:sub], op0=ALU.mult, op1=ALU.mult),
                      reads=[xt, gt, rs], writes=[dstT])

    def consts_phase(self):
        kb = self.kb
        self.eps_t = kb.sb("eps", [128, 1], F32)
        kb.op("dve", lambda e: e.memset(self.eps_t[:], EPS), writes=[self.eps_t])
        self.ones_f = kb.sb("ones_f", [128, 128], F32)
        kb.op("dve", lambda e: e.memset(self.ones_f[:], 1.0), writes=[self.ones_f])
        self.ones_b = kb.sb("ones_b", [128, 128], BF16)
        kb.op("dve", lambda e: e.memset(self.ones_b[:], 1.0), writes=[self.ones_b])
        self.ident_b = kb.sb("ident_b", [128, 128], BF16)
        kb.dma("pool", self.ident_b[:], self.C["ident"].ap()[:, :], writes=[self.ident_b])
        self.rmat = kb.sb("rmat", [128, 2, 128], BF16)
        kb.dma("pool", self.rmat[:], self.C["rmats"].ap().rearrange("r p c -> p r c"), writes=[self.rmat])


    def _qk_stages(self, ps, k, sg_, tsl, dsl, dst, sqs3, rss3, qnbs, t1s, t2s, obs, psRa, psRb, gains):
        kb = self.kb
        sq, rs, qnb = sqs3[k % 3], rss3[k % 3], qnbs[k % 3]
        t1, t2, ob = t1s[k % 2], t2s[k % 2], obs[k % 3]
        pa, pb = psRa[k % 2], psRb[k % 2]
        rope = sg_["rope"]
        gi = sg_["gi"]
        rm = sg_["rm"]

        def e1():
            kb.op("act", lambda e: e.activation(out=sq[:], in_=ps[:], func=AF.Square), reads=[ps], writes=[sq])

        def e2():
            kb.op("pe", lambda e: e.matmul(pa[:], lhsT=self.ones_f[:], rhs=sq[:], start=True, stop=True), reads=[self.ones_f, sq], writes=[pa])
            kb.op("act", lambda e: e.activation(out=rs[:], in_=pa[:], func=AF.Ln, scale=1.0 / 128, bias=self.eps_t[:, 0:1]), reads=[pa, self.eps_t], writes=[rs])
            kb.op("act", lambda e: e.activation(out=rs[:], in_=rs[:], func=AF.Exp, scale=-0.5), reads=[rs], writes=[rs])
            kb.op("dve", lambda e: e.scalar_tensor_tensor(out=qnb[:], in0=ps[:], scalar=gains[:, gi:gi + 1], in1=rs[:], op0=ALU.mult, op1=ALU.mult),
                  reads=[ps, gains, rs], writes=[qnb])

        def e3():
            kb.op("pe", lambda e: e.matmul(pb[:], lhsT=self.rmat[:, rm, :], rhs=qnb[:], start=True, stop=True), reads=[self.rmat, qnb], writes=[pb])
            kb.op("dve", lambda e: e.tensor_tensor(out=t1[:], in0=qnb[:], in1=rope[:, 0, tsl], op=ALU.mult), reads=[qnb, rope], writes=[t1])
            kb.op("dve", lambda e: e.tensor_tensor(out=t2[:], in0=pb[:], in1=rope[:, 1, tsl], op=ALU.mult), reads=[pb, rope], writes=[t2])
            kb.op("pool", lambda e: e.tensor_tensor(out=ob[:], in0=t1[:], in1=t2[:], op=ALU.add), reads=[t1, t2], writes=[ob])
            kb.dma("pool", dsl, ob[:], reads=[ob], writes=[dst], waw=False)

        return [e1, e2, e3]

    def p1(self, L, xin, sg):
        kb, sc = self.kb, self.sc
        NT = 2048
        T0 = sg * NT
        WB = self.WB[L % 2]["w_in"]
        with kb.phase():
            xnT = kb.sb("xnT", [128, 16, NT], BF16)
            xts = [kb.sb("xt%d" % i, [128, 16, 128], F32) for i in range(2)]
            sqs = [kb.sb("sq%d" % i, [128, 512], F32) for i in range(2)]
            rss = [kb.sb("rs%d" % i, [128, 512], F32) for i in range(2)]
            wts = [kb.sb("wt%d" % i, [128, 16, 512], BF16) for i in range(2)]
            g1 = kb.sb("g1", [128, 16], F32)
            gains = kb.sb("gains", [128, 4], F32)
            ropeA = kb.sb("ropeA", [128, 2, NT], F32)
            ropeB = kb.sb("ropeB", [128, 2, NT], F32)
            qnbs = [kb.sb("qnb%d" % i, [128, 512], BF16) for i in range(3)]
            sqs3 = sqs + [kb.sb("sq2", [128, 512], F32)]
            rss3 = rss + [kb.sb("rs2", [128, 512], F32)]
            t1s = [kb.sb("t1%d" % i, [128, 512], F32) for i in range(2)]
            t2s = [kb.sb("t2%d" % i, [128, 512], F32) for i in range(2)]
            obs = [kb.sb("ob%d" % i, [128, 512], BF16) for i in range(3)]
            ofs = [kb.sb("of%d" % i, [128, 512], F32) for i in range(3)]
            psX = [kb.ps("psX%d" % i, [128, 512]) for i in range(3)]
            psRa = [kb.ps("psRa%d" % i, [128, 512]) for i in range(2)]
            psRb = [kb.ps("psRb%d" % i, [128, 512]) for i in range(2)]
            psN = kb.ps("psN", [128, 512])
            pend = []
            qcnt = 0
            kb.dma("sp", g1[:], self.P["n1g"].ap()[L], writes=[g1])
            kb.dma("sp", gains[:], self.P["gains"].ap()[L], writes=[gains])
            for j in range(2):
                kb.dma("sp", ropeA[:, j, :], self.C["ropeA"].ap()[j, :, T0:T0 + NT], writes=[ropeA])
                kb.dma("sp", ropeB[:, j, :], self.C["ropeB"].ap()[j, :, T0:T0 + NT], writes=[ropeB])
            xv = xin.ap().rearrange("(kc p) t -> p kc t", p=128)
            self.rmsnorm_T(lambda t0, n: xv[:, :, T0 + t0:T0 + t0 + n], NT, g1, xnT, 0, xts, sqs, rss[0], psN, self.ones_f)

            segs = []
            def add(col0, n, kind, dst, drow0=0, **kw):
                c = 0
                while c < n:
                    w = min(512, n - c)
                    segs.append(dict(col0=col0 + c, n=w, kind=kind, dst=dst, d0=drow0 + c, **kw))
                    c += w
            add(O_AQ, 768, "qk", sc["aqT"], gi=0, rope=ropeA, rm=0)
            add(O_AK, 768, "qk", sc["akT"], gi=1, rope=ropeA, rm=0)
            add(O_BQ, 1024, "qk", sc["bqT"], gi=2, rope=ropeB, rm=1)
            add(O_BK, 256, "qk", sc["bkT"], gi=3, rope=ropeB, rm=1)
            add(O_CQK, 1024, "copyT", sc["cqkT"])
            add(O_G, 3 * D, "sigT", sc["gT"])
            add(O_AV, 768, "tok_bf", sc["av"])
            add(O_BV, 256, "tok_bf", sc["bv"])
            add(O_CV, 1024, "tok_f", sc["cv"])
            add(O_CO, 1024, "tok_sig", sc["co"])
            add(O_CG, 32, "tok_f", sc["cg"])
            wv = WB.ap().rearrange("(kc p) c -> p kc c", p=128)
            cnt = 0
            ocnt = 0
            for si, sg_ in enumerate(segs):
                wt = wts[si % 2]
                n = sg_["n"]
                c0 = sg_["col0"]
                kb.dma_multi([("sp", wt[:, 0:8, 0:n], wv[:, 0:8, c0:c0 + n]), ("sp", wt[:, 8:16, 0:n], wv[:, 8:16, c0:c0 + n])], reads=[WB], writes=[wt])
                kind = sg_["kind"]
                dst = sg_["dst"]
                if kind in ("qk", "copyT", "sigT"):
                    for cc in range(n // 128):
                        r0 = sg_["d0"] + cc * 128
                        for tg in range(NT // 512):
                            ps = psX[cnt % 3]
                            cnt += 1
                            for kc in range(16):
                                kb.op("pe", lambda e: e.matmul(ps[:, :], lhsT=wt[:, kc, cc * 128:(cc + 1) * 128], rhs=xnT[:, kc, tg * 512:(tg + 1) * 512],
                                                               start=(kc == 0), stop=(kc == 15)), reads=[wt, xnT], writes=[ps], track=(kc == 15))
                            tsl = slice(tg * 512, (tg + 1) * 512)
                            dsl = dst.ap()[r0:r0 + 128, T0 + tg * 512:T0 + (tg + 1) * 512]
                            if kind == "sigT":
                                ob = obs[ocnt % 3]
                                ocnt += 1
                                kb.op("act", lambda e: e.activation(out=ob[:], in_=ps[:], func=AF.Sigmoid), reads=[ps], writes=[ob])
                                kb.dma("pool", dsl, ob[:], reads=[ob], writes=[dst], waw=False)
                            elif kind == "copyT":
                                of = ofs[ocnt % 3]
                                ocnt += 1
                                kb.op("dve", lambda e: e.tensor_copy(out=of[:], in_=ps[:]), reads=[ps], writes=[of])
                                kb.dma("pool", dsl, of[:], reads=[of], writes=[dst], waw=False)
                            else:
                                pend.append(self._qk_stages(ps, qcnt, sg_, tsl, dsl, dst, sqs3, rss3, qnbs, t1s, t2s, obs, psRa, psRb, gains))
                                qcnt += 1
                            for st in list(pend):
                                st.pop(0)()
                                if not st:
                                    pend.remove(st)
                    while pend:
                        for st in list(pend):
                            st.pop(0)()
                            if not st:
                                pend.remove(st)
                else:
                    for tt in range(NT // 128):
                        ps = psX[cnt % 3]
                        cnt += 1
                        for kc in range(16):
                            kb.op("pe", lambda e: e.matmul(ps[:, 0:n], lhsT=xnT[:, kc, tt * 128:(tt + 1) * 128], rhs=wt[:, kc, 0:n],
                                                           start=(kc == 0), stop=(kc == 15)), reads=[wt, xnT], writes=[ps], track=(kc == 15))
                        dsl = dst.ap()[T0 + tt * 128:T0 + (tt + 1) * 128, sg_["d0"]:sg_["d0"] + n]
                        if kind == "tok_bf":
                            ob = obs[ocnt % 3]
                            ocnt += 1
                            kb.op("dve", lambda e: e.tensor_copy(out=ob[:, 0:n], in_=ps[:, 0:n]), reads=[ps], writes=[ob])
                            kb.dma("pool", dsl, ob[:, 0:n], reads=[ob], writes=[dst], waw=False)
                        else:
                            of = ofs[ocnt % 3]
                            ocnt += 1
                            if kind == "tok_sig":
                                kb.op("act", lambda e: e.activation(out=of[:, 0:n], in_=ps[:, 0:n], func=AF.Sigmoid), reads=[ps], writes=[of])
                            else:
                                kb.op("dve", lambda e: e.tensor_copy(out=of[:, 0:n], in_=ps[:, 0:n]), reads=[ps], writes=[of])
                            kb.dma("pool", dsl, of[:, 0:n], reads=[of], writes=[dst], waw=False)

    def p3a(self, L, xin):
        kb, sc = self.kb, self.sc
        WBs = self.WB[L % 2]
        br = [("w_branch_a", 0, 6), ("w_branch_b", 6, 8), ("w_branch_c", 14, 8)]
        with kb.phase():
            yTs = [kb.sb("yT%d" % i, [128, 22, 512], BF16) for i in range(2)]
            xts_ = [kb.sb("xt%d" % i, [128, 16, 512], F32) for i in range(2)]
            mT = kb.sb("mT", [128, 16, 512], BF16)
            wbs = [kb.sb("wb%d" % i, [128, 22, 256], BF16) for i in range(2)]
            wos = [kb.sb("wo%d" % i, [128, 16, 256], BF16) for i in range(2)]
            gts = [kb.sb("gt%d" % i, [128, 3, 512], BF16) for i in range(2)]
            ms = [kb.sb("m%d" % i, [128, 512], F32) for i in range(2)]
            tmps = [kb.sb("tmp%d" % i, [128, 512], F32) for i in range(2)]
            psX = [kb.ps("psX%d" % i, [128, 512]) for i in range(4)]
            yv = sc["yT"].ap().rearrange("(kc p) t -> p kc t", p=128)
            xv = xin.ap().rearrange("(kc p) t -> p kc t", p=128)
            hv = sc["hT"].ap().rearrange("(kc p) t -> p kc t", p=128)
            gv = sc["gT"].ap().rearrange("(b oc p) t -> p b oc t", p=128, b=3)
            wvs = [WBs[n].ap().rearrange("(kc p) c -> p kc c", p=128) for n, _, _ in br]
            wov = WBs["w_out"].ap().rearrange("(kc p) c -> p kc c", p=128)
            cnt = 0
            def ld(tg):
                tsl_ = slice(tg * 512, (tg + 1) * 512)
                yT_, xt_ = yTs[tg % 2], xts_[tg % 2]
                kb.dma_multi([("sp", yT_[:, 0:11, :], yv[:, 0:11, tsl_]), ("sp", yT_[:, 11:22, :], yv[:, 11:22, tsl_])], reads=[sc["yT"]], writes=[yT_])
                kb.dma_multi([("sp", xt_[:, 0:8, :], xv[:, 0:8, tsl_]), ("sp", xt_[:, 8:16, :], xv[:, 8:16, tsl_])], reads=[xin], writes=[xt_])

            ld(0)
            for tg in range(S // 512):
                tsl = slice(tg * 512, (tg + 1) * 512)
                yT, xt = yTs[tg % 2], xts_[tg % 2]
                for o2 in range(8):
                    if o2 == 2 and tg + 1 < S // 512:
                        ld(tg + 1)
                    wb = wbs[o2 % 2]
                    for (n, k0, nk), wv in zip(br, wvs):
                        kb.dma("sp", wb[:, k0:k0 + nk, :], wv[:, :, o2 * 256:(o2 + 1) * 256], reads=[WBs[n]], writes=[wb])
                    for oi in range(2):
                        oc = o2 * 2 + oi
                        gt = gts[oc % 2]
                        kb.dma("sp", gt[:], gv[:, :, oc, tsl], reads=[sc["gT"]], writes=[gt])
                        m = ms[oc % 2]
                        for bi, (n, k0, nk) in enumerate(br):
                            ps = psX[cnt % 4]
                            cnt += 1
                            for kc in range(nk):
                                kb.op("pe", lambda e: e.matmul(ps[:], lhsT=wb[:, k0 + kc, oi * 128:(oi + 1) * 128], rhs=yT[:, k0 + kc, :],
                                                               start=(kc == 0), stop=(kc == nk - 1)), reads=[wb, yT], writes=[ps], track=(kc == nk - 1))
                            if bi == 0:
                                kb.op("dve", lambda e: e.tensor_tensor(out=m[:], in0=ps[:], in1=gt[:, 0, :], op=ALU.mult), reads=[ps, gt], writes=[m])
                            else:
                                tmp = tmps[bi % 2]
                                kb.op("dve", lambda e: e.tensor_tensor(out=tmp[:], in0=ps[:], in1=gt[:, bi, :], op=ALU.mult), reads=[ps, gt], writes=[tmp])
                                if bi == 1:
                                    kb.op("pool", lambda e: e.tensor_tensor(out=m[:], in0=m[:], in1=tmp[:], op=ALU.add), reads=[m, tmp], writes=[m])
                                else:
                                    kb.op("pool", lambda e: e.tensor_tensor(out=mT[:, oc, :], in0=m[:], in1=tmp[:], op=ALU.add), reads=[m, tmp], writes=[mT])
                for o2 in range(8):
                    wo = wos[o2 % 2]
                    kb.dma("sp", wo[:], wov[:, :, o2 * 256:(o2 + 1) * 256], reads=[WBs["w_out"]], writes=[wo])
                    for oi in range(2):
                        oc = o2 * 2 + oi
                        ps = psX[cnt % 4]
                        cnt += 1
                        for kc in range(16):
                            kb.op("pe", lambda e: e.matmul(ps[:], lhsT=wo[:, kc, oi * 128:(oi + 1) * 128], rhs=mT[:, kc, :],
                                                           start=(kc == 0), stop=(kc == 15)), reads=[wo, mT], writes=[ps], track=(kc == 15))
                        kb.op("dve", lambda e: e.tensor_tensor(out=xt[:, oc, :], in0=ps[:], in1=xt[:, oc, :], op=ALU.add), reads=[ps, xt], writes=[xt])
                kb.dma("pool", hv[:, 0:8, tsl], xt[:, 0:8, :], reads=[xt], writes=[sc["hT"]], waw=False)
                kb.dma("pool", hv[:, 8:16, tsl], xt[:, 8:16, :], reads=[xt], writes=[sc["hT"]], waw=False)

    def p3b(self, L, xout):
        kb, sc = self.kb, self.sc
        WBs = self.WB[L % 2]
        with kb.phase():
            hT = kb.sb("hT", [128, 16, 512], F32)
            hnT = kb.sb("hnT", [128, 16, 512], BF16)
            uT = kb.sb("uT", [128, 64, 512], BF16)
            sqs = [kb.sb("sq%d" % i, [128, 512], F32) for i in range(2)]
            rs = kb.sb("rs", [128, 512], F32)
            g2 = kb.sb("g2", [128, 16], F32)
            wus = [kb.sb("wu%d" % i, [128, 16, 256], BF16) for i in range(2)]
            wds = [kb.sb("wd%d" % i, [128, 64, 128], BF16) for i in range(2)]
            rls = [kb.sb("rl%d" % i, [128, 512], F32) for i in range(2)]
            psX = [kb.ps("psX%d" % i, [128, 512]) for i in range(4)]
            psN = kb.ps("psN", [128, 512])
            kb.dma("sp", g2[:], self.P["n2g"].ap()[L], writes=[g2])
            hv = sc["hT"].ap().rearrange("(kc p) t -> p kc t", p=128)
            ov = xout.ap().rearrange("(kc p) t -> p kc t", p=128)
            wuv = WBs["w_up"].ap().rearrange("(kc p) c -> p kc c", p=128)
            wdv = WBs["w_down"].ap().rearrange("(fc p) c -> p fc c", p=128)
            cnt = 0
            for tg in range(S // 512):
                tsl = slice(tg * 512, (tg + 1) * 512)
                kb.dma_multi([("sp", hT[:, 0:8, :], hv[:, 0:8, tsl]), ("pool", hT[:, 8:16, :], hv[:, 8:16, tsl])], reads=[sc["hT"]], writes=[hT])
                for kc in range(16):
                    sq = sqs[kc % 2]
                    kb.op("act", lambda e: e.activation(out=sq[:], in_=hT[:, kc, :], func=AF.Square), reads=[hT], writes=[sq])
                    kb.op("pe", lambda e: e.matmul(psN[:], lhsT=self.ones_f[:], rhs=sq[:], start=(kc == 0), stop=(kc == 15)),
                          reads=[self.ones_f, sq], writes=[psN])
                kb.op("act", lambda e: e.activation(out=rs[:], in_=psN[:], func=AF.Ln, scale=1.0 / D, bias=self.eps_t[:, 0:1]), reads=[psN, self.eps_t], writes=[rs])
                kb.op("act", lambda e: e.activation(out=rs[:], in_=rs[:], func=AF.Exp, scale=-0.5), reads=[rs], writes=[rs])
                for kc in range(16):
                    kb.op("dve", lambda e: e.scalar_tensor_tensor(out=hnT[:, kc, :], in0=hT[:, kc, :], scalar=g2[:, kc:kc + 1], in1=rs[:],
                                                                  op0=ALU.mult, op1=ALU.mult), reads=[hT, g2, rs], writes=[hnT])
                for f2 in range(32):
                    wu = wus[f2 % 2]
                    kb.dma("sp", wu[:], wuv[:, :, f2 * 256:(f2 + 1) * 256], reads=[WBs["w_up"]], writes=[wu])
                    for fi in range(2):
                        fc = f2 * 2 + fi
                        ps = psX[cnt % 4]
                        cnt += 1
                        for kc in range(16):
                            kb.op("pe", lambda e: e.matmul(ps[:], lhsT=wu[:, kc, fi * 128:(fi + 1) * 128], rhs=hnT[:, kc, :],
                                                           start=(kc == 0), stop=(kc == 15)), reads=[wu, hnT], writes=[ps], track=(kc == 15))
                        rl = rls[fc % 2]
                        kb.op("act", lambda e: e.activation(out=rl[:], in_=ps[:], func=AF.Relu), reads=[ps], writes=[rl])
                        kb.op("pool" if fc % 2 else "dve", lambda e: e.tensor_tensor(out=uT[:, fc, :], in0=rl[:], in1=rl[:], op=ALU.mult), reads=[rl], writes=[uT])
                for oc in range(16):
                    wd = wds[oc % 2]
                    kb.dma_multi([("sp", wd[:, 0:32, :], wdv[:, 0:32, oc * 128:(oc + 1) * 128]), ("sp", wd[:, 32:64, :], wdv[:, 32:64, oc * 128:(oc + 1) * 128])],
                                 reads=[WBs["w_down"]], writes=[wd])
                    ps = psX[cnt % 4]
                    cnt += 1
                    for fc in range(64):
                        kb.op("pe", lambda e: e.matmul(ps[:], lhsT=wd[:, fc, :], rhs=uT[:, fc, :], start=(fc == 0), stop=(fc == 63)),
                              reads=[wd, uT], writes=[ps], track=(fc == 63))
                    kb.op("dve", lambda e: e.tensor_tensor(out=hT[:, oc, :], in0=ps[:], in1=hT[:, oc, :], op=ALU.add), reads=[ps, hT], writes=[hT])
                kb.dma("pool", ov[:, 0:8, tsl], hT[:, 0:8, :], reads=[hT], writes=[xout], waw=False)
                kb.dma("pool", ov[:, 8:16, tsl], hT[:, 8:16, :], reads=[hT], writes=[xout], waw=False)

    def attn(self, L):
        kb, sc = self.kb, self.sc
        SKEW = 2
        scale = 1.0 / np.sqrt(128.0)
        with kb.phase():
            KTs = [kb.sb("KT%d" % i, [128, S], BF16) for i in range(2)]
            Vs = [kb.sb("V%d" % i, [128, 32, 128], BF16) for i in range(2)]
            QTs = [kb.sb("QT%d" % i, [128, S], BF16) for i in range(2)]
            am = kb.sb("am", [128, 20, 512], BF16)
            PTs = [kb.sb("PT%d" % i, [128, 512], BF16) for i in range(4)]
            yos = [kb.sb("yo%d" % i, [128, 512], BF16) for i in range(2)]
            rzs = [kb.sb("rz%d" % i, [128, 512], F32) for i in range(2)]
            psS = [kb.ps("psS%d" % i, [128, 512]) for i in range(3)]
            psO = [kb.ps("psO%d" % i, [128, 512]) for i in range(2)]
            psZ = [kb.ps("psZ%d" % i, [128, 512]) for i in range(2)]
            amv = self.C["amask"].ap().rearrange("r p c -> p r c")
            for r0 in range(0, 20, 5):
                kb.dma("pool", am[:, r0:r0 + 5, :], amv[:, r0:r0 + 5, :], writes=[am])
            jobs = []
            for h in range(6):
                jobs.append(("A", sc["aqT"], h * 128, sc["akT"], h * 128, sc["av"], h * 128, h * 128, ("A", h)))
            for h in range(8):
                kvh = h // 4
                jobs.append(("B", sc["bqT"], h * 128, sc["bkT"], kvh * 128, sc["bv"], kvh * 128, 768 + h * 128, ("B", kvh)))
            cur_kv = None
            nkv = 0
            g = 0
            for ji, (kind, qsrc, q0, ksrc, k0, vsrc, v0, y0, kvid) in enumerate(jobs):
                QT = QTs[ji % 2]
                kb.dma("sp", QT[:], qsrc.ap()[q0:q0 + 128, :], reads=[qsrc], writes=[QT])
                if kvid != cur_kv:
                    cur_kv = kvid
                    KT = KTs[nkv % 2]
                    V = Vs[nkv % 2]
                    nkv += 1
                    kb.dma("sp", KT[:], ksrc.ap()[k0:k0 + 128, :], reads=[ksrc], writes=[KT])
                    vv = vsrc.ap()[:, v0:v0 + 128].rearrange("(kc p) d -> p kc d", p=128)
                    for c0 in range(0, 32, 8):
                        kb.dma("sp", V[:, c0:c0 + 8, :], vv[:, c0:c0 + 8, :], reads=[vsrc], writes=[V])
                for qg in range(8):
                    kcs = list(range(32)) if kind == "B" else list(range(max(0, 4 * qg - 8), min(32, 4 * qg + 12)))
                    pso, psz = psO[g % 2], psZ[g % 2]
                    n = len(kcs)
                    for i in range(n + SKEW):
                        if i < n:
                            kc = kcs[i]
                            ps = psS[i % 3]
                            PT = PTs[i % 4]
                            kb.op("pe", lambda e: e.matmul(ps[:], lhsT=KT[:, kc * 128:(kc + 1) * 128], rhs=QT[:, qg * 512:(qg + 1) * 512], start=True, stop=True),
                                  reads=[KT, QT], writes=[ps])
                            kb.op("act", lambda e: e.activation(out=PT[:], in_=ps[:], func=AF.Exp, scale=float(scale)), reads=[ps], writes=[PT])
                            if kind == "A":
                                r = kc - 4 * qg + 8
                                kb.op("dve", lambda e: e.tensor_tensor(out=PT[:], in0=PT[:], in1=am[:, r, :], op=ALU.mult), reads=[PT, am], writes=[PT])
                        if i >= SKEW:
                            j = i - SKEW
                            kc = kcs[j]
                            PT = PTs[j % 4]
                            kb.op("pe", lambda e: e.matmul(pso[:], lhsT=V[:, kc, :], rhs=PT[:], start=(j == 0), stop=(j == n - 1)),
                                  reads=[V, PT], writes=[pso], track=(j == n - 1))
                            kb.op("pe", lambda e: e.matmul(psz[:], lhsT=self.ones_b[:], rhs=PT[:], start=(j == 0), stop=(j == n - 1)),
                                  reads=[self.ones_b, PT], writes=[psz], track=True)
                    rz, yo = rzs[g % 2], yos[g % 2]
                    kb.op("dve", lambda e: e.reciprocal(out=rz[:], in_=psz[:]), reads=[psz], writes=[rz])
                    kb.op("dve", lambda e: e.tensor_tensor(out=yo[:], in0=pso[:], in1=rz[:], op=ALU.mult), reads=[pso, rz], writes=[yo])
                    kb.dma("pool", sc["yT"].ap()[y0:y0 + 128, qg * 512:(qg + 1) * 512], yo[:], reads=[yo], writes=[sc["yT"]], waw=False)
                    g += 1

    def mlstm(self, L):
        kb, sc = self.kb, self.sc
        NCH = S // 64
        with kb.phase():
            Wt = kb.sb("Wt", [64, 2, NCH, 8], F32)
            EBt = kb.sb("EBt", [64, 2, NCH, 8], F32)
            EBTt = kb.sb("EBTt", [64, 2, NCH, 8], F32)
            W2t = kb.sb("W2t", [64, 2, NCH, 8], F32)
            Wb = kb.sb("Wb", [64, 2, NCH, 8], BF16)
            W2b = kb.sb("W2b", [64, 2, NCH, 8], BF16)
            cm = kb.sb("cm", [64, 3, 64], F32)
            trib = kb.sb("trib", [64, 2, 64], BF16)
            outg = kb.sb("outg", [64, 1024], F32)
            kb.dma("sp", cm[:], self.C["cmats"].ap().rearrange("r p c -> p r c"), writes=[cm])
            kb.dma("pool", trib[:], self.C["cmats"].ap()[0:2].rearrange("r p c -> p r c"), writes=[trib])
            kb.dma("sp", outg[:], self.P["outg"].ap()[L].partition_broadcast(64), writes=[outg])

            with kb.phase():
                G = kb.sb("G", [64, NCH, 32], F32)
                gb = kb.sb("gb", [64, 32], F32)
                nlf = kb.sb("nlf", [64, 2, NCH, 8], F32)
                tmp = kb.sb("gtmp", [64, 2, NCH, 8], F32)
                psA = [kb.ps("psg%d" % i, [128, 512]) for i in range(2)]
                psB = [kb.ps("psh%d" % i, [128, 512]) for i in range(2)]
                cgv = sc["cg"].ap().rearrange("(c t) g -> t c g", t=64)
                for c0 in range(0, NCH, 16):
                    kb.dma("sp", G[:, c0:c0 + 16, :], cgv[:, c0:c0 + 16, :], reads=[sc["cg"]], writes=[G])
                kb.dma("sp", gb[:], self.P["gateb"].ap()[L].partition_broadcast(64), writes=[gb])
                kb.op("dve", lambda e: e.tensor_tensor(out=G[:], in0=G[:], in1=gb[:, :].unsqueeze(1).broadcast_to([64, NCH, 32]), op=ALU.add),
                      reads=[G, gb], writes=[G])
                for d in range(2):
                    fcol = 8 + 16 * d
                    kb.op("act", lambda e: e.activation(out=nlf[:, d], in_=G[:, :, fcol:fcol + 8], func=AF.Exp, scale=-1.0), reads=[G], writes=[nlf])
                    kb.op("act", lambda e: e.activation(out=nlf[:, d], in_=nlf[:, d], func=AF.Ln, bias=1.0), reads=[nlf], writes=[nlf])
                for d in range(2):
                    icol = 16 * d
                    rhs = nlf[:, d].rearrange("p c h -> p (c h)")
                    kb.op("pe", lambda e: e.matmul(psA[d][0:64, :], lhsT=cm[:, d, :], rhs=rhs, start=True, stop=True), reads=[cm, nlf], writes=[psA[d]])
                    kb.op("pe", lambda e: e.matmul(psB[d][0:64, :], lhsT=cm[:, 2, :], rhs=rhs, start=True, stop=True), reads=[cm, nlf], writes=[psB[d]])
                    csv = psA[d][0:64, :].rearrange("p (c h) -> p c h", h=8)
                    kb.op("dve", lambda e: e.tensor_tensor(out=tmp[:, d], in0=csv, in1=G[:, :, icol:icol + 8], op=ALU.add), reads=[psA[d], G], writes=[tmp])
                    kb.op("act", lambda e: e.activation(out=Wt[:, d], in_=tmp[:, d], func=AF.Exp), reads=[tmp], writes=[Wt])
                    kb.op("act", lambda e: e.activation(out=EBt[:, d], in_=csv, func=AF.Exp, scale=-1.0), reads=[psA[d]], writes=[EBt])
                    kb.op("act", lambda e: e.activation(out=EBTt[:, d], in_=psB[d][0:64, :].rearrange("p (c h) -> p c h", h=8), func=AF.Exp, scale=-1.0),
                          reads=[psB[d]], writes=[EBTt])
                    kb.op("dve", lambda e: e.tensor_tensor(out=W2t[:, d], in0=Wt[:, d], in1=EBTt[:, d], op=ALU.mult), reads=[Wt, EBTt], writes=[W2t])
                kb.op("dve", lambda e: e.tensor_copy(out=Wb[:], in_=Wt[:]), reads=[Wt], writes=[Wb])
                kb.op("dve", lambda e: e.tensor_copy(out=W2b[:], in_=W2t[:]), reads=[W2t], writes=[W2b])

            with kb.phase():
                xs = [kb.sb("cx%d" % i, [128, 8, 516], F32) for i in range(2)]
                cw = kb.sb("cw", [128, 8, 5], F32)
                cb = kb.sb("cb", [128, 8], F32)
                accs = [kb.sb("acc%d" % i, [128, 512], F32) for i in range(2)]
                qbs = [kb.sb("qb%d" % i, [128, 512], BF16) for i in range(3)]
                kfs = [kb.sb("kf%d" % i, [128, 512], F32) for i in range(2)]
                kfm = kb.sb("kfm", [128, 4, 512], BF16)
                kts = [kb.sb("kt%d" % i, [128, 512], BF16) for i in range(2)]
                psT = [kb.ps("psT%d" % i, [128, 1024], BF16) for i in range(2)]
                kb.dma("sp", cw[:], self.P["convw"].ap()[L], writes=[cw])
                kb.dma("sp", cb[:], self.P["convb"].ap()[L], writes=[cb])
                cv_ = sc["cqkT"].ap().rearrange("(c p) t -> p c t", p=128)
                for tg in range(8):
                    x = xs[tg % 2]
                    t0 = tg * 512
                    lo, hi = max(0, t0 - 2), min(S, t0 + 514)
                    if tg == 0:
                        kb.op("dve", lambda e: e.memset(x[:, :, 0:2], 0.0), writes=[x])
                    if tg == 7:
                        kb.op("dve", lambda e: e.memset(x[:, :, 514:516], 0.0), writes=[x])
                    b0 = lo - (t0 - 2)
                    kb.dma("sp", x[:, 0:4, b0:b0 + (hi - lo)], cv_[:, 0:4, lo:hi], reads=[sc["cqkT"]], writes=[x])
                    kb.dma("sp", x[:, 4:8, b0:b0 + (hi - lo)], cv_[:, 4:8, lo:hi], reads=[sc["cqkT"]], writes=[x])
                    for c in range(8):
                        acc = accs[c % 2]
                        kb.op("dve", lambda e: e.tensor_scalar(out=acc[:], in0=x[:, c, 0:512], scalar1=cw[:, c, 0:1], scalar2=None, op0=ALU.mult),
                              reads=[x, cw], writes=[acc])
                        for j in range(1, 5):
                            kb.op("dve", lambda e: e.scalar_tensor_tensor(out=acc[:], in0=x[:, c, j:j + 512], scalar=cw[:, c, j:j + 1], in1=acc[:],
                                                                          op0=ALU.mult, op1=ALU.add), reads=[x, cw, acc], writes=[acc])
                        if c < 4:
                            qb = qbs[c % 3]
                            kb.op("act", lambda e: e.activation(out=qb[:], in_=acc[:], func=AF.Silu, bias=cb[:, c:c + 1]), reads=[acc, cb], writes=[qb])
                            kb.dma("pool", sc["cqT"].ap()[c * 128:(c + 1) * 128, t0:t0 + 512], qb[:], reads=[qb], writes=[sc["cqT"]], waw=False)
                        else:
                            kf = kfs[c % 2]
                            kb.op("act", lambda e: e.activation(out=kf[:], in_=acc[:], func=AF.Silu, bias=cb[:, c:c + 1]), reads=[acc, cb], writes=[kf])
                            kb.op("pool", lambda e: e.tensor_scalar(out=kfm[:, c - 4, :], in0=kf[:], scalar1=0.125, scalar2=None, op0=ALU.mult),
                                  reads=[kf], writes=[kfm])
                    kb.dma("pool", sc["ckT"].ap().rearrange("(c p) t -> p c t", p=128)[:, :, t0:t0 + 512], kfm[:], reads=[kfm], writes=[sc["ckT"]], waw=False)
                    for tt in range(4):
                        pt = psT[tt % 2]
                        for c in range(4):
                            kb.op("pe", lambda e: e.transpose(out=pt[:, c * 128:(c + 1) * 128], in_=kfm[:, c, tt * 128:(tt + 1) * 128], identity=self.ident_b[:]),
                                  reads=[kfm, self.ident_b], writes=[pt])
                        kt = kts[tt % 2]
                        kb.op("act", lambda e: e.activation(out=kt[:], in_=pt[:, 0:512], func=AF.Copy), reads=[pt], writes=[kt])
                        kb.dma("pool", sc["ck"].ap()[t0 + tt * 128:t0 + (tt + 1) * 128, :], kt[:], reads=[kt], writes=[sc["ck"]], waw=False)

            with kb.phase():
                gens = [self._mlstm_dir(L, d, (Wt, EBt, EBTt, W2t, Wb, W2b, trib)) for d in range(2)]
                live = [True, True]
                while any(live):
                    for d in range(2):
                        if live[d]:
                            try:
                                next(gens[d])
                            except StopIteration:
                                live[d] = False
            with kb.phase():
                hfs = [kb.sb("hf%d" % i, [128, 1024], F32) for i in range(2)]
                hbs = [kb.sb("hb%d" % i, [128, 1024], F32) for i in range(2)]
                cos_ = [kb.sb("co%d" % i, [128, 1024], F32) for i in range(2)]
                og = kb.sb("og", [128, 1024], F32)
                junk = kb.sb("junk", [128, 1024], F32)
                ybs = [kb.sb("yb%d" % i, [128, 1024], BF16) for i in range(2)]
                yts = [kb.sb("yt%d" % i, [128, 8, 128], BF16) for i in range(2)]
                s0s = [kb.sb("s0%d" % i, [128, 8], F32) for i in range(2)]
                s1s = [kb.sb("s1%d" % i, [128, 8], F32) for i in range(2)]
                psT = [kb.ps("psT%d" % i, [128, 1024], BF16) for i in range(2)]
                kb.dma("sp", og[:], self.P["outg"].ap()[L].partition_broadcast(128), writes=[og])
                yv = sc["yT"].ap()[1792:2816, :].rearrange("(h p) t -> p h t", p=128)
                for tt in range(S // 128):
                    b = tt % 2
                    rsl = slice(tt * 128, (tt + 1) * 128)
                    hf, hb, co, yb, yt, s0, s1, pt = hfs[b], hbs[b], cos_[b], ybs[b], yts[b], s0s[b], s1s[b], psT[b]
                    kb.dma("sp", hf[:], sc["hfw"].ap()[rsl, :], reads=[sc["hfw"]], writes=[hf])
                    kb.dma("sp", hb[:], sc["hbw"].ap()[rsl, :], reads=[sc["hbw"]], writes=[hb])
                    kb.dma("sp", co[:], sc["co"].ap()[rsl, :], reads=[sc["co"]], writes=[co])
                    kb.op("pool", lambda e: e.tensor_tensor(out=hf[:], in0=hf[:], in1=hb[:], op=ALU.add), reads=[hf, hb], writes=[hf])
                    kb.op("act", lambda e: e.activation(out=junk[:], in_=hf[:], func=AF.Square), reads=[hf], writes=[junk])
                    kb.op("dve", lambda e: e.tensor_reduce(out=s0[:], in_=junk[:].rearrange("p (h v) -> p h v", h=8), axis=AX.X, op=ALU.add), reads=[junk], writes=[s0])
                    kb.op("act", lambda e: e.activation(out=s1[:], in_=s0[:], func=AF.Ln, scale=1.0 / 128, bias=self.eps_t[:, 0:1]), reads=[s0, self.eps_t], writes=[s1])
                    kb.op("act", lambda e: e.activation(out=s1[:], in_=s1[:], func=AF.Exp, scale=-0.5), reads=[s1], writes=[s1])
                    kb.op("pool", lambda e: e.tensor_tensor(out=co[:], in0=co[:], in1=og[:], op=ALU.mult), reads=[co, og], writes=[co])
                    kb.op("dve", lambda e: e.tensor_tensor(out=hf[:].rearrange("p (h v) -> p h v", h=8), in0=hf[:].rearrange("p (h v) -> p h v", h=8),
                                                           in1=s1[:, :].unsqueeze(2).broadcast_to([128, 8, 128]), op=ALU.mult), reads=[hf, s1], writes=[hf])
                    kb.op("pool", lambda e: e.tensor_tensor(out=yb[:], in0=hf[:], in1=co[:], op=ALU.mult), reads=[hf, co], writes=[yb])
                    for h in range(8):
                        kb.op("pe", lambda e: e.transpose(out=pt[:, h * 128:(h + 1) * 128], in_=yb[:, h * 128:(h + 1) * 128], identity=self.ident_b[:]),
                              reads=[yb, self.ident_b], writes=[pt], track=(h == 7))
                    kb.op("act", lambda e: e.activation(out=yt[:], in_=pt[:, :].rearrange("p (h t) -> p h t", h=8), func=AF.Copy), reads=[pt], writes=[yt])
                    kb.dma("pool", yv[:, :, rsl], yt[:], reads=[yt], writes=[sc["yT"]], waw=False)

    def _mlstm_dir(self, L, d, tl):
        kb, sc = self.kb, self.sc
        NCH = S // 64
        Wt, EBt, EBTt, W2t, Wb, W2b, trib = tl
        n_ = "f" if d == 0 else "b"
        QTc = [kb.sb("QTc%s%d" % (n_, i), [64, 8, 64], BF16) for i in range(2)]
        KTc = [kb.sb("KTc%s%d" % (n_, i), [64, 8, 64], BF16) for i in range(2)]
        Ktk = [kb.sb("Ktk%s%d" % (n_, i), [64, 512], BF16) for i in range(2)]
        vcs = [kb.sb("vc%s%d" % (n_, i), [64, 8, 128], F32) for i in range(2)]
        ATs = [kb.sb("AT%s%d" % (n_, i), [64, 8, 64], BF16) for i in range(2)]
        v1s = [kb.sb("v1%s%d" % (n_, i), [64, 8, 128], BF16) for i in range(2)]
        v2s = [kb.sb("v2%s%d" % (n_, i), [64, 8, 128], BF16) for i in range(2)]
        Cst = kb.sb("Cst" + n_, [64, 8, 128], F32)
        nst = kb.sb("nst" + n_, [64, 8], F32)
        Cbfs = [kb.sb("Cbf%s%d" % (n_, i), [64, 8, 128], BF16) for i in range(2)]
        nbfs = [kb.sb("nbf%s%d" % (n_, i), [64, 8], BF16) for i in range(2)]
        sm = [kb.sb("sm%s%d" % (n_, i), [64, 8], F32) for i in range(4)]
        hds = [kb.sb("hd%s%d" % (n_, i), [64, 8, 128], F32) for i in range(2)]
        psS = kb.ps("mpsS" + n_, [128, 512])
        psN = kb.ps("mpsN" + n_, [128, 512])
        psU = kb.ps("mpsU" + n_, [128, 512])
        psD = kb.ps("mpsD" + n_, [128, 512])
        hdst = sc["hfw"] if d == 0 else sc["hbw"]
        e_v = "pool"
        kb.op("dve", lambda e: e.memset(Cst[:], 0.0), writes=[Cst])
        kb.op("dve", lambda e: e.memset(nst[:], 0.0), writes=[nst])
        kb.op("dve", lambda e: e.memset(Cbfs[0][:], 0.0), writes=[Cbfs[0]])
        kb.op("dve", lambda e: e.memset(nbfs[0][:], 0.0), writes=[nbfs[0]])
        qv = sc["cqT"].ap().rearrange("(h dd) t -> dd h t", dd=64)
        kv = sc["ckT"].ap().rearrange("(h dd) t -> dd h t", dd=64)
        order = list(range(NCH)) if d == 0 else list(range(NCH - 1, -1, -1))

        def loads(it):
            c = order[it]
            b = it % 2
            tsl = slice(c * 64, (c + 1) * 64)
            kb.dma("sp", QTc[b][:], qv[:, :, tsl], reads=[sc["cqT"]], writes=[QTc[b]])
            kb.dma("sp", KTc[b][:], kv[:, :, tsl], reads=[sc["ckT"]], writes=[KTc[b]])
            kb.dma("sp", Ktk[b][:], sc["ck"].ap()[tsl, :], reads=[sc["ck"]], writes=[Ktk[b]])
            kb.dma("sp", vcs[b][:], sc["cv"].ap()[tsl, :].rearrange("t (h v) -> t h v", h=8), reads=[sc["cv"]], writes=[vcs[b]])

        loads(0)
        for it, c in enumerate(order):
            b = it % 2
            tsl = slice(c * 64, (c + 1) * 64)
            Q, K, Kt, vc, AT, v1, v2 = QTc[b], KTc[b], Ktk[b], vcs[b], ATs[b], v1s[b], v2s[b]
            Cbf, nbf, Cbn, nbn = Cbfs[b], nbfs[b], Cbfs[1 - b], nbfs[1 - b]
            for h in range(8):
                kb.op("pe", lambda e: e.matmul(psS[0:64, h * 64:(h + 1) * 64], lhsT=K[:, h, :], rhs=Q[:, h, :], start=True, stop=True),
                      reads=[K, Q], writes=[psS], track=(h == 7))
            if it + 1 < NCH:
                loads(it + 1)
            yield
            kb.op("dve", lambda e: e.tensor_tensor(out=AT[:], in0=psS[0:64, :].rearrange("p (h t) -> p h t", h=8),
                                                   in1=trib[:, d, :].unsqueeze(1).broadcast_to([64, 8, 64]), op=ALU.mult), reads=[psS, trib], writes=[AT])
            kb.op(e_v, lambda e: e.tensor_tensor(out=v1[:], in0=vc[:], in1=Wt[:, d, c, :].unsqueeze(2).broadcast_to([64, 8, 128]), op=ALU.mult),
                  reads=[vc, Wt], writes=[v1])
            kb.op(e_v, lambda e: e.tensor_tensor(out=v2[:], in0=vc[:], in1=W2t[:, d, c, :].unsqueeze(2).broadcast_to([64, 8, 128]), op=ALU.mult),
                  reads=[vc, W2t], writes=[v2])
            yield
            eb = EBt[:, d, c, :]
            ebt = EBTt[:, d, c, :]
            hd = hds[b]
            s0, s1, s2, s3 = sm
            for h in range(8):
                kb.op("pe", lambda e: e.matmul(psD[0:64, h:h + 1], lhsT=AT[:, h, :], rhs=Wb[:, d, c, h:h + 1], start=True, stop=False),
                      reads=[AT, Wb], writes=[psD], track=False)
                kb.op("pe", lambda e: e.matmul(psD[0:64, h:h + 1], lhsT=Q[:, h, :], rhs=nbf[:, h:h + 1], start=False, stop=True),
                      reads=[Q, nbf], writes=[psD], track=(h == 7))
            kb.op("dve", lambda e: e.tensor_tensor(out=s0[:], in0=psD[0:64, 0:8], in1=eb, op=ALU.mult), reads=[psD, EBt], writes=[s0])
            kb.op("dve", lambda e: e.scalar_tensor_tensor(out=s1[:], in0=s0[:], scalar=-1.0, in1=s0[:], op0=ALU.mult, op1=ALU.max), reads=[s0], writes=[s1])
            kb.op("dve", lambda e: e.tensor_scalar(out=s1[:], in0=s1[:], scalar1=1.0, scalar2=None, op0=ALU.max), reads=[s1], writes=[s1])
            kb.op("dve", lambda e: e.reciprocal(out=s2[:], in_=s1[:]), reads=[s1], writes=[s2])
            kb.op("dve", lambda e: e.tensor_tensor(out=s3[:], in0=s2[:], in1=eb, op=ALU.mult), reads=[s2, EBt], writes=[s3])
            for hf_ in range(2):
                h0 = hf_ * 4
                hs = slice(h0, h0 + 4)
                for hh in range(4):
                    h = h0 + hh
                    kb.op("pe", lambda e: e.matmul(psN[0:64, hh * 128:(hh + 1) * 128], lhsT=AT[:, h, :], rhs=v1[:, h, :], start=True, stop=False),
                          reads=[AT, v1], writes=[psN], track=False)
                    kb.op("pe", lambda e: e.matmul(psN[0:64, hh * 128:(hh + 1) * 128], lhsT=Q[:, h, :], rhs=Cbf[:, h, :], start=False, stop=True),
                          reads=[Q, Cbf], writes=[psN], track=(hh == 3))
                for hh in range(4):
                    h = h0 + hh
                    kb.op("pe", lambda e: e.matmul(psU[0:64, hh * 128:(hh + 1) * 128], lhsT=Kt[:, h * 64:(h + 1) * 64], rhs=v2[:, h, :], start=True, stop=True),
                          reads=[Kt, v2], writes=[psU], track=False)
                    kb.op("pe", lambda e: e.matmul(psD[0:64, 8 + h:9 + h], lhsT=Kt[:, h * 64:(h + 1) * 64], rhs=W2b[:, d, c, h:h + 1], start=True, stop=True),
                          reads=[Kt, W2b], writes=[psD], track=(hh == 3))
                yield
                if hf_ == 0:
                    kb.op("pool", lambda e: e.tensor_tensor(out=Cst[:], in0=Cst[:], in1=ebt.unsqueeze(2).broadcast_to([64, 8, 128]), op=ALU.mult),
                          reads=[Cst, EBTt], writes=[Cst])
                    kb.op("pool", lambda e: e.tensor_tensor(out=nst[:], in0=nst[:], in1=ebt, op=ALU.mult), reads=[nst, EBTt], writes=[nst])
                kb.op("dve", lambda e: e.tensor_tensor(out=Cst[:, hs, :], in0=Cst[:, hs, :], in1=psU[0:64, :].rearrange("p (h v) -> p h v", h=4), op=ALU.add),
                      reads=[Cst, psU], writes=[Cst])
                kb.op("dve", lambda e: e.tensor_tensor(out=nst[:, hs], in0=nst[:, hs], in1=psD[0:64, 8 + h0:12 + h0], op=ALU.add), reads=[nst, psD], writes=[nst])
                kb.op("act", lambda e: e.activation(out=Cbn[:, hs, :], in_=Cst[:, hs, :], func=AF.Copy), reads=[Cst], writes=[Cbn])
                kb.op("act", lambda e: e.activation(out=nbn[:, hs], in_=nst[:, hs], func=AF.Copy), reads=[nst], writes=[nbn])
                for hh in range(4):
                    h = h0 + hh
                    kb.op("act", lambda e: e.activation(out=hd[:, h, :], in_=psN[0:64, hh * 128:(hh + 1) * 128], func=AF.Copy, scale=s3[:, h:h + 1]),
                          reads=[psN, s3], writes=[hd])
                yield
            kb.dma("pool", hdst.ap()[tsl, :], hd[:].rearrange("p h v -> p (h v)"), reads=[hd], writes=[hdst], waw=False)
            yield

    def build(self, phases=("p1", "attn", "mlstm", "p3a", "p3b")):
        kb = self.kb
        with kb.phase():
            self.consts_phase()
            self.cast_weights(0)
            for L in range(self.n_layers):
                xin = self.xT if L == 0 else self.xbuf[(L - 1) % 2]
                xout = self.outT if L == self.n_layers - 1 else self.xbuf[L % 2]
                if "p1" in phases:
                    self.p1(L, xin, 0)
                    self.p1(L, xin, 1)
                if L + 1 < self.n_layers:
                    for n, _, _ in WEIGHTS:
                        kb.wait_tile("pool", self.WB[L % 2][n])
                    self.cast_weights(L + 1)
                if "attn" in phases:
                    self.attn(L)
                if "mlstm" in phases:
                    self.mlstm(L)
                if "p3a" in phases:
                    self.p3a(L, xin)
                if "p3b" in phases:
                    self.p3b(L, xout)
            kb.barrier()
        return self.nc


_PROG_CACHE = {}


def kernel(**inputs):
    x = np.asarray(inputs["x"], dtype=np.float32)
    B = x.shape[0]
    consts = make_consts()
    params = prep_params(inputs)
    if "nc" not in _PROG_CACHE:
        _PROG_CACHE["nc"] = Prog().build()
    nc = _PROG_CACHE["nc"]
    shared = {}
    for n, _, _ in WEIGHTS:
        shared[n] = np.ascontiguousarray(inputs[n], dtype=np.float32)
    shared.update(params)
    shared.update(consts)
    in_maps = []
    active = [0, 1, 4, 5]
    zero_x = np.zeros((D, S), np.float32)
    for c in range(NCORES):
        m = dict(shared)
        m["xT"] = np.ascontiguousarray(x[active.index(c)].T) if c in active else zero_x
        in_maps.append(m)
    res = run_bass_kernel_spmd(nc, in_maps, core_ids=list(range(NCORES)))
    out = np.stack([np.ascontiguousarray(res.results[active[b]]["outT"].T) for b in range(B)], axis=0)
    return out.astype(np.float32)
```

```python
import contextlib
import numpy as np
import concourse.bass as bass
import concourse.mybir as mybir
from concourse.bass_utils import run_bass_kernel_spmd

F32 = mybir.dt.float32
BF16 = mybir.dt.bfloat16
AF = mybir.ActivationFunctionType
ALU = mybir.AluOpType
AX = mybir.AxisListType

S = 4096
D = 2048
DEPTH = 4
NCORES = 8
IN_COLS = 13088
DFF = 8192
EPS = 1e-6
O_AQ, O_AK, O_AV, O_BQ, O_BK, O_BV, O_CQK, O_CV, O_CO, O_CG, O_G = 0, 768, 1536, 2304, 3328, 3584, 3840, 4864, 5888, 6912, 6944


class T:
    def __init__(self, kb, h, name):
        self.h = h
        self.name = name
        self.last_w = None
        self.readers = {}
        self.sem = None
        self.dcnt = 0
        self.kb = kb

    def ap(self):
        return self.h.ap()

    def __getitem__(self, idx):
        return self.h.__getitem__(idx)

    def get_sem(self):
        if self.sem is None:
            self.sem, self.dcnt = self.kb.take_sem(self.name)
        return self.sem


class KB:
    def __init__(self, nc):
        self.nc = nc
        self.engs = {"pe": nc.tensor, "act": nc.scalar, "dve": nc.vector, "pool": nc.gpsimd, "sp": nc.sync}
        self.sem = {e: nc.alloc_semaphore(name="s_" + e) for e in self.engs}
        self.cnt = {e: 0 for e in self.engs}
        self.waited = {e: {} for e in self.engs}
        self.sem_pool = []
        self.live_dma = {}
        self.n_ins = 0
        self.n_wait = 0
        self.uid = 0
        self.stack = None

    def take_sem(self, name):
        if self.sem_pool:
            return self.sem_pool.pop()
        self.uid += 1
        return self.nc.alloc_semaphore(name="d%d" % self.uid), 0

    def sb(self, name, shape, dt):
        self.uid += 1
        h = self.stack.enter_context(self.nc.sbuf_tensor("%s_%d" % (name, self.uid), list(shape), dt))
        t = T(self, h, name)
        self.phase_tiles.append(t)
        return t

    def ps(self, name, shape, dt=F32):
        self.uid += 1
        h = self.stack.enter_context(self.nc.psum_tensor("%s_%d" % (name, self.uid), list(shape), dt))
        t = T(self, h, name)
        self.phase_tiles.append(t)
        return t

    def dram(self, name, shape, dt, kind="Internal"):
        return T(self, self.nc.dram_tensor(name, list(shape), dt, kind=kind), name)

    @contextlib.contextmanager
    def phase(self):
        old = self.stack
        old_tiles = getattr(self, "phase_tiles", None)
        self.phase_tiles = []
        with contextlib.ExitStack() as st:
            self.stack = st
            yield
            self.barrier()
            for t in self.phase_tiles:
                if t.sem is not None:
                    self.sem_pool.append((t.sem, t.dcnt))
                    t.sem = None
        self.stack = old
        self.phase_tiles = old_tiles

    def _wait(self, eng, evs):
        w = self.waited[eng]
        need = {}
        for ev in evs:
            if ev is None:
                continue
            s, v = ev
            if eng == "pe" and s is self.sem["pe"]:
                continue
            if w.get(id(s), 0) >= v:
                continue
            if id(s) not in need or need[id(s)][1] < v:
                need[id(s)] = (s, v)
        for s, v in need.values():
            self.engs[eng].wait_ge(s, v)
            w[id(s)] = v
            self.n_wait += 1

    def _deps(self, reads, writes):
        evs = []
        for t in reads:
            evs.append(t.last_w)
        for t in writes:
            evs.append(t.last_w)
            evs.extend(t.readers.values())
        return evs

    def _add_reader(self, t, ev):
        s, v = ev
        cur = t.readers.get(id(s))
        if cur is None or cur[1] < v:
            t.readers[id(s)] = ev

    def op(self, eng, fn, reads=(), writes=(), track=True):
        self._wait(eng, self._deps(reads, writes))
        ins = fn(self.engs[eng])
        self.n_ins += 1
        if track:
            self.cnt[eng] += 1
            ins.then_inc(self.sem[eng], 1)
            ev = (self.sem[eng], self.cnt[eng])
        else:
            ev = (self.sem[eng], self.cnt[eng] + 1)
        for t in reads:
            self._add_reader(t, ev)
        for t in writes:
            t.last_w = ev
            t.readers = {}
        return ins

    def dma(self, q, out, in_, reads=(), writes=(), waw=True, **kw):
        evs = self._deps(reads, writes)
        if not waw:
            evs = [t.last_w for t in reads]
            for t in writes:
                evs.extend(t.readers.values())
        self._wait(q, evs)
        owner = writes[0]
        sem = owner.get_sem()
        ins = self.engs[q].dma_start(out=out, in_=in_, **kw)
        owner.dcnt += 16
        ins.then_inc(sem, 16)
        self.n_ins += 1
        ev = (sem, owner.dcnt)
        self.live_dma[id(sem)] = ev
        for t in reads:
            self._add_reader(t, ev)
        for t in writes:
            t.last_w = ev
            t.readers = {}
        return ins

    def dma_multi(self, items, reads=(), writes=()):
        evs = self._deps(reads, writes)
        owner = writes[0]
        sem = owner.get_sem()
        for q in dict.fromkeys(q for q, _, _ in items):
            self._wait(q, evs)
        for q, o, i in items:
            ins = self.engs[q].dma_start(out=o, in_=i)
            owner.dcnt += 16
            ins.then_inc(sem, 16)
            self.n_ins += 1
        ev = (sem, owner.dcnt)
        self.live_dma[id(sem)] = ev
        for t in reads:
            self._add_reader(t, ev)
        for t in writes:
            t.last_w = ev
            t.readers = {}

    def wait_tile(self, eng, t):
        self._wait(eng, [t.last_w])

    def barrier(self):
        evs = [(self.sem[e], self.cnt[e]) for e in self.engs if self.cnt[e] > 0]
        evs += list(self.live_dma.values())
        for e in self.engs:
            self._wait(e, evs)


def bcast_mid(ap2d, n, inner):
    return ap2d.rearrange("p (o i) -> p o i", o=1).broadcast_to([ap2d.shape[0], n, inner])


def _rope_tables_np(pos, dim):
    inv = (np.float32(10000.0) ** (-(np.arange(0, dim, 2, dtype=np.float32)) / np.float32(dim))).astype(np.float32)
    ang = pos.astype(np.float32)[:, None] * inv[None, :]
    return np.cos(ang).astype(np.float32), np.sin(ang).astype(np.float32)


def make_consts():
    c = {}
    pos = np.arange(S)
    cos1, sin1 = _rope_tables_np(pos, 128)
    ropeA = np.zeros((2, 128, S), np.float32)
    ropeA[0] = np.concatenate([cos1, cos1], 1).T
    ropeA[1] = np.concatenate([sin1, sin1], 1).T
    cr, sr = _rope_tables_np(pos // 64, 64)
    cc, sc = _rope_tables_np(pos % 64, 64)
    ropeB = np.zeros((2, 128, S), np.float32)
    ropeB[0] = np.concatenate([cr, cr, cc, cc], 1).T
    ropeB[1] = np.concatenate([sr, sr, sc, sc], 1).T
    c["ropeA"] = ropeA
    c["ropeB"] = ropeB
    RA = np.zeros((128, 128), np.float32)
    for dp in range(128):
        if dp < 64:
            RA[dp + 64, dp] = -1.0
        else:
            RA[dp - 64, dp] = 1.0
    RB = np.zeros((128, 128), np.float32)
    for dp in range(128):
        if dp % 64 < 32:
            RB[dp + 32, dp] = -1.0
        else:
            RB[dp - 32, dp] = 1.0
    c["rmats"] = np.stack([RA, RB])
    i = np.arange(128)[:, None]
    j = np.arange(512)[None, :]
    am = np.zeros((20, 128, 512), np.float32)
    for ri, r in enumerate(range(-8, 12)):
        delta = r * 128 + i - j
        m = np.zeros_like(delta, dtype=np.float32)
        for dil in (1, 4, 16):
            m += ((delta % dil == 0) & (np.abs(delta) <= 64 * dil)).astype(np.float32)
        am[ri] = m
    c["amask"] = am
    c["ident"] = np.eye(128, dtype=np.float32)
    u = np.arange(64)[:, None]
    t = np.arange(64)[None, :]
    c["cmats"] = np.stack([(u <= t), (u >= t), np.ones((64, 64), bool)]).astype(np.float32)
    return c


def prep_params(inp):
    p = {}
    f = lambda a: np.ascontiguousarray(a, dtype=np.float32)
    p["n1g"] = f(inp["norm1_g"].reshape(DEPTH, 16, 128).transpose(0, 2, 1))
    p["n2g"] = f(inp["norm2_g"].reshape(DEPTH, 16, 128).transpose(0, 2, 1))
    p["gains"] = f(np.stack([inp["a_q_gain"], inp["a_k_gain"], inp["b_q_gain"], inp["b_k_gain"]], -1))
    p["convw"] = f(inp["c_conv_w"].reshape(DEPTH, 5, 8, 128).transpose(0, 3, 2, 1))
    p["convb"] = f(inp["c_conv_b"].reshape(DEPTH, 8, 128).transpose(0, 2, 1))
    p["gateb"] = f(inp["c_gate_b"].reshape(DEPTH, 1, 32))
    p["outg"] = f(inp["c_out_g"].reshape(DEPTH, 1, 1024))
    return p


WEIGHTS = [("w_in", D, IN_COLS), ("w_branch_a", 768, D), ("w_branch_b", 1024, D), ("w_branch_c", 1024, D),
           ("w_out", D, D), ("w_up", D, DFF), ("w_down", DFF, D)]


class Prog:
    def __init__(self, n_layers=DEPTH, debug=False, stop_after=None):
        self.nc = nc = bass.Bass("TRN2", target_bir_lowering=False)
        self.kb = kb = KB(nc)
        self.n_layers = n_layers
        self.debug = debug
        self.stop_after = stop_after
        ext = "ExternalInput"
        self.xT = kb.dram("xT", [D, S], F32, kind=ext)
        self.W = {n: kb.dram(n, [n_layers, r, c], F32, kind=ext) for n, r, c in WEIGHTS}
        self.P = {}
        for n, shp in (("n1g", [DEPTH, 128, 16]), ("n2g", [DEPTH, 128, 16]), ("gains", [DEPTH, 128, 4]),
                       ("convw", [DEPTH, 128, 8, 5]), ("convb", [DEPTH, 128, 8]), ("gateb", [DEPTH, 1, 32]),
                       ("outg", [DEPTH, 1, 1024])):
            self.P[n] = kb.dram(n, shp, F32, kind=ext)
        self.C = {}
        for n, shp in (("ropeA", [2, 128, S]), ("ropeB", [2, 128, S]), ("rmats", [2, 128, 128]),
                       ("amask", [20, 128, 512]), ("ident", [128, 128]), ("cmats", [3, 64, 64])):
            self.C[n] = kb.dram(n, shp, F32, kind=ext)
        self.outT = kb.dram("outT", [D, S], F32, kind="ExternalOutput")
        sk = "ExternalOutput" if debug else "Internal"
        self.WB = [{n: kb.dram("wb%d_%s" % (par, n), [r, c], BF16) for n, r, c in WEIGHTS} for par in range(2)]
        self.sc = {}
        for n, shp, dt in (("aqT", [768, S], BF16), ("akT", [768, S], BF16), ("bqT", [1024, S], BF16),
                           ("bkT", [256, S], BF16), ("cqkT", [1024, S], F32), ("gT", [3 * D, S], BF16),
                           ("av", [S, 768], BF16), ("bv", [S, 256], BF16), ("cv", [S, 1024], F32),
                           ("co", [S, 1024], F32), ("cg", [S, 32], F32), ("yT", [2816, S], BF16),
                           ("hT", [D, S], F32), ("cqT", [512, S], BF16), ("ckT", [512, S], BF16),
                           ("ck", [S, 512], BF16), ("hfw", [S, 1024], F32), ("hbw", [S, 1024], F32), ("xnT", [D, S], BF16)):
            self.sc[n] = kb.dram("sc_" + n, shp, dt, kind=sk)
        self.xbuf = [kb.dram("xbuf%d" % i, [D, S], F32) for i in range(2)]

    def cast_weights(self, L):
        kb = self.kb
        for n, r, c in WEIGHTS:
            dst = self.WB[L % 2][n]
            src = self.W[n]
            step = 256
            for r0 in range(0, r, step):
                kb.dma("pool", dst.ap()[r0:r0 + step, :], src.ap()[L, r0:r0 + step, :], writes=[dst], waw=False)

    def rmsnorm_T(self, src_ap_fn, ntok, gt, dstT, dst_tok0, xts, sqs, rs, psN, ones_f, sub=128):
        kb = self.kb
        for i in range(ntok // sub):
            xt = xts[i % len(xts)]
            src = src_ap_fn(i * sub, sub)
            kb.dma_multi([("sp", xt[:, 0:8, :], src[:, 0:8, :]), ("sp", xt[:, 8:16, :], src[:, 8:16, :])], writes=[xt])
            for kc in range(16):
                sq = sqs[kc % len(sqs)]
                kb.op("act", lambda e: e.activation(out=sq[:, 0:sub], in_=xt[:, kc, :], func=AF.Square), reads=[xt], writes=[sq])
                kb.op("pe", lambda e: e.matmul(psN[:, 0:sub], lhsT=ones_f[:], rhs=sq[:, 0:sub], start=(kc == 0), stop=(kc == 15)),
                      reads=[ones_f, sq], writes=[psN])
            kb.op("act", lambda e: e.activation(out=rs[:, 0:sub], in_=psN[:, 0:sub], func=AF.Ln, scale=1.0 / D, bias=self.eps_t[:, 0:1]),
                  reads=[psN, self.eps_t], writes=[rs])
            kb.op("act", lambda e: e.activation(out=rs[:, 0:sub], in_=rs[:, 0:sub], func=AF.Exp, scale=-0.5), reads=[rs], writes=[rs])
            c0 = dst_tok0 + i * sub
            for kc in range(16):
                kb.op("dve", lambda e: e.scalar_tensor_tensor(out=dstT[:, kc, c0:c0 + sub], in0=xt[:, kc, :], scalar=gt[:, kc:kc + 1],
                                                              in1=rs[:, 0:sub], op0=ALU.mult, op1=ALU.mult),
                      reads=[xt, gt, rs], writes=[dstT])

    def consts_phase(self):
        kb = self.kb
        self.eps_t = kb.sb("eps", [128, 1], F32)
        kb.op("dve", lambda e: e.memset(self.eps_t[:], EPS), writes=[self.eps_t])
        self.ones_f = kb.sb("ones_f", [128, 128], F32)
        kb.op("dve", lambda e: e.memset(self.ones_f[:], 1.0), writes=[self.ones_f])
        self.ones_b = kb.sb("ones_b", [128, 128], BF16)
        kb.op("dve", lambda e: e.memset(self.ones_b[:], 1.0), writes=[self.ones_b])
        self.ident_b = kb.sb("ident_b", [128, 128], BF16)
        kb.dma("pool", self.ident_b[:], self.C["ident"].ap()[:, :], writes=[self.ident_b])
        self.rmat = kb.sb("rmat", [128, 2, 128], BF16)
        kb.dma("pool", self.rmat[:], self.C["rmats"].ap().rearrange("r p c -> p r c"), writes=[self.rmat])


    def _qk_stages(self, ps, k, sg_, tsl, dsl, dst, sqs3, rss3, qnbs, t1s, t2s, obs, psRa, psRb, gains):
        kb = self.kb
        sq, rs, qnb = sqs3[k % 3], rss3[k % 3], qnbs[k % 3]
        t1, t2, ob = t1s[k % 2], t2s[k % 2], obs[k % 3]
        pa, pb = psRa[k % 2], psRb[k % 2]
        rope = sg_["rope"]
        gi = sg_["gi"]
        rm = sg_["rm"]

        def e1():
            kb.op("act", lambda e: e.activation(out=sq[:], in_=ps[:], func=AF.Square), reads=[ps], writes=[sq])

        def e2():
            kb.op("pe", lambda e: e.matmul(pa[:], lhsT=self.ones_f[:], rhs=sq[:], start=True, stop=True), reads=[self.ones_f, sq], writes=[pa])
            kb.op("act", lambda e: e.activation(out=rs[:], in_=pa[:], func=AF.Ln, scale=1.0 / 128, bias=self.eps_t[:, 0:1]), reads=[pa, self.eps_t], writes=[rs])
            kb.op("act", lambda e: e.activation(out=rs[:], in_=rs[:], func=AF.Exp, scale=-0.5), reads=[rs], writes=[rs])
            kb.op("dve", lambda e: e.scalar_tensor_tensor(out=qnb[:], in0=ps[:], scalar=gains[:, gi:gi + 1], in1=rs[:], op0=ALU.mult, op1=ALU.mult),
                  reads=[ps, gains, rs], writes=[qnb])

        def e3():
            kb.op("pe", lambda e: e.matmul(pb[:], lhsT=self.rmat[:, rm, :], rhs=qnb[:], start=True, stop=True), reads=[self.rmat, qnb], writes=[pb])
            kb.op("dve", lambda e: e.tensor_tensor(out=t1[:], in0=qnb[:], in1=rope[:, 0, tsl], op=ALU.mult), reads=[qnb, rope], writes=[t1])
            kb.op("dve", lambda e: e.tensor_tensor(out=t2[:], in0=pb[:], in1=rope[:, 1, tsl], op=ALU.mult), reads=[pb, rope], writes=[t2])
            kb.op("pool", lambda e: e.tensor_tensor(out=ob[:], in0=t1[:], in1=t2[:], op=ALU.add), reads=[t1, t2], writes=[ob])
            kb.dma("pool", dsl, ob[:], reads=[ob], writes=[dst], waw=False)

        return [e1, e2, e3]

    def p1(self, L, xin, sg):
        kb, sc = self.kb, self.sc
        NT = 2048
        T0 = sg * NT
        WB = self.WB[L % 2]["w_in"]
        with kb.phase():
            xnT = kb.sb("xnT", [128, 16, NT], BF16)
            xts = [kb.sb("xt%d" % i, [128, 16, 128], F32) for i in range(2)]
            sqs = [kb.sb("sq%d" % i, [128, 512], F32) for i in range(2)]
            rss = [kb.sb("rs%d" % i, [128, 512], F32) for i in range(2)]
            wts = [kb.sb("wt%d" % i, [128, 16, 512], BF16) for i in range(2)]
            g1 = kb.sb("g1", [128, 16], F32)
            gains = kb.sb("gains", [128, 4], F32)
            ropeA = kb.sb("ropeA", [128, 2, NT], F32)
            ropeB = kb.sb("ropeB", [128, 2, NT], F32)
            qnbs = [kb.sb("qnb%d" % i, [128, 512], BF16) for i in range(3)]
            sqs3 = sqs + [kb.sb("sq2", [128, 512], F32)]
            rss3 = rss + [kb.sb("rs2", [128, 512], F32)]
            t1s = [kb.sb("t1%d" % i, [128, 512], F32) for i in range(2)]
            t2s = [kb.sb("t2%d" % i, [128, 512], F32) for i in range(2)]
            obs = [kb.sb("ob%d" % i, [128, 512], BF16) for i in range(3)]
            ofs = [kb.sb("of%d" % i, [128, 512], F32) for i in range(3)]
            psX = [kb.ps("psX%d" % i, [128, 512]) for i in range(3)]
            psRa = [kb.ps("psRa%d" % i, [128, 512]) for i in range(2)]
            psRb = [kb.ps("psRb%d" % i, [128, 512]) for i in range(2)]
            psN = kb.ps("psN", [128, 512])
            pend = []
            qcnt = 0
            kb.dma("sp", g1[:], self.P["n1g"].ap()[L], writes=[g1])
            kb.dma("sp", gains[:], self.P["gains"].ap()[L], writes=[gains])
            for j in range(2):
                kb.dma("sp", ropeA[:, j, :], self.C["ropeA"].ap()[j, :, T0:T0 + NT], writes=[ropeA])
                kb.dma("sp", ropeB[:, j, :], self.C["ropeB"].ap()[j, :, T0:T0 + NT], writes=[ropeB])
            xv = xin.ap().rearrange("(kc p) t -> p kc t", p=128)
            if L == 0:
                self.rmsnorm_T(lambda t0, n: xv[:, :, T0 + t0:T0 + t0 + n], NT, g1, xnT, 0, xts, sqs, rss[0], psN, self.ones_f)
            else:
                xnv = sc["xnT"].ap().rearrange("(kc p) t -> p kc t", p=128)
                kb.dma_multi([("sp", xnT[:, k0:k0 + 4, :], xnv[:, k0:k0 + 4, T0:T0 + NT]) for k0 in range(0, 16, 4)], reads=[sc["xnT"]], writes=[xnT])

            segs = []
            def add(col0, n, kind, dst, drow0=0, **kw):
                c = 0
                while c < n:
                    w = min(512, n - c)
                    segs.append(dict(col0=col0 + c, n=w, kind=kind, dst=dst, d0=drow0 + c, **kw))
                    c += w
            add(O_AQ, 768, "qk", sc["aqT"], gi=0, rope=ropeA, rm=0)
            add(O_AK, 768, "qk", sc["akT"], gi=1, rope=ropeA, rm=0)
            add(O_BQ, 1024, "qk", sc["bqT"], gi=2, rope=ropeB, rm=1)
            add(O_BK, 256, "qk", sc["bkT"], gi=3, rope=ropeB, rm=1)
            add(O_CQK, 1024, "copyT", sc["cqkT"])
            add(O_G, 3 * D, "sigT", sc["gT"])
            add(O_AV, 768, "tok_bf", sc["av"])
            add(O_BV, 256, "tok_bf", sc["bv"])
            add(O_CV, 1024, "tok_f", sc["cv"])
            add(O_CO, 1024, "tok_sig", sc["co"])
            add(O_CG, 32, "tok_f", sc["cg"])
            wv = WB.ap().rearrange("(kc p) c -> p kc c", p=128)
            cnt = 0
            ocnt = 0
            for si, sg_ in enumerate(segs):
                wt = wts[si % 2]
                n = sg_["n"]
                c0 = sg_["col0"]
                kb.dma_multi([("sp", wt[:, 0:8, 0:n], wv[:, 0:8, c0:c0 + n]), ("sp", wt[:, 8:16, 0:n], wv[:, 8:16, c0:c0 + n])], reads=[WB], writes=[wt])
                kind = sg_["kind"]
                dst = sg_["dst"]
                if kind in ("qk", "copyT", "sigT"):
                    for cc in range(n // 128):
                        r0 = sg_["d0"] + cc * 128
                        for tg in range(NT // 512):
                            ps = psX[cnt % 3]
                            cnt += 1
                            for kc in range(16):
                                kb.op("pe", lambda e: e.matmul(ps[:, :], lhsT=wt[:, kc, cc * 128:(cc + 1) * 128], rhs=xnT[:, kc, tg * 512:(tg + 1) * 512],
                                                               start=(kc == 0), stop=(kc == 15)), reads=[wt, xnT], writes=[ps], track=(kc == 15))
                            tsl = slice(tg * 512, (tg + 1) * 512)
                            dsl = dst.ap()[r0:r0 + 128, T0 + tg * 512:T0 + (tg + 1) * 512]
                            if kind == "sigT":
                                ob = obs[ocnt % 3]
                                ocnt += 1
                                kb.op("act", lambda e: e.activation(out=ob[:], in_=ps[:], func=AF.Sigmoid), reads=[ps], writes=[ob])
                                kb.dma("pool", dsl, ob[:], reads=[ob], writes=[dst], waw=False)
                            elif kind == "copyT":
                                of = ofs[ocnt % 3]
                                ocnt += 1
                                kb.op("dve", lambda e: e.tensor_copy(out=of[:], in_=ps[:]), reads=[ps], writes=[of])
                                kb.dma("pool", dsl, of[:], reads=[of], writes=[dst], waw=False)
                            else:
                                pend.append(self._qk_stages(ps, qcnt, sg_, tsl, dsl, dst, sqs3, rss3, qnbs, t1s, t2s, obs, psRa, psRb, gains))
                                qcnt += 1
                            for st in list(pend):
                                st.pop(0)()
                                if not st:
                                    pend.remove(st)
                    while pend:
                        for st in list(pend):
                            st.pop(0)()
                            if not st:
                                pend.remove(st)
                else:
                    for tt in range(NT // 128):
                        ps = psX[cnt % 3]
                        cnt += 1
                        for kc in range(16):
                            kb.op("pe", lambda e: e.matmul(ps[:, 0:n], lhsT=xnT[:, kc, tt * 128:(tt + 1) * 128], rhs=wt[:, kc, 0:n],
                                                           start=(kc == 0), stop=(kc == 15)), reads=[wt, xnT], writes=[ps], track=(kc == 15))
                        dsl = dst.ap()[T0 + tt * 128:T0 + (tt + 1) * 128, sg_["d0"]:sg_["d0"] + n]
                        if kind == "tok_bf":
                            ob = obs[ocnt % 3]
                            ocnt += 1
                            kb.op("dve", lambda e: e.tensor_copy(out=ob[:, 0:n], in_=ps[:, 0:n]), reads=[ps], writes=[ob])
                            kb.dma("pool", dsl, ob[:, 0:n], reads=[ob], writes=[dst], waw=False)
                        else:
                            of = ofs[ocnt % 3]
                            ocnt += 1
                            if kind == "tok_sig":
                                kb.op("act", lambda e: e.activation(out=of[:, 0:n], in_=ps[:, 0:n], func=AF.Sigmoid), reads=[ps], writes=[of])
                            else:
                                kb.op("dve", lambda e: e.tensor_copy(out=of[:, 0:n], in_=ps[:, 0:n]), reads=[ps], writes=[of])
                            kb.dma("pool", dsl, of[:, 0:n], reads=[of], writes=[dst], waw=False)

    def p3a(self, L, xin):
        kb, sc = self.kb, self.sc
        WBs = self.WB[L % 2]
        br = [("w_branch_a", 0, 6), ("w_branch_b", 6, 8), ("w_branch_c", 14, 8)]
        with kb.phase():
            yTs = [kb.sb("yT%d" % i, [128, 22, 512], BF16) for i in range(2)]
            xts_ = [kb.sb("xt%d" % i, [128, 16, 512], F32) for i in range(2)]
            mT = kb.sb("mT", [128, 16, 512], BF16)
            wbs = [kb.sb("wb%d" % i, [128, 22, 256], BF16) for i in range(2)]
            wos = [kb.sb("wo%d" % i, [128, 16, 256], BF16) for i in range(2)]
            gts = [kb.sb("gt%d" % i, [128, 3, 512], BF16) for i in range(2)]
            ms = [kb.sb("m%d" % i, [128, 512], F32) for i in range(2)]
            tmps = [kb.sb("tmp%d" % i, [128, 512], F32) for i in range(2)]
            psX = [kb.ps("psX%d" % i, [128, 512]) for i in range(4)]
            yv = sc["yT"].ap().rearrange("(kc p) t -> p kc t", p=128)
            xv = xin.ap().rearrange("(kc p) t -> p kc t", p=128)
            hv = sc["hT"].ap().rearrange("(kc p) t -> p kc t", p=128)
            gv = sc["gT"].ap().rearrange("(b oc p) t -> p b oc t", p=128, b=3)
            wvs = [WBs[n].ap().rearrange("(kc p) c -> p kc c", p=128) for n, _, _ in br]
            wov = WBs["w_out"].ap().rearrange("(kc p) c -> p kc c", p=128)
            cnt = 0
            def ld(tg):
                tsl_ = slice(tg * 512, (tg + 1) * 512)
                yT_, xt_ = yTs[tg % 2], xts_[tg % 2]
                kb.dma_multi([("sp", yT_[:, 0:11, :], yv[:, 0:11, tsl_]), ("sp", yT_[:, 11:22, :], yv[:, 11:22, tsl_])], reads=[sc["yT"]], writes=[yT_])
                kb.dma_multi([("sp", xt_[:, 0:8, :], xv[:, 0:8, tsl_]), ("sp", xt_[:, 8:16, :], xv[:, 8:16, tsl_])], reads=[xin], writes=[xt_])

            ld(0)
            for tg in range(S // 512):
                tsl = slice(tg * 512, (tg + 1) * 512)
                yT, xt = yTs[tg % 2], xts_[tg % 2]
                for o2 in range(8):
                    if o2 == 2 and tg + 1 < S // 512:
                        ld(tg + 1)
                    wb = wbs[o2 % 2]
                    for (n, k0, nk), wv in zip(br, wvs):
                        kb.dma("sp", wb[:, k0:k0 + nk, :], wv[:, :, o2 * 256:(o2 + 1) * 256], reads=[WBs[n]], writes=[wb])
                    for oi in range(2):
                        oc = o2 * 2 + oi
                        gt = gts[oc % 2]
                        kb.dma("sp", gt[:], gv[:, :, oc, tsl], reads=[sc["gT"]], writes=[gt])
                        m = ms[oc % 2]
                        for bi, (n, k0, nk) in enumerate(br):
                            ps = psX[cnt % 4]
                            cnt += 1
                            for kc in range(nk):
                                kb.op("pe", lambda e: e.matmul(ps[:], lhsT=wb[:, k0 + kc, oi * 128:(oi + 1) * 128], rhs=yT[:, k0 + kc, :],
                                                               start=(kc == 0), stop=(kc == nk - 1)), reads=[wb, yT], writes=[ps], track=(kc == nk - 1))
                            if bi == 0:
                                kb.op("dve", lambda e: e.tensor_tensor(out=m[:], in0=ps[:], in1=gt[:, 0, :], op=ALU.mult), reads=[ps, gt], writes=[m])
                            else:
                                tmp = tmps[bi % 2]
                                kb.op("dve", lambda e: e.tensor_tensor(out=tmp[:], in0=ps[:], in1=gt[:, bi, :], op=ALU.mult), reads=[ps, gt], writes=[tmp])
                                if bi == 1:
                                    kb.op("pool", lambda e: e.tensor_tensor(out=m[:], in0=m[:], in1=tmp[:], op=ALU.add), reads=[m, tmp], writes=[m])
                                else:
                                    kb.op("pool", lambda e: e.tensor_tensor(out=mT[:, oc, :], in0=m[:], in1=tmp[:], op=ALU.add), reads=[m, tmp], writes=[mT])
                for o2 in range(8):
                    wo = wos[o2 % 2]
                    kb.dma("sp", wo[:], wov[:, :, o2 * 256:(o2 + 1) * 256], reads=[WBs["w_out"]], writes=[wo])
                    for oi in range(2):
                        oc = o2 * 2 + oi
                        ps = psX[cnt % 4]
                        cnt += 1
                        for kc in range(16):
                            kb.op("pe", lambda e: e.matmul(ps[:], lhsT=wo[:, kc, oi * 128:(oi + 1) * 128], rhs=mT[:, kc, :],
                                                           start=(kc == 0), stop=(kc == 15)), reads=[wo, mT], writes=[ps], track=(kc == 15))
                        kb.op("dve", lambda e: e.tensor_tensor(out=xt[:, oc, :], in0=ps[:], in1=xt[:, oc, :], op=ALU.add), reads=[ps, xt], writes=[xt])
                kb.dma("pool", hv[:, 0:8, tsl], xt[:, 0:8, :], reads=[xt], writes=[sc["hT"]], waw=False)
                kb.dma("pool", hv[:, 8:16, tsl], xt[:, 8:16, :], reads=[xt], writes=[sc["hT"]], waw=False)

    def p3b(self, L, xout):
        kb, sc = self.kb, self.sc
        WBs = self.WB[L % 2]
        nxt = (L + 1 < self.n_layers)
        with kb.phase():
            if nxt:
                g1n = kb.sb("g1n", [128, 16], F32)
                kb.dma("sp", g1n[:], self.P["n1g"].ap()[L + 1], writes=[g1n])
                xns = [kb.sb("xn%d" % i, [128, 512], BF16) for i in range(3)]
                xnv = sc["xnT"].ap().rearrange("(kc p) t -> p kc t", p=128)
            hT = kb.sb("hT", [128, 16, 512], F32)
            hnT = kb.sb("hnT", [128, 16, 512], BF16)
            uT = kb.sb("uT", [128, 64, 512], BF16)
            sqs = [kb.sb("sq%d" % i, [128, 512], F32) for i in range(2)]
            rs = kb.sb("rs", [128, 512], F32)
            g2 = kb.sb("g2", [128, 16], F32)
            wus = [kb.sb("wu%d" % i, [128, 16, 256], BF16) for i in range(2)]
            wds = [kb.sb("wd%d" % i, [128, 64, 128], BF16) for i in range(2)]
            rls = [kb.sb("rl%d" % i, [128, 512], F32) for i in range(2)]
            psX = [kb.ps("psX%d" % i, [128, 512]) for i in range(4)]
            psN = kb.ps("psN", [128, 512])
            kb.dma("sp", g2[:], self.P["n2g"].ap()[L], writes=[g2])
            hv = sc["hT"].ap().rearrange("(kc p) t -> p kc t", p=128)
            ov = xout.ap().rearrange("(kc p) t -> p kc t", p=128)
            wuv = WBs["w_up"].ap().rearrange("(kc p) c -> p kc c", p=128)
            wdv = WBs["w_down"].ap().rearrange("(fc p) c -> p fc c", p=128)
            cnt = 0
            for tg in range(S // 512):
                tsl = slice(tg * 512, (tg + 1) * 512)
                kb.dma_multi([("sp", hT[:, 0:8, :], hv[:, 0:8, tsl]), ("sp", hT[:, 8:16, :], hv[:, 8:16, tsl])], reads=[sc["hT"]], writes=[hT])
                for kc in range(16):
                    sq = sqs[kc % 2]
                    kb.op("act", lambda e: e.activation(out=sq[:], in_=hT[:, kc, :], func=AF.Square), reads=[hT], writes=[sq])
                    kb.op("pe", lambda e: e.matmul(psN[:], lhsT=self.ones_f[:], rhs=sq[:], start=(kc == 0), stop=(kc == 15)),
                          reads=[self.ones_f, sq], writes=[psN])
                kb.op("act", lambda e: e.activation(out=rs[:], in_=psN[:], func=AF.Ln, scale=1.0 / D, bias=self.eps_t[:, 0:1]), reads=[psN, self.eps_t], writes=[rs])
                kb.op("act", lambda e: e.activation(out=rs[:], in_=rs[:], func=AF.Exp, scale=-0.5), reads=[rs], writes=[rs])
                for kc in range(16):
                    kb.op("dve", lambda e: e.scalar_tensor_tensor(out=hnT[:, kc, :], in0=hT[:, kc, :], scalar=g2[:, kc:kc + 1], in1=rs[:],
                                                                  op0=ALU.mult, op1=ALU.mult), reads=[hT, g2, rs], writes=[hnT])
                for f2 in range(32):
                    wu = wus[f2 % 2]
                    kb.dma("sp", wu[:], wuv[:, :, f2 * 256:(f2 + 1) * 256], reads=[WBs["w_up"]], writes=[wu])
                    for fi in range(2):
                        fc = f2 * 2 + fi
                        ps = psX[cnt % 4]
                        cnt += 1
                        for kc in range(16):
                            kb.op("pe", lambda e: e.matmul(ps[:], lhsT=wu[:, kc, fi * 128:(fi + 1) * 128], rhs=hnT[:, kc, :],
                                                           start=(kc == 0), stop=(kc == 15)), reads=[wu, hnT], writes=[ps], track=(kc == 15))
                        rl = rls[fc % 2]
                        kb.op("act", lambda e: e.activation(out=rl[:], in_=ps[:], func=AF.Relu), reads=[ps], writes=[rl])
                        kb.op("pool" if fc % 2 else "dve", lambda e: e.tensor_tensor(out=uT[:, fc, :], in0=rl[:], in1=rl[:], op=ALU.mult), reads=[rl], writes=[uT])
                for oc in range(16):
                    wd = wds[oc % 2]
                    kb.dma_multi([("sp", wd[:, 0:32, :], wdv[:, 0:32, oc * 128:(oc + 1) * 128]), ("sp", wd[:, 32:64, :], wdv[:, 32:64, oc * 128:(oc + 1) * 128])],
                                 reads=[WBs["w_down"]], writes=[wd])
                    ps = psX[cnt % 4]
                    cnt += 1
                    for fc in range(64):
                        kb.op("pe", lambda e: e.matmul(ps[:], lhsT=wd[:, fc, :], rhs=uT[:, fc, :], start=(fc == 0), stop=(fc == 63)),
                              reads=[wd, uT], writes=[ps], track=(fc == 63))
                    kb.op("dve", lambda e: e.tensor_tensor(out=hT[:, oc, :], in0=ps[:], in1=hT[:, oc, :], op=ALU.add), reads=[ps, hT], writes=[hT])
                kb.dma("pool", ov[:, 0:8, tsl], hT[:, 0:8, :], reads=[hT], writes=[xout], waw=False)
                kb.dma("pool", ov[:, 8:16, tsl], hT[:, 8:16, :], reads=[hT], writes=[xout], waw=False)
                if nxt:
                    for kc in range(16):
                        sq = sqs[kc % 2]
                        kb.op("act", lambda e: e.activation(out=sq[:], in_=hT[:, kc, :], func=AF.Square), reads=[hT], writes=[sq])
                        kb.op("pe", lambda e: e.matmul(psN[:], lhsT=self.ones_f[:], rhs=sq[:], start=(kc == 0), stop=(kc == 15)),
                              reads=[self.ones_f, sq], writes=[psN])
                    kb.op("act", lambda e: e.activation(out=rs[:], in_=psN[:], func=AF.Ln, scale=1.0 / D, bias=self.eps_t[:, 0:1]), reads=[psN, self.eps_t], writes=[rs])
                    kb.op("act", lambda e: e.activation(out=rs[:], in_=rs[:], func=AF.Exp, scale=-0.5), reads=[rs], writes=[rs])
                    for kc in range(16):
                        xn = xns[kc % 3]
                        kb.op("dve", lambda e: e.scalar_tensor_tensor(out=xn[:], in0=hT[:, kc, :], scalar=g1n[:, kc:kc + 1], in1=rs[:],
                                                                                          op0=ALU.mult, op1=ALU.mult), reads=[hT, g1n, rs], writes=[xn])
                        kb.dma("pool", xnv[:, kc, tsl], xn[:], reads=[xn], writes=[sc["xnT"]], waw=False)

    def attn(self, L):
        kb, sc = self.kb, self.sc
        SKEW = 2
        scale = 1.0 / np.sqrt(128.0)
        with kb.phase():
            KTs = [kb.sb("KT%d" % i, [128, S], BF16) for i in range(2)]
            Vs = [kb.sb("V%d" % i, [128, 32, 128], BF16) for i in range(2)]
            QTs = [kb.sb("QT%d" % i, [128, S], BF16) for i in range(2)]
            am = kb.sb("am", [128, 20, 512], BF16)
            PTs = [kb.sb("PT%d" % i, [128, 512], BF16) for i in range(4)]
            yos = [kb.sb("yo%d" % i, [128, 512], BF16) for i in range(2)]
            rzs = [kb.sb("rz%d" % i, [128, 512], F32) for i in range(2)]
            psS = [kb.ps("psS%d" % i, [128, 512]) for i in range(3)]
            psO = [kb.ps("psO%d" % i, [128, 512]) for i in range(2)]
            psZ = [kb.ps("psZ%d" % i, [128, 512]) for i in range(2)]
            amv = self.C["amask"].ap().rearrange("r p c -> p r c")
            for r0 in range(0, 20, 5):
                kb.dma("pool", am[:, r0:r0 + 5, :], amv[:, r0:r0 + 5, :], writes=[am])
            jobs = []
            for h in range(6):
                jobs.append(("A", sc["aqT"], h * 128, sc["akT"], h * 128, sc["av"], h * 128, h * 128, ("A", h)))
            for h in range(8):
                kvh = h // 4
                jobs.append(("B", sc["bqT"], h * 128, sc["bkT"], kvh * 128, sc["bv"], kvh * 128, 768 + h * 128, ("B", kvh)))
            cur_kv = None
            nkv = 0
            g = 0
            for ji, (kind, qsrc, q0, ksrc, k0, vsrc, v0, y0, kvid) in enumerate(jobs):
                QT = QTs[ji % 2]
                kb.dma("sp", QT[:], qsrc.ap()[q0:q0 + 128, :], reads=[qsrc], writes=[QT])
                if kvid != cur_kv:
                    cur_kv = kvid
                    KT = KTs[nkv % 2]
                    V = Vs[nkv % 2]
                    nkv += 1
                    kb.dma("sp", KT[:], ksrc.ap()[k0:k0 + 128, :], reads=[ksrc], writes=[KT])
                    vv = vsrc.ap()[:, v0:v0 + 128].rearrange("(kc p) d -> p kc d", p=128)
                    for c0 in range(0, 32, 8):
                        kb.dma("sp", V[:, c0:c0 + 8, :], vv[:, c0:c0 + 8, :], reads=[vsrc], writes=[V])
                for qg in range(8):
                    kcs = list(range(32)) if kind == "B" else list(range(max(0, 4 * qg - 8), min(32, 4 * qg + 12)))
                    pso, psz = psO[g % 2], psZ[g % 2]
                    n = len(kcs)
                    for i in range(n + SKEW):
                        if i < n:
                            kc = kcs[i]
                            ps = psS[i % 3]
                            PT = PTs[i % 4]
                            kb.op("pe", lambda e: e.matmul(ps[:], lhsT=KT[:, kc * 128:(kc + 1) * 128], rhs=QT[:, qg * 512:(qg + 1) * 512], start=True, stop=True),
                                  reads=[KT, QT], writes=[ps])
                            kb.op("act", lambda e: e.activation(out=PT[:], in_=ps[:], func=AF.Exp, scale=float(scale)), reads=[ps], writes=[PT])
                            if kind == "A":
                                r = kc - 4 * qg + 8
                                kb.op("dve", lambda e: e.tensor_tensor(out=PT[:], in0=PT[:], in1=am[:, r, :], op=ALU.mult), reads=[PT, am], writes=[PT])
                        if i >= SKEW:
                            j = i - SKEW
                            kc = kcs[j]
                            PT = PTs[j % 4]
                            kb.op("pe", lambda e: e.matmul(pso[:], lhsT=V[:, kc, :], rhs=PT[:], start=(j == 0), stop=(j == n - 1)),
                                  reads=[V, PT], writes=[pso], track=(j == n - 1))
                            kb.op("pe", lambda e: e.matmul(psz[:], lhsT=self.ones_b[:], rhs=PT[:], start=(j == 0), stop=(j == n - 1)),
                                  reads=[self.ones_b, PT], writes=[psz], track=True)
                    rz, yo = rzs[g % 2], yos[g % 2]
                    kb.op("dve", lambda e: e.reciprocal(out=rz[:], in_=psz[:]), reads=[psz], writes=[rz])
                    kb.op("dve", lambda e: e.tensor_tensor(out=yo[:], in0=pso[:], in1=rz[:], op=ALU.mult), reads=[pso, rz], writes=[yo])
                    kb.dma("pool", sc["yT"].ap()[y0:y0 + 128, qg * 512:(qg + 1) * 512], yo[:], reads=[yo], writes=[sc["yT"]], waw=False)
                    g += 1

    def mlstm(self, L):
        kb, sc = self.kb, self.sc
        NCH = S // 64
        with kb.phase():
            Wt = kb.sb("Wt", [64, 2, NCH, 8], F32)
            EBt = kb.sb("EBt", [64, 2, NCH, 8], F32)
            EBTt = kb.sb("EBTt", [64, 2, NCH, 8], F32)
            W2t = kb.sb("W2t", [64, 2, NCH, 8], F32)
            Wb = kb.sb("Wb", [64, 2, NCH, 8], BF16)
            W2b = kb.sb("W2b", [64, 2, NCH, 8], BF16)
            cm = kb.sb("cm", [64, 3, 64], F32)
            trib = kb.sb("trib", [64, 2, 64], BF16)
            outg = kb.sb("outg", [64, 1024], F32)
            kb.dma("sp", cm[:], self.C["cmats"].ap().rearrange("r p c -> p r c"), writes=[cm])
            kb.dma("pool", trib[:], self.C["cmats"].ap()[0:2].rearrange("r p c -> p r c"), writes=[trib])
            kb.dma("sp", outg[:], self.P["outg"].ap()[L].partition_broadcast(64), writes=[outg])

            with kb.phase():
                G = kb.sb("G", [64, NCH, 32], F32)
                gb = kb.sb("gb", [64, 32], F32)
                nlf = kb.sb("nlf", [64, 2, NCH, 8], F32)
                tmp = kb.sb("gtmp", [64, 2, NCH, 8], F32)
                psA = [kb.ps("psg%d" % i, [128, 512]) for i in range(2)]
                psB = [kb.ps("psh%d" % i, [128, 512]) for i in range(2)]
                cgv = sc["cg"].ap().rearrange("(c t) g -> t c g", t=64)
                for c0 in range(0, NCH, 16):
                    kb.dma("sp", G[:, c0:c0 + 16, :], cgv[:, c0:c0 + 16, :], reads=[sc["cg"]], writes=[G])
                kb.dma("sp", gb[:], self.P["gateb"].ap()[L].partition_broadcast(64), writes=[gb])
                kb.op("dve", lambda e: e.tensor_tensor(out=G[:], in0=G[:], in1=gb[:, :].unsqueeze(1).broadcast_to([64, NCH, 32]), op=ALU.add),
                      reads=[G, gb], writes=[G])
                for d in range(2):
                    fcol = 8 + 16 * d
                    kb.op("act", lambda e: e.activation(out=nlf[:, d], in_=G[:, :, fcol:fcol + 8], func=AF.Exp, scale=-1.0), reads=[G], writes=[nlf])
                    kb.op("act", lambda e: e.activation(out=nlf[:, d], in_=nlf[:, d], func=AF.Ln, bias=1.0), reads=[nlf], writes=[nlf])
                for d in range(2):
                    icol = 16 * d
                    rhs = nlf[:, d].rearrange("p c h -> p (c h)")
                    kb.op("pe", lambda e: e.matmul(psA[d][0:64, :], lhsT=cm[:, d, :], rhs=rhs, start=True, stop=True), reads=[cm, nlf], writes=[psA[d]])
                    kb.op("pe", lambda e: e.matmul(psB[d][0:64, :], lhsT=cm[:, 2, :], rhs=rhs, start=True, stop=True), reads=[cm, nlf], writes=[psB[d]])
                    csv = psA[d][0:64, :].rearrange("p (c h) -> p c h", h=8)
                    kb.op("dve", lambda e: e.tensor_tensor(out=tmp[:, d], in0=csv, in1=G[:, :, icol:icol + 8], op=ALU.add), reads=[psA[d], G], writes=[tmp])
                    kb.op("act", lambda e: e.activation(out=Wt[:, d], in_=tmp[:, d], func=AF.Exp), reads=[tmp], writes=[Wt])
                    kb.op("act", lambda e: e.activation(out=EBt[:, d], in_=csv, func=AF.Exp, scale=-1.0), reads=[psA[d]], writes=[EBt])
                    kb.op("act", lambda e: e.activation(out=EBTt[:, d], in_=psB[d][0:64, :].rearrange("p (c h) -> p c h", h=8), func=AF.Exp, scale=-1.0),
                          reads=[psB[d]], writes=[EBTt])
                    kb.op("dve", lambda e: e.tensor_tensor(out=W2t[:, d], in0=Wt[:, d], in1=EBTt[:, d], op=ALU.mult), reads=[Wt, EBTt], writes=[W2t])
                kb.op("dve", lambda e: e.tensor_copy(out=Wb[:], in_=Wt[:]), reads=[Wt], writes=[Wb])
                kb.op("dve", lambda e: e.tensor_copy(out=W2b[:], in_=W2t[:]), reads=[W2t], writes=[W2b])

            with kb.phase():
                xs = [kb.sb("cx%d" % i, [128, 8, 516], F32) for i in range(2)]
                cw = kb.sb("cw", [128, 8, 5], F32)
                cb = kb.sb("cb", [128, 8], F32)
                accs = [kb.sb("acc%d" % i, [128, 512], F32) for i in range(2)]
                qbs = [kb.sb("qb%d" % i, [128, 512], BF16) for i in range(3)]
                kfs = [kb.sb("kf%d" % i, [128, 512], F32) for i in range(2)]
                kfm = kb.sb("kfm", [128, 4, 512], BF16)
                kts = [kb.sb("kt%d" % i, [128, 512], BF16) for i in range(2)]
                psT = [kb.ps("psT%d" % i, [128, 1024], BF16) for i in range(2)]
                kb.dma("sp", cw[:], self.P["convw"].ap()[L], writes=[cw])
                kb.dma("sp", cb[:], self.P["convb"].ap()[L], writes=[cb])
                cv_ = sc["cqkT"].ap().rearrange("(c p) t -> p c t", p=128)
                for tg in range(8):
                    x = xs[tg % 2]
                    t0 = tg * 512
                    lo, hi = max(0, t0 - 2), min(S, t0 + 514)
                    if tg == 0:
                        kb.op("dve", lambda e: e.memset(x[:, :, 0:2], 0.0), writes=[x])
                    if tg == 7:
                        kb.op("dve", lambda e: e.memset(x[:, :, 514:516], 0.0), writes=[x])
                    b0 = lo - (t0 - 2)
                    kb.dma("sp", x[:, 0:4, b0:b0 + (hi - lo)], cv_[:, 0:4, lo:hi], reads=[sc["cqkT"]], writes=[x])
                    kb.dma("sp", x[:, 4:8, b0:b0 + (hi - lo)], cv_[:, 4:8, lo:hi], reads=[sc["cqkT"]], writes=[x])
                    for c in range(8):
                        acc = accs[c % 2]
                        kb.op("dve", lambda e: e.tensor_scalar(out=acc[:], in0=x[:, c, 0:512], scalar1=cw[:, c, 0:1], scalar2=None, op0=ALU.mult),
                              reads=[x, cw], writes=[acc])
                        for j in range(1, 5):
                            kb.op("dve", lambda e: e.scalar_tensor_tensor(out=acc[:], in0=x[:, c, j:j + 512], scalar=cw[:, c, j:j + 1], in1=acc[:],
                                                                          op0=ALU.mult, op1=ALU.add), reads=[x, cw, acc], writes=[acc])
                        if c < 4:
                            qb = qbs[c % 3]
                            kb.op("act", lambda e: e.activation(out=qb[:], in_=acc[:], func=AF.Silu, bias=cb[:, c:c + 1]), reads=[acc, cb], writes=[qb])
                            kb.dma("pool", sc["cqT"].ap()[c * 128:(c + 1) * 128, t0:t0 + 512], qb[:], reads=[qb], writes=[sc["cqT"]], waw=False)
                        else:
                            kf = kfs[c % 2]
                            kb.op("act", lambda e: e.activation(out=kf[:], in_=acc[:], func=AF.Silu, bias=cb[:, c:c + 1]), reads=[acc, cb], writes=[kf])
                            kb.op("pool", lambda e: e.tensor_scalar(out=kfm[:, c - 4, :], in0=kf[:], scalar1=0.125, scalar2=None, op0=ALU.mult),
                                  reads=[kf], writes=[kfm])
                    kb.dma("pool", sc["ckT"].ap().rearrange("(c p) t -> p c t", p=128)[:, :, t0:t0 + 512], kfm[:], reads=[kfm], writes=[sc["ckT"]], waw=False)
                    for tt in range(4):
                        pt = psT[tt % 2]
                        for c in range(4):
                            kb.op("pe", lambda e: e.transpose(out=pt[:, c * 128:(c + 1) * 128], in_=kfm[:, c, tt * 128:(tt + 1) * 128], identity=self.ident_b[:]),
                                  reads=[kfm, self.ident_b], writes=[pt])
                        kt = kts[tt % 2]
                        kb.op("act", lambda e: e.activation(out=kt[:], in_=pt[:, 0:512], func=AF.Copy), reads=[pt], writes=[kt])
                        kb.dma("pool", sc["ck"].ap()[t0 + tt * 128:t0 + (tt + 1) * 128, :], kt[:], reads=[kt], writes=[sc["ck"]], waw=False)

            with kb.phase():
                gens = [self._mlstm_dir(L, d, (Wt, EBt, EBTt, W2t, Wb, W2b, trib)) for d in range(2)]
                live = [True, True]
                while any(live):
                    for d in range(2):
                        if live[d]:
                            try:
                                next(gens[d])
                            except StopIteration:
                                live[d] = False
            with kb.phase():
                hfs = [kb.sb("hf%d" % i, [128, 1024], F32) for i in range(2)]
                hbs = [kb.sb("hb%d" % i, [128, 1024], F32) for i in range(2)]
                cos_ = [kb.sb("co%d" % i, [128, 1024], F32) for i in range(2)]
                og = kb.sb("og", [128, 1024], F32)
                junk = kb.sb("junk", [128, 1024], F32)
                ybs = [kb.sb("yb%d" % i, [128, 1024], BF16) for i in range(2)]
                yts = [kb.sb("yt%d" % i, [128, 8, 128], BF16) for i in range(2)]
                s0s = [kb.sb("s0%d" % i, [128, 8], F32) for i in range(2)]
                s1s = [kb.sb("s1%d" % i, [128, 8], F32) for i in range(2)]
                psT = [kb.ps("psT%d" % i, [128, 1024], BF16) for i in range(2)]
                kb.dma("sp", og[:], self.P["outg"].ap()[L].partition_broadcast(128), writes=[og])
                yv = sc["yT"].ap()[1792:2816, :].rearrange("(h p) t -> p h t", p=128)
                for tt in range(S // 128):
                    b = tt % 2
                    rsl = slice(tt * 128, (tt + 1) * 128)
                    hf, hb, co, yb, yt, s0, s1, pt = hfs[b], hbs[b], cos_[b], ybs[b], yts[b], s0s[b], s1s[b], psT[b]
                    kb.dma("sp", hf[:], sc["hfw"].ap()[rsl, :], reads=[sc["hfw"]], writes=[hf])
                    kb.dma("sp", hb[:], sc["hbw"].ap()[rsl, :], reads=[sc["hbw"]], writes=[hb])
                    kb.dma("sp", co[:], sc["co"].ap()[rsl, :], reads=[sc["co"]], writes=[co])
                    kb.op("pool", lambda e: e.tensor_tensor(out=hf[:], in0=hf[:], in1=hb[:], op=ALU.add), reads=[hf, hb], writes=[hf])
                    kb.op("act", lambda e: e.activation(out=junk[:], in_=hf[:], func=AF.Square), reads=[hf], writes=[junk])
                    kb.op("dve", lambda e: e.tensor_reduce(out=s0[:], in_=junk[:].rearrange("p (h v) -> p h v", h=8), axis=AX.X, op=ALU.add), reads=[junk], writes=[s0])
                    kb.op("act", lambda e: e.activation(out=s1[:], in_=s0[:], func=AF.Ln, scale=1.0 / 128, bias=self.eps_t[:, 0:1]), reads=[s0, self.eps_t], writes=[s1])
                    kb.op("act", lambda e: e.activation(out=s1[:], in_=s1[:], func=AF.Exp, scale=-0.5), reads=[s1], writes=[s1])
                    kb.op("pool", lambda e: e.tensor_tensor(out=co[:], in0=co[:], in1=og[:], op=ALU.mult), reads=[co, og], writes=[co])
                    kb.op("dve", lambda e: e.tensor_tensor(out=hf[:].rearrange("p (h v) -> p h v", h=8), in0=hf[:].rearrange("p (h v) -> p h v", h=8),
                                                           in1=s1[:, :].unsqueeze(2).broadcast_to([128, 8, 128]), op=ALU.mult), reads=[hf, s1], writes=[hf])
                    kb.op("pool", lambda e: e.tensor_tensor(out=yb[:], in0=hf[:], in1=co[:], op=ALU.mult), reads=[hf, co], writes=[yb])
                    for h in range(8):
                        kb.op("pe", lambda e: e.transpose(out=pt[:, h * 128:(h + 1) * 128], in_=yb[:, h * 128:(h + 1) * 128], identity=self.ident_b[:]),
                              reads=[yb, self.ident_b], writes=[pt], track=(h == 7))
                    kb.op("act", lambda e: e.activation(out=yt[:], in_=pt[:, :].rearrange("p (h t) -> p h t", h=8), func=AF.Copy), reads=[pt], writes=[yt])
                    kb.dma("pool", yv[:, :, rsl], yt[:], reads=[yt], writes=[sc["yT"]], waw=False)

    def _mlstm_dir(self, L, d, tl):
        kb, sc = self.kb, self.sc
        NCH = S // 64
        Wt, EBt, EBTt, W2t, Wb, W2b, trib = tl
        n_ = "f" if d == 0 else "b"
        QTc = [kb.sb("QTc%s%d" % (n_, i), [64, 8, 64], BF16) for i in range(2)]
        KTc = [kb.sb("KTc%s%d" % (n_, i), [64, 8, 64], BF16) for i in range(2)]
        Ktk = [kb.sb("Ktk%s%d" % (n_, i), [64, 512], BF16) for i in range(2)]
        vcs = [kb.sb("vc%s%d" % (n_, i), [64, 8, 128], F32) for i in range(2)]
        ATs = [kb.sb("AT%s%d" % (n_, i), [64, 8, 64], BF16) for i in range(2)]
        v1s = [kb.sb("v1%s%d" % (n_, i), [64, 8, 128], BF16) for i in range(2)]
        v2s = [kb.sb("v2%s%d" % (n_, i), [64, 8, 128], BF16) for i in range(2)]
        Cst = kb.sb("Cst" + n_, [64, 8, 128], F32)
        nst = kb.sb("nst" + n_, [64, 8], F32)
        Cbfs = [kb.sb("Cbf%s%d" % (n_, i), [64, 8, 128], BF16) for i in range(2)]
        nbfs = [kb.sb("nbf%s%d" % (n_, i), [64, 8], BF16) for i in range(2)]
        sm = [kb.sb("sm%s%d" % (n_, i), [64, 8], F32) for i in range(4)]
        hds = [kb.sb("hd%s%d" % (n_, i), [64, 8, 128], F32) for i in range(2)]
        psS = kb.ps("mpsS" + n_, [128, 512])
        psN = kb.ps("mpsN" + n_, [128, 512])
        psU = kb.ps("mpsU" + n_, [128, 512])
        psD = kb.ps("mpsD" + n_, [128, 512])
        hdst = sc["hfw"] if d == 0 else sc["hbw"]
        e_v = "pool"
        kb.op("dve", lambda e: e.memset(Cst[:], 0.0), writes=[Cst])
        kb.op("dve", lambda e: e.memset(nst[:], 0.0), writes=[nst])
        kb.op("dve", lambda e: e.memset(Cbfs[0][:], 0.0), writes=[Cbfs[0]])
        kb.op("dve", lambda e: e.memset(nbfs[0][:], 0.0), writes=[nbfs[0]])
        qv = sc["cqT"].ap().rearrange("(h dd) t -> dd h t", dd=64)
        kv = sc["ckT"].ap().rearrange("(h dd) t -> dd h t", dd=64)
        order = list(range(NCH)) if d == 0 else list(range(NCH - 1, -1, -1))

        def loads(it):
            c = order[it]
            b = it % 2
            tsl = slice(c * 64, (c + 1) * 64)
            kb.dma("sp", QTc[b][:], qv[:, :, tsl], reads=[sc["cqT"]], writes=[QTc[b]])
            kb.dma("sp", KTc[b][:], kv[:, :, tsl], reads=[sc["ckT"]], writes=[KTc[b]])
            kb.dma("sp", Ktk[b][:], sc["ck"].ap()[tsl, :], reads=[sc["ck"]], writes=[Ktk[b]])
            kb.dma("sp", vcs[b][:], sc["cv"].ap()[tsl, :].rearrange("t (h v) -> t h v", h=8), reads=[sc["cv"]], writes=[vcs[b]])

        loads(0)
        for it, c in enumerate(order):
            b = it % 2
            tsl = slice(c * 64, (c + 1) * 64)
            Q, K, Kt, vc, AT, v1, v2 = QTc[b], KTc[b], Ktk[b], vcs[b], ATs[b], v1s[b], v2s[b]
            Cbf, nbf, Cbn, nbn = Cbfs[b], nbfs[b], Cbfs[1 - b], nbfs[1 - b]
            for h in range(8):
                kb.op("pe", lambda e: e.matmul(psS[0:64, h * 64:(h + 1) * 64], lhsT=K[:, h, :], rhs=Q[:, h, :], start=True, stop=True),
                      reads=[K, Q], writes=[psS], track=(h == 7))
            if it + 1 < NCH:
                loads(it + 1)
            yield
            kb.op("dve", lambda e: e.tensor_tensor(out=AT[:], in0=psS[0:64, :].rearrange("p (h t) -> p h t", h=8),
                                                   in1=trib[:, d, :].unsqueeze(1).broadcast_to([64, 8, 64]), op=ALU.mult), reads=[psS, trib], writes=[AT])
            kb.op(e_v, lambda e: e.tensor_tensor(out=v1[:], in0=vc[:], in1=Wt[:, d, c, :].unsqueeze(2).broadcast_to([64, 8, 128]), op=ALU.mult),
                  reads=[vc, Wt], writes=[v1])
            kb.op(e_v, lambda e: e.tensor_tensor(out=v2[:], in0=vc[:], in1=W2t[:, d, c, :].unsqueeze(2).broadcast_to([64, 8, 128]), op=ALU.mult),
                  reads=[vc, W2t], writes=[v2])
            yield
            eb = EBt[:, d, c, :]
            ebt = EBTt[:, d, c, :]
            hd = hds[b]
            s0, s1, s2, s3 = sm
            for h in range(8):
                kb.op("pe", lambda e: e.matmul(psD[0:64, h:h + 1], lhsT=AT[:, h, :], rhs=Wb[:, d, c, h:h + 1], start=True, stop=False),
                      reads=[AT, Wb], writes=[psD], track=False)
                kb.op("pe", lambda e: e.matmul(psD[0:64, h:h + 1], lhsT=Q[:, h, :], rhs=nbf[:, h:h + 1], start=False, stop=True),
                      reads=[Q, nbf], writes=[psD], track=(h == 7))
            kb.op("dve", lambda e: e.tensor_tensor(out=s0[:], in0=psD[0:64, 0:8], in1=eb, op=ALU.mult), reads=[psD, EBt], writes=[s0])
            kb.op("dve", lambda e: e.scalar_tensor_tensor(out=s1[:], in0=s0[:], scalar=-1.0, in1=s0[:], op0=ALU.mult, op1=ALU.max), reads=[s0], writes=[s1])
            kb.op("dve", lambda e: e.tensor_scalar(out=s1[:], in0=s1[:], scalar1=1.0, scalar2=None, op0=ALU.max), reads=[s1], writes=[s1])
            kb.op("dve", lambda e: e.reciprocal(out=s2[:], in_=s1[:]), reads=[s1], writes=[s2])
            kb.op("dve", lambda e: e.tensor_tensor(out=s3[:], in0=s2[:], in1=eb, op=ALU.mult), reads=[s2, EBt], writes=[s3])
            for hf_ in range(2):
                h0 = hf_ * 4
                hs = slice(h0, h0 + 4)
                for hh in range(4):
                    h = h0 + hh
                    kb.op("pe", lambda e: e.matmul(psN[0:64, hh * 128:(hh + 1) * 128], lhsT=AT[:, h, :], rhs=v1[:, h, :], start=True, stop=False),
                          reads=[AT, v1], writes=[psN], track=False)
                    kb.op("pe", lambda e: e.matmul(psN[0:64, hh * 128:(hh + 1) * 128], lhsT=Q[:, h, :], rhs=Cbf[:, h, :], start=False, stop=True),
                          reads=[Q, Cbf], writes=[psN], track=(hh == 3))
                for hh in range(4):
                    h = h0 + hh
                    kb.op("pe", lambda e: e.matmul(psU[0:64, hh * 128:(hh + 1) * 128], lhsT=Kt[:, h * 64:(h + 1) * 64], rhs=v2[:, h, :], start=True, stop=True),
                          reads=[Kt, v2], writes=[psU], track=False)
                    kb.op("pe", lambda e: e.matmul(psD[0:64, 8 + h:9 + h], lhsT=Kt[:, h * 64:(h + 1) * 64], rhs=W2b[:, d, c, h:h + 1], start=True, stop=True),
                          reads=[Kt, W2b], writes=[psD], track=(hh == 3))
                yield
                if hf_ == 0:
                    kb.op("pool", lambda e: e.tensor_tensor(out=Cst[:], in0=Cst[:], in1=ebt.unsqueeze(2).broadcast_to([64, 8, 128]), op=ALU.mult),
                          reads=[Cst, EBTt], writes=[Cst])
                    kb.op("pool", lambda e: e.tensor_tensor(out=nst[:], in0=nst[:], in1=ebt, op=ALU.mult), reads=[nst, EBTt], writes=[nst])
                kb.op("dve", lambda e: e.tensor_tensor(out=Cst[:, hs, :], in0=Cst[:, hs, :], in1=psU[0:64, :].rearrange("p (h v) -> p h v", h=4), op=ALU.add),
                      reads=[Cst, psU], writes=[Cst])
                kb.op("dve", lambda e: e.tensor_tensor(out=nst[:, hs], in0=nst[:, hs], in1=psD[0:64, 8 + h0:12 + h0], op=ALU.add), reads=[nst, psD], writes=[nst])
                kb.op("act", lambda e: e.activation(out=Cbn[:, hs, :], in_=Cst[:, hs, :], func=AF.Copy), reads=[Cst], writes=[Cbn])
                kb.op("act", lambda e: e.activation(out=nbn[:, hs], in_=nst[:, hs], func=AF.Copy), reads=[nst], writes=[nbn])
                for hh in range(4):
                    h = h0 + hh
                    kb.op("act", lambda e: e.activation(out=hd[:, h, :], in_=psN[0:64, hh * 128:(hh + 1) * 128], func=AF.Copy, scale=s3[:, h:h + 1]),
                          reads=[psN, s3], writes=[hd])
                yield
            kb.dma("pool", hdst.ap()[tsl, :], hd[:].rearrange("p h v -> p (h v)"), reads=[hd], writes=[hdst], waw=False)
            yield

    def build(self, phases=("p1", "attn", "mlstm", "p3a", "p3b")):
        kb = self.kb
        with kb.phase():
            self.consts_phase()
            self.cast_weights(0)
            for L in range(self.n_layers):
                xin = self.xT if L == 0 else self.xbuf[(L - 1) % 2]
                xout = self.outT if L == self.n_layers - 1 else self.xbuf[L % 2]
                if "p1" in phases:
                    self.p1(L, xin, 0)
                    self.p1(L, xin, 1)
                if L + 1 < self.n_layers:
                    for n, _, _ in WEIGHTS:
                        kb.wait_tile("pool", self.WB[L % 2][n])
                    self.cast_weights(L + 1)
                if "attn" in phases:
                    self.attn(L)
                if "mlstm" in phases:
                    self.mlstm(L)
                if "p3a" in phases:
                    self.p3a(L, xin)
                if "p3b" in phases:
                    self.p3b(L, xout)
            kb.barrier()
        return self.nc


_PROG_CACHE = {}


def kernel(**inputs):
    x = np.asarray(inputs["x"], dtype=np.float32)
    B = x.shape[0]
    consts = make_consts()
    params = prep_params(inputs)
    if "nc" not in _PROG_CACHE:
        _PROG_CACHE["nc"] = Prog().build()
    nc = _PROG_CACHE["nc"]
    shared = {}
    for n, _, _ in WEIGHTS:
        shared[n] = np.ascontiguousarray(inputs[n], dtype=np.float32)
    shared.update(params)
    shared.update(consts)
    in_maps = []
    active = [0, 1, 4, 5]
    zero_x = np.zeros((D, S), np.float32)
    for c in range(NCORES):
        m = dict(shared)
        m["xT"] = np.ascontiguousarray(x[active.index(c)].T) if c in active else zero_x
        in_maps.append(m)
    res = run_bass_kernel_spmd(nc, in_maps, core_ids=list(range(NCORES)))
    out = np.stack([np.ascontiguousarray(res.results[active[b]]["outT"].T) for b in range(B)], axis=0)
    return out.astype(np.float32)
```

```python
import contextlib
import numpy as np
import concourse.bass as bass
import concourse.mybir as mybir
from concourse.bass_utils import run_bass_kernel_spmd

F32 = mybir.dt.float32
BF16 = mybir.dt.bfloat16
AF = mybir.ActivationFunctionType
ALU = mybir.AluOpType
AX = mybir.AxisListType

S = 4096
D = 2048
DEPTH = 4
NCORES = 8
IN_COLS = 13088
DFF = 8192
EPS = 1e-6
O_AQ, O_AK, O_AV, O_BQ, O_BK, O_BV, O_CQK, O_CV, O_CO, O_CG, O_G = 0, 768, 1536, 2304, 3328, 3584, 3840, 4864, 5888, 6912, 6944


class T:
    def __init__(self, kb, h, name):
        self.h = h
        self.name = name
        self.last_w = None
        self.readers = {}
        self.sem = None
        self.dcnt = 0
        self.kb = kb

    def ap(self):
        return self.h.ap()

    def __getitem__(self, idx):
        return self.h.__getitem__(idx)

    def get_sem(self):
        if self.sem is None:
            self.sem, self.dcnt = self.kb.take_sem(self.name)
        return self.sem


class KB:
    def __init__(self, nc):
        self.nc = nc
        self.engs = {"pe": nc.tensor, "act": nc.scalar, "dve": nc.vector, "pool": nc.gpsimd, "sp": nc.sync}
        self.sem = {e: nc.alloc_semaphore(name="s_" + e) for e in self.engs}
        self.cnt = {e: 0 for e in self.engs}
        self.waited = {e: {} for e in self.engs}
        self.sem_pool = []
        self.live_dma = {}
        self.n_ins = 0
        self.n_wait = 0
        self.uid = 0
        self.stack = None

    def take_sem(self, name):
        if self.sem_pool:
            return self.sem_pool.pop()
        self.uid += 1
        return self.nc.alloc_semaphore(name="d%d" % self.uid), 0

    def sb(self, name, shape, dt):
        self.uid += 1
        h = self.stack.enter_context(self.nc.sbuf_tensor("%s_%d" % (name, self.uid), list(shape), dt))
        t = T(self, h, name)
        self.phase_tiles.append(t)
        return t

    def ps(self, name, shape, dt=F32):
        self.uid += 1
        h = self.stack.enter_context(self.nc.psum_tensor("%s_%d" % (name, self.uid), list(shape), dt))
        t = T(self, h, name)
        self.phase_tiles.append(t)
        return t

    def dram(self, name, shape, dt, kind="Internal"):
        return T(self, self.nc.dram_tensor(name, list(shape), dt, kind=kind), name)

    @contextlib.contextmanager
    def phase(self):
        old = self.stack
        old_tiles = getattr(self, "phase_tiles", None)
        self.phase_tiles = []
        with contextlib.ExitStack() as st:
            self.stack = st
            yield
            self.barrier()
            for t in self.phase_tiles:
                if t.sem is not None:
                    self.sem_pool.append((t.sem, t.dcnt))
                    t.sem = None
        self.stack = old
        self.phase_tiles = old_tiles

    def _wait(self, eng, evs):
        w = self.waited[eng]
        need = {}
        for ev in evs:
            if ev is None:
                continue
            s, v = ev
            if eng == "pe" and s is self.sem["pe"]:
                continue
            if w.get(id(s), 0) >= v:
                continue
            if id(s) not in need or need[id(s)][1] < v:
                need[id(s)] = (s, v)
        for s, v in need.values():
            self.engs[eng].wait_ge(s, v)
            w[id(s)] = v
            self.n_wait += 1

    def _deps(self, reads, writes):
        evs = []
        for t in reads:
            evs.append(t.last_w)
        for t in writes:
            evs.append(t.last_w)
            evs.extend(t.readers.values())
        return evs

    def _add_reader(self, t, ev):
        s, v = ev
        cur = t.readers.get(id(s))
        if cur is None or cur[1] < v:
            t.readers[id(s)] = ev

    def op(self, eng, fn, reads=(), writes=(), track=True):
        self._wait(eng, self._deps(reads, writes))
        ins = fn(self.engs[eng])
        self.n_ins += 1
        if track:
            self.cnt[eng] += 1
            ins.then_inc(self.sem[eng], 1)
            ev = (self.sem[eng], self.cnt[eng])
        else:
            ev = (self.sem[eng], self.cnt[eng] + 1)
        for t in reads:
            self._add_reader(t, ev)
        for t in writes:
            t.last_w = ev
            t.readers = {}
        return ins

    def dma(self, q, out, in_, reads=(), writes=(), waw=True, **kw):
        evs = self._deps(reads, writes)
        if not waw:
            evs = [t.last_w for t in reads]
            for t in writes:
                evs.extend(t.readers.values())
        self._wait(q, evs)
        owner = writes[0]
        sem = owner.get_sem()
        ins = self.engs[q].dma_start(out=out, in_=in_, **kw)
        owner.dcnt += 16
        ins.then_inc(sem, 16)
        self.n_ins += 1
        ev = (sem, owner.dcnt)
        self.live_dma[id(sem)] = ev
        for t in reads:
            self._add_reader(t, ev)
        for t in writes:
            t.last_w = ev
            t.readers = {}
        return ins

    def dma_multi(self, items, reads=(), writes=()):
        evs = self._deps(reads, writes)
        owner = writes[0]
        sem = owner.get_sem()
        for q in dict.fromkeys(q for q, _, _ in items):
            self._wait(q, evs)
        for q, o, i in items:
            ins = self.engs[q].dma_start(out=o, in_=i)
            owner.dcnt += 16
            ins.then_inc(sem, 16)
            self.n_ins += 1
        ev = (sem, owner.dcnt)
        self.live_dma[id(sem)] = ev
        for t in reads:
            self._add_reader(t, ev)
        for t in writes:
            t.last_w = ev
            t.readers = {}

    def wait_tile(self, eng, t):
        self._wait(eng, [t.last_w])

    def barrier(self):
        evs = [(self.sem[e], self.cnt[e]) for e in self.engs if self.cnt[e] > 0]
        evs += list(self.live_dma.values())
        for e in self.engs:
            self._wait(e, evs)


def bcast_mid(ap2d, n, inner):
    return ap2d.rearrange("p (o i) -> p o i", o=1).broadcast_to([ap2d.shape[0], n, inner])


def _rope_tables_np(pos, dim):
    inv = (np.float32(10000.0) ** (-(np.arange(0, dim, 2, dtype=np.float32)) / np.float32(dim))).astype(np.float32)
    ang = pos.astype(np.float32)[:, None] * inv[None, :]
    return np.cos(ang).astype(np.float32), np.sin(ang).astype(np.float32)


def make_consts():
    c = {}
    pos = np.arange(S)
    cos1, sin1 = _rope_tables_np(pos, 128)
    ropeA = np.zeros((2, 128, S), np.float32)
    ropeA[0] = np.concatenate([cos1, cos1], 1).T
    ropeA[1] = np.concatenate([sin1, sin1], 1).T
    cr, sr = _rope_tables_np(pos // 64, 64)
    cc, sc = _rope_tables_np(pos % 64, 64)
    ropeB = np.zeros((2, 128, S), np.float32)
    ropeB[0] = np.concatenate([cr, cr, cc, cc], 1).T
    ropeB[1] = np.concatenate([sr, sr, sc, sc], 1).T
    c["ropeA"] = ropeA
    c["ropeB"] = ropeB
    RA = np.zeros((128, 128), np.float32)
    for dp in range(128):
        if dp < 64:
            RA[dp + 64, dp] = -1.0
        else:
            RA[dp - 64, dp] = 1.0
    RB = np.zeros((128, 128), np.float32)
    for dp in range(128):
        if dp % 64 < 32:
            RB[dp + 32, dp] = -1.0
        else:
            RB[dp - 32, dp] = 1.0
    c["rmats"] = np.stack([RA, RB])
    i = np.arange(128)[:, None]
    j = np.arange(512)[None, :]
    am = np.zeros((20, 128, 512), np.float32)
    for ri, r in enumerate(range(-8, 12)):
        delta = r * 128 + i - j
        m = np.zeros_like(delta, dtype=np.float32)
        for dil in (1, 4, 16):
            m += ((delta % dil == 0) & (np.abs(delta) <= 64 * dil)).astype(np.float32)
        am[ri] = m
    c["amask"] = am
    c["ident"] = np.eye(128, dtype=np.float32)
    u = np.arange(64)[:, None]
    t = np.arange(64)[None, :]
    c["cmats"] = np.stack([(u <= t), (u >= t), np.ones((64, 64), bool)]).astype(np.float32)
    return c


def prep_params(inp):
    p = {}
    f = lambda a: np.ascontiguousarray(a, dtype=np.float32)
    p["n1g"] = f(inp["norm1_g"].reshape(DEPTH, 16, 128).transpose(0, 2, 1))
    p["n2g"] = f(inp["norm2_g"].reshape(DEPTH, 16, 128).transpose(0, 2, 1))
    p["gains"] = f(np.stack([inp["a_q_gain"], inp["a_k_gain"], inp["b_q_gain"], inp["b_k_gain"]], -1))
    p["convw"] = f(inp["c_conv_w"].reshape(DEPTH, 5, 8, 128).transpose(0, 3, 2, 1))
    p["convb"] = f(inp["c_conv_b"].reshape(DEPTH, 8, 128).transpose(0, 2, 1))
    p["gateb"] = f(inp["c_gate_b"].reshape(DEPTH, 1, 32))
    p["outg"] = f(inp["c_out_g"].reshape(DEPTH, 1, 1024))
    return p


WEIGHTS = [("w_in", D, IN_COLS), ("w_branch_a", 768, D), ("w_branch_b", 1024, D), ("w_branch_c", 1024, D),
           ("w_out", D, D), ("w_up", D, DFF), ("w_down", DFF, D)]


class Prog:
    def __init__(self, n_layers=DEPTH, debug=False, stop_after=None):
        self.nc = nc = bass.Bass("TRN2", target_bir_lowering=False)
        self.kb = kb = KB(nc)
        self.n_layers = n_layers
        self.debug = debug
        self.stop_after = stop_after
        ext = "ExternalInput"
        self.xT = kb.dram("xT", [D, S], F32, kind=ext)
        self.W = {n: kb.dram(n, [n_layers, r, c], F32, kind=ext) for n, r, c in WEIGHTS}
        self.P = {}
        for n, shp in (("n1g", [DEPTH, 128, 16]), ("n2g", [DEPTH, 128, 16]), ("gains", [DEPTH, 128, 4]),
                       ("convw", [DEPTH, 128, 8, 5]), ("convb", [DEPTH, 128, 8]), ("gateb", [DEPTH, 1, 32]),
                       ("outg", [DEPTH, 1, 1024])):
            self.P[n] = kb.dram(n, shp, F32, kind=ext)
        self.C = {}
        for n, shp in (("ropeA", [2, 128, S]), ("ropeB", [2, 128, S]), ("rmats", [2, 128, 128]),
                       ("amask", [20, 128, 512]), ("ident", [128, 128]), ("cmats", [3, 64, 64])):
            self.C[n] = kb.dram(n, shp, F32, kind=ext)
        self.outT = kb.dram("outT", [D, S], F32, kind="ExternalOutput")
        sk = "ExternalOutput" if debug else "Internal"
        self.WB = [{n: kb.dram("wb%d_%s" % (par, n), [r, c], BF16) for n, r, c in WEIGHTS} for par in range(2)]
        self.sc = {}
        for n, shp, dt in (("aqT", [768, S], BF16), ("akT", [768, S], BF16), ("bqT", [1024, S], BF16),
                           ("bkT", [256, S], BF16), ("cqkT", [1024, S], F32), ("gT", [3 * D, S], BF16),
                           ("av", [S, 768], BF16), ("bv", [S, 256], BF16), ("cv", [S, 1024], F32),
                           ("co", [S, 1024], F32), ("cg", [S, 32], F32), ("yT", [2816, S], BF16),
                           ("hT", [D, S], F32), ("cqT", [512, S], BF16), ("ckT", [512, S], BF16),
                           ("ck", [S, 512], BF16), ("hfw", [S, 1024], F32), ("hbw", [S, 1024], F32), ("xnT", [D, S], BF16)):
            self.sc[n] = kb.dram("sc_" + n, shp, dt, kind=sk)
        self.xbuf = [kb.dram("xbuf%d" % i, [D, S], F32) for i in range(2)]

    def cast_items(self, L):
        kb = self.kb
        items = []
        for n, r, c in WEIGHTS:
            dst = self.WB[L % 2][n]
            src = self.W[n]
            step = 256
            for r0 in range(0, r, step):
                items.append(lambda dst=dst, src=src, r0=r0, step=step: kb.dma("pool", dst.ap()[r0:r0 + step, :], src.ap()[L, r0:r0 + step, :], writes=[dst], waw=False))
        return items

    def cast_weights(self, L):
        for it in self.cast_items(L):
            it()

    def rmsnorm_T(self, src_ap_fn, ntok, gt, dstT, dst_tok0, xts, sqs, rs, psN, ones_f, sub=128):
        kb = self.kb
        for i in range(ntok // sub):
            xt = xts[i % len(xts)]
            src = src_ap_fn(i * sub, sub)
            kb.dma_multi([("sp", xt[:, 0:8, :], src[:, 0:8, :]), ("sp", xt[:, 8:16, :], src[:, 8:16, :])], writes=[xt])
            for kc in range(16):
                sq = sqs[kc % len(sqs)]
                kb.op("act", lambda e: e.activation(out=sq[:, 0:sub], in_=xt[:, kc, :], func=AF.Square), reads=[xt], writes=[sq])
                kb.op("pe", lambda e: e.matmul(psN[:, 0:sub], lhsT=ones_f[:], rhs=sq[:, 0:sub], start=(kc == 0), stop=(kc == 15)),
                      reads=[ones_f, sq], writes=[psN])
            kb.op("act", lambda e: e.activation(out=rs[:, 0:sub], in_=psN[:, 0:sub], func=AF.Ln, scale=1.0 / D, bias=self.eps_t[:, 0:1]),
                  reads=[psN, self.eps_t], writes=[rs])
            kb.op("act", lambda e: e.activation(out=rs[:, 0:sub], in_=rs[:, 0:sub], func=AF.Exp, scale=-0.5), reads=[rs], writes=[rs])
            c0 = dst_tok0 + i * sub
            for kc in range(16):
                kb.op("dve", lambda e: e.scalar_tensor_tensor(out=dstT[:, kc, c0:c0 + sub], in0=xt[:, kc, :], scalar=gt[:, kc:kc + 1],
                                                              in1=rs[:, 0:sub], op0=ALU.mult, op1=ALU.mult),
                      reads=[xt, gt, rs], writes=[dstT])

    def consts_phase(self):
        kb = self.kb
        self.eps_t = kb.sb("eps", [128, 1], F32)
        kb.op("dve", lambda e: e.memset(self.eps_t[:], EPS), writes=[self.eps_t])
        self.ones_f = kb.sb("ones_f", [128, 128], F32)
        kb.op("dve", lambda e: e.memset(self.ones_f[:], 1.0), writes=[self.ones_f])
        self.ones_b = kb.sb("ones_b", [128, 128], BF16)
        kb.op("dve", lambda e: e.memset(self.ones_b[:], 1.0), writes=[self.ones_b])
        self.ident_b = kb.sb("ident_b", [128, 128], BF16)
        kb.dma("pool", self.ident_b[:], self.C["ident"].ap()[:, :], writes=[self.ident_b])
        self.rmat = kb.sb("rmat", [128, 2, 128], BF16)
        kb.dma("pool", self.rmat[:], self.C["rmats"].ap().rearrange("r p c -> p r c"), writes=[self.rmat])


    def _qk_stages(self, ps, k, sg_, tsl, dsl, dst, sqs3, rss3, qnbs, t1s, t2s, obs, psRa, psRb, gains):
        kb = self.kb
        sq, rs, qnb = sqs3[k % 3], rss3[k % 3], qnbs[k % 3]
        t1, t2, ob = t1s[k % 2], t2s[k % 2], obs[k % 3]
        pa, pb = psRa[k % 2], psRb[k % 2]
        rope = sg_["rope"]
        gi = sg_["gi"]
        rm = sg_["rm"]

        def e1():
            kb.op("act", lambda e: e.activation(out=sq[:], in_=ps[:], func=AF.Square), reads=[ps], writes=[sq])

        def e2():
            kb.op("pe", lambda e: e.matmul(pa[:], lhsT=self.ones_f[:], rhs=sq[:], start=True, stop=True), reads=[self.ones_f, sq], writes=[pa])
            kb.op("act", lambda e: e.activation(out=rs[:], in_=pa[:], func=AF.Ln, scale=1.0 / 128, bias=self.eps_t[:, 0:1]), reads=[pa, self.eps_t], writes=[rs])
            kb.op("act", lambda e: e.activation(out=rs[:], in_=rs[:], func=AF.Exp, scale=-0.5), reads=[rs], writes=[rs])
            kb.op("dve", lambda e: e.scalar_tensor_tensor(out=qnb[:], in0=ps[:], scalar=gains[:, gi:gi + 1], in1=rs[:], op0=ALU.mult, op1=ALU.mult),
                  reads=[ps, gains, rs], writes=[qnb])

        def e3():
            kb.op("pe", lambda e: e.matmul(pb[:], lhsT=self.rmat[:, rm, :], rhs=qnb[:], start=True, stop=True), reads=[self.rmat, qnb], writes=[pb])
            kb.op("dve", lambda e: e.tensor_tensor(out=t1[:], in0=qnb[:], in1=rope[:, 0, tsl], op=ALU.mult), reads=[qnb, rope], writes=[t1])
            kb.op("dve", lambda e: e.tensor_tensor(out=t2[:], in0=pb[:], in1=rope[:, 1, tsl], op=ALU.mult), reads=[pb, rope], writes=[t2])
            kb.op("pool", lambda e: e.tensor_tensor(out=ob[:], in0=t1[:], in1=t2[:], op=ALU.add), reads=[t1, t2], writes=[ob])
            kb.dma("pool", dsl, ob[:], reads=[ob], writes=[dst], waw=False)

        return [e1, e2, e3]

    def p1(self, L, xin, sg):
        kb, sc = self.kb, self.sc
        NT = 2048
        T0 = sg * NT
        WB = self.WB[L % 2]["w_in"]
        with kb.phase():
            xnT = kb.sb("xnT", [128, 16, NT], BF16)
            xts = [kb.sb("xt%d" % i, [128, 16, 128], F32) for i in range(2)]
            sqs = [kb.sb("sq%d" % i, [128, 512], F32) for i in range(2)]
            rss = [kb.sb("rs%d" % i, [128, 512], F32) for i in range(2)]
            wts = [kb.sb("wt%d" % i, [128, 16, 512], BF16) for i in range(2)]
            g1 = kb.sb("g1", [128, 16], F32)
            gains = kb.sb("gains", [128, 4], F32)
            ropeA = kb.sb("ropeA", [128, 2, NT], F32)
            ropeB = kb.sb("ropeB", [128, 2, NT], F32)
            qnbs = [kb.sb("qnb%d" % i, [128, 512], BF16) for i in range(3)]
            sqs3 = sqs + [kb.sb("sq2", [128, 512], F32)]
            rss3 = rss + [kb.sb("rs2", [128, 512], F32)]
            t1s = [kb.sb("t1%d" % i, [128, 512], F32) for i in range(2)]
            t2s = [kb.sb("t2%d" % i, [128, 512], F32) for i in range(2)]
            obs = [kb.sb("ob%d" % i, [128, 512], BF16) for i in range(3)]
            ofs = [kb.sb("of%d" % i, [128, 512], F32) for i in range(3)]
            psX = [kb.ps("psX%d" % i, [128, 512]) for i in range(3)]
            psRa = [kb.ps("psRa%d" % i, [128, 512]) for i in range(2)]
            psRb = [kb.ps("psRb%d" % i, [128, 512]) for i in range(2)]
            psN = kb.ps("psN", [128, 512])
            pend = []
            qcnt = 0
            kb.dma("sp", g1[:], self.P["n1g"].ap()[L], writes=[g1])
            kb.dma("sp", gains[:], self.P["gains"].ap()[L], writes=[gains])
            for j in range(2):
                kb.dma("sp", ropeA[:, j, :], self.C["ropeA"].ap()[j, :, T0:T0 + NT], writes=[ropeA])
                kb.dma("sp", ropeB[:, j, :], self.C["ropeB"].ap()[j, :, T0:T0 + NT], writes=[ropeB])
            xv = xin.ap().rearrange("(kc p) t -> p kc t", p=128)
            if L == 0:
                self.rmsnorm_T(lambda t0, n: xv[:, :, T0 + t0:T0 + t0 + n], NT, g1, xnT, 0, xts, sqs, rss[0], psN, self.ones_f)
            else:
                xnv = sc["xnT"].ap().rearrange("(kc p) t -> p kc t", p=128)
                kb.dma_multi([("sp", xnT[:, k0:k0 + 4, :], xnv[:, k0:k0 + 4, T0:T0 + NT]) for k0 in range(0, 16, 4)], reads=[sc["xnT"]], writes=[xnT])

            segs = []
            def add(col0, n, kind, dst, drow0=0, **kw):
                c = 0
                while c < n:
                    w = min(512, n - c)
                    segs.append(dict(col0=col0 + c, n=w, kind=kind, dst=dst, d0=drow0 + c, **kw))
                    c += w
            add(O_AQ, 768, "qk", sc["aqT"], gi=0, rope=ropeA, rm=0)
            add(O_AK, 768, "qk", sc["akT"], gi=1, rope=ropeA, rm=0)
            add(O_BQ, 1024, "qk", sc["bqT"], gi=2, rope=ropeB, rm=1)
            add(O_BK, 256, "qk", sc["bkT"], gi=3, rope=ropeB, rm=1)
            add(O_CQK, 1024, "copyT", sc["cqkT"])
            add(O_G, 3 * D, "sigT", sc["gT"])
            add(O_AV, 768, "tok_bf", sc["av"])
            add(O_BV, 256, "tok_bf", sc["bv"])
            add(O_CV, 1024, "tok_f", sc["cv"])
            add(O_CO, 1024, "tok_sig", sc["co"])
            add(O_CG, 32, "tok_f", sc["cg"])
            wv = WB.ap().rearrange("(kc p) c -> p kc c", p=128)
            cnt = 0
            ocnt = 0
            for si, sg_ in enumerate(segs):
                wt = wts[si % 2]
                n = sg_["n"]
                c0 = sg_["col0"]
                kb.dma_multi([("sp", wt[:, 0:8, 0:n], wv[:, 0:8, c0:c0 + n]), ("sp", wt[:, 8:16, 0:n], wv[:, 8:16, c0:c0 + n])], reads=[WB], writes=[wt])
                kind = sg_["kind"]
                dst = sg_["dst"]
                if kind in ("qk", "copyT", "sigT"):
                    for cc in range(n // 128):
                        r0 = sg_["d0"] + cc * 128
                        for tg in range(NT // 512):
                            ps = psX[cnt % 3]
                            cnt += 1
                            for kc in range(16):
                                kb.op("pe", lambda e: e.matmul(ps[:, :], lhsT=wt[:, kc, cc * 128:(cc + 1) * 128], rhs=xnT[:, kc, tg * 512:(tg + 1) * 512],
                                                               start=(kc == 0), stop=(kc == 15)), reads=[wt, xnT], writes=[ps], track=(kc == 15))
                            tsl = slice(tg * 512, (tg + 1) * 512)
                            dsl = dst.ap()[r0:r0 + 128, T0 + tg * 512:T0 + (tg + 1) * 512]
                            if kind == "sigT":
                                ob = obs[ocnt % 3]
                                ocnt += 1
                                kb.op("act", lambda e: e.activation(out=ob[:], in_=ps[:], func=AF.Sigmoid), reads=[ps], writes=[ob])
                                kb.dma("pool", dsl, ob[:], reads=[ob], writes=[dst], waw=False)
                            elif kind == "copyT":
                                of = ofs[ocnt % 3]
                                ocnt += 1
                                kb.op("dve", lambda e: e.tensor_copy(out=of[:], in_=ps[:]), reads=[ps], writes=[of])
                                kb.dma("pool", dsl, of[:], reads=[of], writes=[dst], waw=False)
                            else:
                                pend.append(self._qk_stages(ps, qcnt, sg_, tsl, dsl, dst, sqs3, rss3, qnbs, t1s, t2s, obs, psRa, psRb, gains))
                                qcnt += 1
                            for st in list(pend):
                                st.pop(0)()
                                if not st:
                                    pend.remove(st)
                    while pend:
                        for st in list(pend):
                            st.pop(0)()
                            if not st:
                                pend.remove(st)
                else:
                    for tt in range(NT // 128):
                        ps = psX[cnt % 3]
                        cnt += 1
                        for kc in range(16):
                            kb.op("pe", lambda e: e.matmul(ps[:, 0:n], lhsT=xnT[:, kc, tt * 128:(tt + 1) * 128], rhs=wt[:, kc, 0:n],
                                                           start=(kc == 0), stop=(kc == 15)), reads=[wt, xnT], writes=[ps], track=(kc == 15))
                        dsl = dst.ap()[T0 + tt * 128:T0 + (tt + 1) * 128, sg_["d0"]:sg_["d0"] + n]
                        if kind == "tok_bf":
                            ob = obs[ocnt % 3]
                            ocnt += 1
                            kb.op("dve", lambda e: e.tensor_copy(out=ob[:, 0:n], in_=ps[:, 0:n]), reads=[ps], writes=[ob])
                            kb.dma("pool", dsl, ob[:, 0:n], reads=[ob], writes=[dst], waw=False)
                        else:
                            of = ofs[ocnt % 3]
                            ocnt += 1
                            if kind == "tok_sig":
                                kb.op("act", lambda e: e.activation(out=of[:, 0:n], in_=ps[:, 0:n], func=AF.Sigmoid), reads=[ps], writes=[of])
                            else:
                                kb.op("dve", lambda e: e.tensor_copy(out=of[:, 0:n], in_=ps[:, 0:n]), reads=[ps], writes=[of])
                            kb.dma("pool", dsl, of[:, 0:n], reads=[of], writes=[dst], waw=False)

    def p3a(self, L, xin):
        kb, sc = self.kb, self.sc
        WBs = self.WB[L % 2]
        br = [("w_branch_a", 0, 6), ("w_branch_b", 6, 8), ("w_branch_c", 14, 8)]
        with kb.phase():
            yTs = [kb.sb("yT%d" % i, [128, 22, 512], BF16) for i in range(2)]
            xts_ = [kb.sb("xt%d" % i, [128, 16, 512], F32) for i in range(2)]
            mT = kb.sb("mT", [128, 16, 512], BF16)
            wbs = [kb.sb("wb%d" % i, [128, 22, 256], BF16) for i in range(2)]
            wos = [kb.sb("wo%d" % i, [128, 16, 256], BF16) for i in range(2)]
            gts = [kb.sb("gt%d" % i, [128, 3, 512], BF16) for i in range(2)]
            ms = [kb.sb("m%d" % i, [128, 512], F32) for i in range(2)]
            tmps = [kb.sb("tmp%d" % i, [128, 512], F32) for i in range(2)]
            psX = [kb.ps("psX%d" % i, [128, 512]) for i in range(4)]
            yv = sc["yT"].ap().rearrange("(kc p) t -> p kc t", p=128)
            xv = xin.ap().rearrange("(kc p) t -> p kc t", p=128)
            hv = sc["hT"].ap().rearrange("(kc p) t -> p kc t", p=128)
            gv = sc["gT"].ap().rearrange("(b oc p) t -> p b oc t", p=128, b=3)
            wvs = [WBs[n].ap().rearrange("(kc p) c -> p kc c", p=128) for n, _, _ in br]
            wov = WBs["w_out"].ap().rearrange("(kc p) c -> p kc c", p=128)
            cnt = 0
            def ld(tg):
                tsl_ = slice(tg * 512, (tg + 1) * 512)
                yT_, xt_ = yTs[tg % 2], xts_[tg % 2]
                kb.dma_multi([("sp", yT_[:, 0:11, :], yv[:, 0:11, tsl_]), ("sp", yT_[:, 11:22, :], yv[:, 11:22, tsl_])], reads=[sc["yT"]], writes=[yT_])
                kb.dma_multi([("sp", xt_[:, 0:8, :], xv[:, 0:8, tsl_]), ("sp", xt_[:, 8:16, :], xv[:, 8:16, tsl_])], reads=[xin], writes=[xt_])

            ld(0)
            for tg in range(S // 512):
                tsl = slice(tg * 512, (tg + 1) * 512)
                yT, xt = yTs[tg % 2], xts_[tg % 2]
                for o2 in range(8):
                    if o2 == 2 and tg + 1 < S // 512:
                        ld(tg + 1)
                    wb = wbs[o2 % 2]
                    for (n, k0, nk), wv in zip(br, wvs):
                        kb.dma("sp", wb[:, k0:k0 + nk, :], wv[:, :, o2 * 256:(o2 + 1) * 256], reads=[WBs[n]], writes=[wb])
                    for oi in range(2):
                        oc = o2 * 2 + oi
                        gt = gts[oc % 2]
                        kb.dma("sp", gt[:], gv[:, :, oc, tsl], reads=[sc["gT"]], writes=[gt])
                        m = ms[oc % 2]
                        for bi, (n, k0, nk) in enumerate(br):
                            ps = psX[cnt % 4]
                            cnt += 1
                            for kc in range(nk):
                                kb.op("pe", lambda e: e.matmul(ps[:], lhsT=wb[:, k0 + kc, oi * 128:(oi + 1) * 128], rhs=yT[:, k0 + kc, :],
                                                               start=(kc == 0), stop=(kc == nk - 1)), reads=[wb, yT], writes=[ps], track=(kc == nk - 1))
                            if bi == 0:
                                kb.op("dve", lambda e: e.tensor_tensor(out=m[:], in0=ps[:], in1=gt[:, 0, :], op=ALU.mult), reads=[ps, gt], writes=[m])
                            else:
                                tmp = tmps[bi % 2]
                                kb.op("dve", lambda e: e.tensor_tensor(out=tmp[:], in0=ps[:], in1=gt[:, bi, :], op=ALU.mult), reads=[ps, gt], writes=[tmp])
                                if bi == 1:
                                    kb.op("pool", lambda e: e.tensor_tensor(out=m[:], in0=m[:], in1=tmp[:], op=ALU.add), reads=[m, tmp], writes=[m])
                                else:
                                    kb.op("pool", lambda e: e.tensor_tensor(out=mT[:, oc, :], in0=m[:], in1=tmp[:], op=ALU.add), reads=[m, tmp], writes=[mT])
                for o2 in range(8):
                    wo = wos[o2 % 2]
                    kb.dma("sp", wo[:], wov[:, :, o2 * 256:(o2 + 1) * 256], reads=[WBs["w_out"]], writes=[wo])
                    for oi in range(2):
                        oc = o2 * 2 + oi
                        ps = psX[cnt % 4]
                        cnt += 1
                        for kc in range(16):
                            kb.op("pe", lambda e: e.matmul(ps[:], lhsT=wo[:, kc, oi * 128:(oi + 1) * 128], rhs=mT[:, kc, :],
                                                           start=(kc == 0), stop=(kc == 15)), reads=[wo, mT], writes=[ps], track=(kc == 15))
                        kb.op("dve", lambda e: e.tensor_tensor(out=xt[:, oc, :], in0=ps[:], in1=xt[:, oc, :], op=ALU.add), reads=[ps, xt], writes=[xt])
                kb.dma("pool", hv[:, 0:8, tsl], xt[:, 0:8, :], reads=[xt], writes=[sc["hT"]], waw=False)
                kb.dma("pool", hv[:, 8:16, tsl], xt[:, 8:16, :], reads=[xt], writes=[sc["hT"]], waw=False)

    def p3b(self, L, xout):
        kb, sc = self.kb, self.sc
        WBs = self.WB[L % 2]
        nxt = (L + 1 < self.n_layers)
        with kb.phase():
            if nxt:
                g1n = kb.sb("g1n", [128, 16], F32)
                kb.dma("sp", g1n[:], self.P["n1g"].ap()[L + 1], writes=[g1n])
                xns = [kb.sb("xn%d" % i, [128, 512], BF16) for i in range(3)]
                xnv = sc["xnT"].ap().rearrange("(kc p) t -> p kc t", p=128)
            hT = kb.sb("hT", [128, 16, 512], F32)
            hnT = kb.sb("hnT", [128, 16, 512], BF16)
            uT = kb.sb("uT", [128, 64, 512], BF16)
            sqs = [kb.sb("sq%d" % i, [128, 512], F32) for i in range(2)]
            rs = kb.sb("rs", [128, 512], F32)
            g2 = kb.sb("g2", [128, 16], F32)
            wus = [kb.sb("wu%d" % i, [128, 16, 256], BF16) for i in range(2)]
            wds = [kb.sb("wd%d" % i, [128, 64, 128], BF16) for i in range(2)]
            rls = [kb.sb("rl%d" % i, [128, 512], F32) for i in range(2)]
            psX = [kb.ps("psX%d" % i, [128, 512]) for i in range(4)]
            psN = kb.ps("psN", [128, 512])
            kb.dma("sp", g2[:], self.P["n2g"].ap()[L], writes=[g2])
            hv = sc["hT"].ap().rearrange("(kc p) t -> p kc t", p=128)
            ov = xout.ap().rearrange("(kc p) t -> p kc t", p=128)
            wuv = WBs["w_up"].ap().rearrange("(kc p) c -> p kc c", p=128)
            wdv = WBs["w_down"].ap().rearrange("(fc p) c -> p fc c", p=128)
            cnt = 0
            for tg in range(S // 512):
                tsl = slice(tg * 512, (tg + 1) * 512)
                kb.dma_multi([("sp", hT[:, 0:8, :], hv[:, 0:8, tsl]), ("sp", hT[:, 8:16, :], hv[:, 8:16, tsl])], reads=[sc["hT"]], writes=[hT])
                for kc in range(16):
                    sq = sqs[kc % 2]
                    kb.op("act", lambda e: e.activation(out=sq[:], in_=hT[:, kc, :], func=AF.Square), reads=[hT], writes=[sq])
                    kb.op("pe", lambda e: e.matmul(psN[:], lhsT=self.ones_f[:], rhs=sq[:], start=(kc == 0), stop=(kc == 15)),
                          reads=[self.ones_f, sq], writes=[psN])
                kb.op("act", lambda e: e.activation(out=rs[:], in_=psN[:], func=AF.Ln, scale=1.0 / D, bias=self.eps_t[:, 0:1]), reads=[psN, self.eps_t], writes=[rs])
                kb.op("act", lambda e: e.activation(out=rs[:], in_=rs[:], func=AF.Exp, scale=-0.5), reads=[rs], writes=[rs])
                for kc in range(16):
                    kb.op("dve", lambda e: e.scalar_tensor_tensor(out=hnT[:, kc, :], in0=hT[:, kc, :], scalar=g2[:, kc:kc + 1], in1=rs[:],
                                                                  op0=ALU.mult, op1=ALU.mult), reads=[hT, g2, rs], writes=[hnT])
                for f2 in range(32):
                    wu = wus[f2 % 2]
                    kb.dma("sp", wu[:], wuv[:, :, f2 * 256:(f2 + 1) * 256], reads=[WBs["w_up"]], writes=[wu])
                    for fi in range(2):
                        fc = f2 * 2 + fi
                        ps = psX[cnt % 4]
                        cnt += 1
                        for kc in range(16):
                            kb.op("pe", lambda e: e.matmul(ps[:], lhsT=wu[:, kc, fi * 128:(fi + 1) * 128], rhs=hnT[:, kc, :],
                                                           start=(kc == 0), stop=(kc == 15)), reads=[wu, hnT], writes=[ps], track=(kc == 15))
                        rl = rls[fc % 2]
                        kb.op("act", lambda e: e.activation(out=rl[:], in_=ps[:], func=AF.Relu), reads=[ps], writes=[rl])
                        kb.op("pool" if fc % 2 else "dve", lambda e: e.tensor_tensor(out=uT[:, fc, :], in0=rl[:], in1=rl[:], op=ALU.mult), reads=[rl], writes=[uT])
                for oc in range(16):
                    wd = wds[oc % 2]
                    kb.dma_multi([("sp", wd[:, 0:32, :], wdv[:, 0:32, oc * 128:(oc + 1) * 128]), ("sp", wd[:, 32:64, :], wdv[:, 32:64, oc * 128:(oc + 1) * 128])],
                                 reads=[WBs["w_down"]], writes=[wd])
                    ps = psX[cnt % 4]
                    cnt += 1
                    for fc in range(64):
                        kb.op("pe", lambda e: e.matmul(ps[:], lhsT=wd[:, fc, :], rhs=uT[:, fc, :], start=(fc == 0), stop=(fc == 63)),
                              reads=[wd, uT], writes=[ps], track=(fc == 63))
                    kb.op("dve", lambda e: e.tensor_tensor(out=hT[:, oc, :], in0=ps[:], in1=hT[:, oc, :], op=ALU.add), reads=[ps, hT], writes=[hT])
                kb.dma("pool", ov[:, 0:8, tsl], hT[:, 0:8, :], reads=[hT], writes=[xout], waw=False)
                kb.dma("pool", ov[:, 8:16, tsl], hT[:, 8:16, :], reads=[hT], writes=[xout], waw=False)
                if nxt:
                    for kc in range(16):
                        sq = sqs[kc % 2]
                        kb.op("act", lambda e: e.activation(out=sq[:], in_=hT[:, kc, :], func=AF.Square), reads=[hT], writes=[sq])
                        kb.op("pe", lambda e: e.matmul(psN[:], lhsT=self.ones_f[:], rhs=sq[:], start=(kc == 0), stop=(kc == 15)),
                              reads=[self.ones_f, sq], writes=[psN])
                    kb.op("act", lambda e: e.activation(out=rs[:], in_=psN[:], func=AF.Ln, scale=1.0 / D, bias=self.eps_t[:, 0:1]), reads=[psN, self.eps_t], writes=[rs])
                    kb.op("act", lambda e: e.activation(out=rs[:], in_=rs[:], func=AF.Exp, scale=-0.5), reads=[rs], writes=[rs])
                    for kc in range(16):
                        xn = xns[kc % 3]
                        kb.op("dve", lambda e: e.scalar_tensor_tensor(out=xn[:], in0=hT[:, kc, :], scalar=g1n[:, kc:kc + 1], in1=rs[:],
                                                                                          op0=ALU.mult, op1=ALU.mult), reads=[hT, g1n, rs], writes=[xn])
                        kb.dma("pool", xnv[:, kc, tsl], xn[:], reads=[xn], writes=[sc["xnT"]], waw=False)

    def attn(self, L):
        kb, sc = self.kb, self.sc
        if not hasattr(self, "pending_casts"):
            self.pending_casts = []
        SKEW = 2
        scale = 1.0 / np.sqrt(128.0)
        with kb.phase():
            KTs = [kb.sb("KT%d" % i, [128, S], BF16) for i in range(2)]
            Vs = [kb.sb("V%d" % i, [128, 32, 128], BF16) for i in range(2)]
            QTs = [kb.sb("QT%d" % i, [128, S], BF16) for i in range(2)]
            am = kb.sb("am", [128, 20, 512], BF16)
            PTs = [kb.sb("PT%d" % i, [128, 512], BF16) for i in range(4)]
            yos = [kb.sb("yo%d" % i, [128, 512], BF16) for i in range(2)]
            rzs = [kb.sb("rz%d" % i, [128, 512], F32) for i in range(2)]
            psS = [kb.ps("psS%d" % i, [128, 512]) for i in range(3)]
            psO = [kb.ps("psO%d" % i, [128, 512]) for i in range(2)]
            psZ = [kb.ps("psZ%d" % i, [128, 512]) for i in range(2)]
            amv = self.C["amask"].ap().rearrange("r p c -> p r c")
            for r0 in range(0, 20, 5):
                kb.dma("pool", am[:, r0:r0 + 5, :], amv[:, r0:r0 + 5, :], writes=[am])
            jobs = []
            for h in range(6):
                jobs.append(("A", sc["aqT"], h * 128, sc["akT"], h * 128, sc["av"], h * 128, h * 128, ("A", h)))
            for h in range(8):
                kvh = h // 4
                jobs.append(("B", sc["bqT"], h * 128, sc["bkT"], kvh * 128, sc["bv"], kvh * 128, 768 + h * 128, ("B", kvh)))
            cur_kv = None
            nkv = 0
            g = 0
            for ji, (kind, qsrc, q0, ksrc, k0, vsrc, v0, y0, kvid) in enumerate(jobs):
                QT = QTs[ji % 2]
                kb.dma("sp", QT[:], qsrc.ap()[q0:q0 + 128, :], reads=[qsrc], writes=[QT])
                if kvid != cur_kv:
                    cur_kv = kvid
                    KT = KTs[nkv % 2]
                    V = Vs[nkv % 2]
                    nkv += 1
                    kb.dma("sp", KT[:], ksrc.ap()[k0:k0 + 128, :], reads=[ksrc], writes=[KT])
                    vv = vsrc.ap()[:, v0:v0 + 128].rearrange("(kc p) d -> p kc d", p=128)
                    for c0 in range(0, 32, 8):
                        kb.dma("sp", V[:, c0:c0 + 8, :], vv[:, c0:c0 + 8, :], reads=[vsrc], writes=[V])
                for qg in range(8):
                    kcs = list(range(32)) if kind == "B" else list(range(max(0, 4 * qg - 8), min(32, 4 * qg + 12)))
                    pso, psz = psO[g % 2], psZ[g % 2]
                    n = len(kcs)
                    for i in range(n + SKEW):
                        if i < n:
                            kc = kcs[i]
                            ps = psS[i % 3]
                            PT = PTs[i % 4]
                            kb.op("pe", lambda e: e.matmul(ps[:], lhsT=KT[:, kc * 128:(kc + 1) * 128], rhs=QT[:, qg * 512:(qg + 1) * 512], start=True, stop=True),
                                  reads=[KT, QT], writes=[ps])
                            kb.op("act", lambda e: e.activation(out=PT[:], in_=ps[:], func=AF.Exp, scale=float(scale)), reads=[ps], writes=[PT])
                            if kind == "A":
                                r = kc - 4 * qg + 8
                                kb.op("dve", lambda e: e.tensor_tensor(out=PT[:], in0=PT[:], in1=am[:, r, :], op=ALU.mult), reads=[PT, am], writes=[PT])
                        if i >= SKEW:
                            j = i - SKEW
                            kc = kcs[j]
                            PT = PTs[j % 4]
                            kb.op("pe", lambda e: e.matmul(pso[:], lhsT=V[:, kc, :], rhs=PT[:], start=(j == 0), stop=(j == n - 1)),
                                  reads=[V, PT], writes=[pso], track=(j == n - 1))
                            kb.op("pe", lambda e: e.matmul(psz[:], lhsT=self.ones_b[:], rhs=PT[:], start=(j == 0), stop=(j == n - 1)),
                                  reads=[self.ones_b, PT], writes=[psz], track=True)
                    rz, yo = rzs[g % 2], yos[g % 2]
                    kb.op("dve", lambda e: e.reciprocal(out=rz[:], in_=psz[:]), reads=[psz], writes=[rz])
                    kb.op("dve", lambda e: e.tensor_tensor(out=yo[:], in0=pso[:], in1=rz[:], op=ALU.mult), reads=[pso, rz], writes=[yo])
                    kb.dma("pool", sc["yT"].ap()[y0:y0 + 128, qg * 512:(qg + 1) * 512], yo[:], reads=[yo], writes=[sc["yT"]], waw=False)
                    g += 1
                    if self.pending_casts:
                        self.pending_casts.pop(0)()

    def mlstm(self, L):
        kb, sc = self.kb, self.sc
        NCH = S // 64
        with kb.phase():
            Wt = kb.sb("Wt", [64, 2, NCH, 8], F32)
            EBt = kb.sb("EBt", [64, 2, NCH, 8], F32)
            EBTt = kb.sb("EBTt", [64, 2, NCH, 8], F32)
            W2t = kb.sb("W2t", [64, 2, NCH, 8], F32)
            Wb = kb.sb("Wb", [64, 2, NCH, 8], BF16)
            W2b = kb.sb("W2b", [64, 2, NCH, 8], BF16)
            cm = kb.sb("cm", [64, 3, 64], F32)
            trib = kb.sb("trib", [64, 2, 64], BF16)
            outg = kb.sb("outg", [64, 1024], F32)
            kb.dma("sp", cm[:], self.C["cmats"].ap().rearrange("r p c -> p r c"), writes=[cm])
            kb.dma("pool", trib[:], self.C["cmats"].ap()[0:2].rearrange("r p c -> p r c"), writes=[trib])
            kb.dma("sp", outg[:], self.P["outg"].ap()[L].partition_broadcast(64), writes=[outg])

            with kb.phase():
                G = kb.sb("G", [64, NCH, 32], F32)
                gb = kb.sb("gb", [64, 32], F32)
                nlf = kb.sb("nlf", [64, 2, NCH, 8], F32)
                tmp = kb.sb("gtmp", [64, 2, NCH, 8], F32)
                psA = [kb.ps("psg%d" % i, [128, 512]) for i in range(2)]
                psB = [kb.ps("psh%d" % i, [128, 512]) for i in range(2)]
                cgv = sc["cg"].ap().rearrange("(c t) g -> t c g", t=64)
                for c0 in range(0, NCH, 16):
                    kb.dma("sp", G[:, c0:c0 + 16, :], cgv[:, c0:c0 + 16, :], reads=[sc["cg"]], writes=[G])
                kb.dma("sp", gb[:], self.P["gateb"].ap()[L].partition_broadcast(64), writes=[gb])
                kb.op("dve", lambda e: e.tensor_tensor(out=G[:], in0=G[:], in1=gb[:, :].unsqueeze(1).broadcast_to([64, NCH, 32]), op=ALU.add),
                      reads=[G, gb], writes=[G])
                for d in range(2):
                    fcol = 8 + 16 * d
                    kb.op("act", lambda e: e.activation(out=nlf[:, d], in_=G[:, :, fcol:fcol + 8], func=AF.Exp, scale=-1.0), reads=[G], writes=[nlf])
                    kb.op("act", lambda e: e.activation(out=nlf[:, d], in_=nlf[:, d], func=AF.Ln, bias=1.0), reads=[nlf], writes=[nlf])
                for d in range(2):
                    icol = 16 * d
                    rhs = nlf[:, d].rearrange("p c h -> p (c h)")
                    kb.op("pe", lambda e: e.matmul(psA[d][0:64, :], lhsT=cm[:, d, :], rhs=rhs, start=True, stop=True), reads=[cm, nlf], writes=[psA[d]])
                    kb.op("pe", lambda e: e.matmul(psB[d][0:64, :], lhsT=cm[:, 2, :], rhs=rhs, start=True, stop=True), reads=[cm, nlf], writes=[psB[d]])
                    csv = psA[d][0:64, :].rearrange("p (c h) -> p c h", h=8)
                    kb.op("dve", lambda e: e.tensor_tensor(out=tmp[:, d], in0=csv, in1=G[:, :, icol:icol + 8], op=ALU.add), reads=[psA[d], G], writes=[tmp])
                    kb.op("act", lambda e: e.activation(out=Wt[:, d], in_=tmp[:, d], func=AF.Exp), reads=[tmp], writes=[Wt])
                    kb.op("act", lambda e: e.activation(out=EBt[:, d], in_=csv, func=AF.Exp, scale=-1.0), reads=[psA[d]], writes=[EBt])
                    kb.op("act", lambda e: e.activation(out=EBTt[:, d], in_=psB[d][0:64, :].rearrange("p (c h) -> p c h", h=8), func=AF.Exp, scale=-1.0),
                          reads=[psB[d]], writes=[EBTt])
                    kb.op("dve", lambda e: e.tensor_tensor(out=W2t[:, d], in0=Wt[:, d], in1=EBTt[:, d], op=ALU.mult), reads=[Wt, EBTt], writes=[W2t])
                kb.op("dve", lambda e: e.tensor_copy(out=Wb[:], in_=Wt[:]), reads=[Wt], writes=[Wb])
                kb.op("dve", lambda e: e.tensor_copy(out=W2b[:], in_=W2t[:]), reads=[W2t], writes=[W2b])

            with kb.phase():
                xs = [kb.sb("cx%d" % i, [128, 8, 516], F32) for i in range(2)]
                cw = kb.sb("cw", [128, 8, 5], F32)
                cb = kb.sb("cb", [128, 8], F32)
                accs = [kb.sb("acc%d" % i, [128, 512], F32) for i in range(2)]
                qbs = [kb.sb("qb%d" % i, [128, 512], BF16) for i in range(3)]
                kfs = [kb.sb("kf%d" % i, [128, 512], F32) for i in range(2)]
                kfm = kb.sb("kfm", [128, 4, 512], BF16)
                kts = [kb.sb("kt%d" % i, [128, 512], BF16) for i in range(2)]
                psT = [kb.ps("psT%d" % i, [128, 1024], BF16) for i in range(2)]
                kb.dma("sp", cw[:], self.P["convw"].ap()[L], writes=[cw])
                kb.dma("sp", cb[:], self.P["convb"].ap()[L], writes=[cb])
                cv_ = sc["cqkT"].ap().rearrange("(c p) t -> p c t", p=128)
                for tg in range(8):
                    x = xs[tg % 2]
                    t0 = tg * 512
                    lo, hi = max(0, t0 - 2), min(S, t0 + 514)
                    if tg == 0:
                        kb.op("dve", lambda e: e.memset(x[:, :, 0:2], 0.0), writes=[x])
                    if tg == 7:
                        kb.op("dve", lambda e: e.memset(x[:, :, 514:516], 0.0), writes=[x])
                    b0 = lo - (t0 - 2)
                    kb.dma("sp", x[:, 0:4, b0:b0 + (hi - lo)], cv_[:, 0:4, lo:hi], reads=[sc["cqkT"]], writes=[x])
                    kb.dma("sp", x[:, 4:8, b0:b0 + (hi - lo)], cv_[:, 4:8, lo:hi], reads=[sc["cqkT"]], writes=[x])
                    for c in range(8):
                        acc = accs[c % 2]
                        kb.op("dve", lambda e: e.tensor_scalar(out=acc[:], in0=x[:, c, 0:512], scalar1=cw[:, c, 0:1], scalar2=None, op0=ALU.mult),
                              reads=[x, cw], writes=[acc])
                        for j in range(1, 5):
                            kb.op("dve", lambda e: e.scalar_tensor_tensor(out=acc[:], in0=x[:, c, j:j + 512], scalar=cw[:, c, j:j + 1], in1=acc[:],
                                                                          op0=ALU.mult, op1=ALU.add), reads=[x, cw, acc], writes=[acc])
                        if c < 4:
                            qb = qbs[c % 3]
                            kb.op("act", lambda e: e.activation(out=qb[:], in_=acc[:], func=AF.Silu, bias=cb[:, c:c + 1]), reads=[acc, cb], writes=[qb])
                            kb.dma("pool", sc["cqT"].ap()[c * 128:(c + 1) * 128, t0:t0 + 512], qb[:], reads=[qb], writes=[sc["cqT"]], waw=False)
                        else:
                            kf = kfs[c % 2]
                            kb.op("act", lambda e: e.activation(out=kf[:], in_=acc[:], func=AF.Silu, bias=cb[:, c:c + 1]), reads=[acc, cb], writes=[kf])
                            kb.op("pool", lambda e: e.tensor_scalar(out=kfm[:, c - 4, :], in0=kf[:], scalar1=0.125, scalar2=None, op0=ALU.mult),
                                  reads=[kf], writes=[kfm])
                    kb.dma("pool", sc["ckT"].ap().rearrange("(c p) t -> p c t", p=128)[:, :, t0:t0 + 512], kfm[:], reads=[kfm], writes=[sc["ckT"]], waw=False)
                    for tt in range(4):
                        pt = psT[tt % 2]
                        for c in range(4):
                            kb.op("pe", lambda e: e.transpose(out=pt[:, c * 128:(c + 1) * 128], in_=kfm[:, c, tt * 128:(tt + 1) * 128], identity=self.ident_b[:]),
                                  reads=[kfm, self.ident_b], writes=[pt])
                        kt = kts[tt % 2]
                        kb.op("act", lambda e: e.activation(out=kt[:], in_=pt[:, 0:512], func=AF.Copy), reads=[pt], writes=[kt])
                        kb.dma("pool", sc["ck"].ap()[t0 + tt * 128:t0 + (tt + 1) * 128, :], kt[:], reads=[kt], writes=[sc["ck"]], waw=False)

            with kb.phase():
                gens = [self._mlstm_dir(L, d, (Wt, EBt, EBTt, W2t, Wb, W2b, trib)) for d in range(2)]
                live = [True, True]
                while any(live):
                    for d in range(2):
                        if live[d]:
                            try:
                                next(gens[d])
                            except StopIteration:
                                live[d] = False
            with kb.phase():
                hfs = [kb.sb("hf%d" % i, [128, 1024], F32) for i in range(2)]
                hbs = [kb.sb("hb%d" % i, [128, 1024], F32) for i in range(2)]
                cos_ = [kb.sb("co%d" % i, [128, 1024], F32) for i in range(2)]
                og = kb.sb("og", [128, 1024], F32)
                junk = kb.sb("junk", [128, 1024], F32)
                ybs = [kb.sb("yb%d" % i, [128, 1024], BF16) for i in range(2)]
                yts = [kb.sb("yt%d" % i, [128, 8, 128], BF16) for i in range(2)]
                s0s = [kb.sb("s0%d" % i, [128, 8], F32) for i in range(2)]
                s1s = [kb.sb("s1%d" % i, [128, 8], F32) for i in range(2)]
                psT = [kb.ps("psT%d" % i, [128, 1024], BF16) for i in range(2)]
                kb.dma("sp", og[:], self.P["outg"].ap()[L].partition_broadcast(128), writes=[og])
                yv = sc["yT"].ap()[1792:2816, :].rearrange("(h p) t -> p h t", p=128)
                for tt in range(S // 128):
                    b = tt % 2
                    rsl = slice(tt * 128, (tt + 1) * 128)
                    hf, hb, co, yb, yt, s0, s1, pt = hfs[b], hbs[b], cos_[b], ybs[b], yts[b], s0s[b], s1s[b], psT[b]
                    kb.dma("sp", hf[:], sc["hfw"].ap()[rsl, :], reads=[sc["hfw"]], writes=[hf])
                    kb.dma("sp", hb[:], sc["hbw"].ap()[rsl, :], reads=[sc["hbw"]], writes=[hb])
                    kb.dma("sp", co[:], sc["co"].ap()[rsl, :], reads=[sc["co"]], writes=[co])
                    kb.op("pool", lambda e: e.tensor_tensor(out=hf[:], in0=hf[:], in1=hb[:], op=ALU.add), reads=[hf, hb], writes=[hf])
                    kb.op("act", lambda e: e.activation(out=junk[:], in_=hf[:], func=AF.Square), reads=[hf], writes=[junk])
                    kb.op("dve", lambda e: e.tensor_reduce(out=s0[:], in_=junk[:].rearrange("p (h v) -> p h v", h=8), axis=AX.X, op=ALU.add), reads=[junk], writes=[s0])
                    kb.op("act", lambda e: e.activation(out=s1[:], in_=s0[:], func=AF.Ln, scale=1.0 / 128, bias=self.eps_t[:, 0:1]), reads=[s0, self.eps_t], writes=[s1])
                    kb.op("act", lambda e: e.activation(out=s1[:], in_=s1[:], func=AF.Exp, scale=-0.5), reads=[s1], writes=[s1])
                    kb.op("pool", lambda e: e.tensor_tensor(out=co[:], in0=co[:], in1=og[:], op=ALU.mult), reads=[co, og], writes=[co])
                    kb.op("dve", lambda e: e.tensor_tensor(out=hf[:].rearrange("p (h v) -> p h v", h=8), in0=hf[:].rearrange("p (h v) -> p h v", h=8),
                                                           in1=s1[:, :].unsqueeze(2).broadcast_to([128, 8, 128]), op=ALU.mult), reads=[hf, s1], writes=[hf])
                    kb.op("pool", lambda e: e.tensor_tensor(out=yb[:], in0=hf[:], in1=co[:], op=ALU.mult), reads=[hf, co], writes=[yb])
                    for h in range(8):
                        kb.op("pe", lambda e: e.transpose(out=pt[:, h * 128:(h + 1) * 128], in_=yb[:, h * 128:(h + 1) * 128], identity=self.ident_b[:]),
                              reads=[yb, self.ident_b], writes=[pt], track=(h == 7))
                    kb.op("act", lambda e: e.activation(out=yt[:], in_=pt[:, :].rearrange("p (h t) -> p h t", h=8), func=AF.Copy), reads=[pt], writes=[yt])
                    kb.dma("pool", yv[:, :, rsl], yt[:], reads=[yt], writes=[sc["yT"]], waw=False)

    def _mlstm_dir(self, L, d, tl):
        kb, sc = self.kb, self.sc
        NCH = S // 64
        Wt, EBt, EBTt, W2t, Wb, W2b, trib = tl
        n_ = "f" if d == 0 else "b"
        QTc = [kb.sb("QTc%s%d" % (n_, i), [64, 8, 64], BF16) for i in range(2)]
        KTc = [kb.sb("KTc%s%d" % (n_, i), [64, 8, 64], BF16) for i in range(2)]
        Ktk = [kb.sb("Ktk%s%d" % (n_, i), [64, 512], BF16) for i in range(2)]
        vcs = [kb.sb("vc%s%d" % (n_, i), [64, 8, 128], F32) for i in range(2)]
        ATs = [kb.sb("AT%s%d" % (n_, i), [64, 8, 64], BF16) for i in range(2)]
        v1s = [kb.sb("v1%s%d" % (n_, i), [64, 8, 128], BF16) for i in range(2)]
        v2s = [kb.sb("v2%s%d" % (n_, i), [64, 8, 128], BF16) for i in range(2)]
        Cst = kb.sb("Cst" + n_, [64, 8, 128], F32)
        nst = kb.sb("nst" + n_, [64, 8], F32)
        Cbfs = [kb.sb("Cbf%s%d" % (n_, i), [64, 8, 128], BF16) for i in range(2)]
        nbfs = [kb.sb("nbf%s%d" % (n_, i), [64, 8], BF16) for i in range(2)]
        sm = [kb.sb("sm%s%d" % (n_, i), [64, 8], F32) for i in range(4)]
        hds = [kb.sb("hd%s%d" % (n_, i), [64, 8, 128], F32) for i in range(2)]
        psS = kb.ps("mpsS" + n_, [128, 512])
        psN = kb.ps("mpsN" + n_, [128, 512])
        psU = kb.ps("mpsU" + n_, [128, 512])
        psD = kb.ps("mpsD" + n_, [128, 512])
        hdst = sc["hfw"] if d == 0 else sc["hbw"]
        e_v = "pool"
        kb.op("dve", lambda e: e.memset(Cst[:], 0.0), writes=[Cst])
        kb.op("dve", lambda e: e.memset(nst[:], 0.0), writes=[nst])
        kb.op("dve", lambda e: e.memset(Cbfs[0][:], 0.0), writes=[Cbfs[0]])
        kb.op("dve", lambda e: e.memset(nbfs[0][:], 0.0), writes=[nbfs[0]])
        qv = sc["cqT"].ap().rearrange("(h dd) t -> dd h t", dd=64)
        kv = sc["ckT"].ap().rearrange("(h dd) t -> dd h t", dd=64)
        order = list(range(NCH)) if d == 0 else list(range(NCH - 1, -1, -1))

        def loads(it):
            c = order[it]
            b = it % 2
            tsl = slice(c * 64, (c + 1) * 64)
            kb.dma("sp", QTc[b][:], qv[:, :, tsl], reads=[sc["cqT"]], writes=[QTc[b]])
            kb.dma("sp", KTc[b][:], kv[:, :, tsl], reads=[sc["ckT"]], writes=[KTc[b]])
            kb.dma("sp", Ktk[b][:], sc["ck"].ap()[tsl, :], reads=[sc["ck"]], writes=[Ktk[b]])
            kb.dma("sp", vcs[b][:], sc["cv"].ap()[tsl, :].rearrange("t (h v) -> t h v", h=8), reads=[sc["cv"]], writes=[vcs[b]])

        loads(0)
        for it, c in enumerate(order):
            b = it % 2
            tsl = slice(c * 64, (c + 1) * 64)
            Q, K, Kt, vc, AT, v1, v2 = QTc[b], KTc[b], Ktk[b], vcs[b], ATs[b], v1s[b], v2s[b]
            Cbf, nbf, Cbn, nbn = Cbfs[b], nbfs[b], Cbfs[1 - b], nbfs[1 - b]
            for h in range(8):
                kb.op("pe", lambda e: e.matmul(psS[0:64, h * 64:(h + 1) * 64], lhsT=K[:, h, :], rhs=Q[:, h, :], start=True, stop=True),
                      reads=[K, Q], writes=[psS], track=(h == 7))
            if it + 1 < NCH:
                loads(it + 1)
            yield
            kb.op("dve", lambda e: e.tensor_tensor(out=AT[:], in0=psS[0:64, :].rearrange("p (h t) -> p h t", h=8),
                                                   in1=trib[:, d, :].unsqueeze(1).broadcast_to([64, 8, 64]), op=ALU.mult), reads=[psS, trib], writes=[AT])
            kb.op(e_v, lambda e: e.tensor_tensor(out=v1[:], in0=vc[:], in1=Wt[:, d, c, :].unsqueeze(2).broadcast_to([64, 8, 128]), op=ALU.mult),
                  reads=[vc, Wt], writes=[v1])
            kb.op(e_v, lambda e: e.tensor_tensor(out=v2[:], in0=vc[:], in1=W2t[:, d, c, :].unsqueeze(2).broadcast_to([64, 8, 128]), op=ALU.mult),
                  reads=[vc, W2t], writes=[v2])
            yield
            eb = EBt[:, d, c, :]
            ebt = EBTt[:, d, c, :]
            hd = hds[b]
            s0, s1, s2, s3 = sm
            for h in range(8):
                kb.op("pe", lambda e: e.matmul(psD[0:64, h:h + 1], lhsT=AT[:, h, :], rhs=Wb[:, d, c, h:h + 1], start=True, stop=False),
                      reads=[AT, Wb], writes=[psD], track=False)
                kb.op("pe", lambda e: e.matmul(psD[0:64, h:h + 1], lhsT=Q[:, h, :], rhs=nbf[:, h:h + 1], start=False, stop=True),
                      reads=[Q, nbf], writes=[psD], track=(h == 7))
            kb.op("dve", lambda e: e.tensor_tensor(out=s0[:], in0=psD[0:64, 0:8], in1=eb, op=ALU.mult), reads=[psD, EBt], writes=[s0])
            kb.op("dve", lambda e: e.scalar_tensor_tensor(out=s1[:], in0=s0[:], scalar=-1.0, in1=s0[:], op0=ALU.mult, op1=ALU.max), reads=[s0], writes=[s1])
            kb.op("dve", lambda e: e.tensor_scalar(out=s1[:], in0=s1[:], scalar1=1.0, scalar2=None, op0=ALU.max), reads=[s1], writes=[s1])
            kb.op("dve", lambda e: e.reciprocal(out=s2[:], in_=s1[:]), reads=[s1], writes=[s2])
            kb.op("dve", lambda e: e.tensor_tensor(out=s3[:], in0=s2[:], in1=eb, op=ALU.mult), reads=[s2, EBt], writes=[s3])
            for hf_ in range(2):
                h0 = hf_ * 4
                hs = slice(h0, h0 + 4)
                for hh in range(4):
                    h = h0 + hh
                    kb.op("pe", lambda e: e.matmul(psN[0:64, hh * 128:(hh + 1) * 128], lhsT=AT[:, h, :], rhs=v1[:, h, :], start=True, stop=False),
                          reads=[AT, v1], writes=[psN], track=False)
                    kb.op("pe", lambda e: e.matmul(psN[0:64, hh * 128:(hh + 1) * 128], lhsT=Q[:, h, :], rhs=Cbf[:, h, :], start=False, stop=True),
                          reads=[Q, Cbf], writes=[psN], track=(hh == 3))
                for hh in range(4):
                    h = h0 + hh
                    kb.op("pe", lambda e: e.matmul(psU[0:64, hh * 128:(hh + 1) * 128], lhsT=Kt[:, h * 64:(h + 1) * 64], rhs=v2[:, h, :], start=True, stop=True),
                          reads=[Kt, v2], writes=[psU], track=False)
                    kb.op("pe", lambda e: e.matmul(psD[0:64, 8 + h:9 + h], lhsT=Kt[:, h * 64:(h + 1) * 64], rhs=W2b[:, d, c, h:h + 1], start=True, stop=True),
                          reads=[Kt, W2b], writes=[psD], track=(hh == 3))
                yield
                if hf_ == 0:
                    kb.op("pool", lambda e: e.tensor_tensor(out=Cst[:], in0=Cst[:], in1=ebt.unsqueeze(2).broadcast_to([64, 8, 128]), op=ALU.mult),
                          reads=[Cst, EBTt], writes=[Cst])
                    kb.op("pool", lambda e: e.tensor_tensor(out=nst[:], in0=nst[:], in1=ebt, op=ALU.mult), reads=[nst, EBTt], writes=[nst])
                kb.op("dve", lambda e: e.tensor_tensor(out=Cst[:, hs, :], in0=Cst[:, hs, :], in1=psU[0:64, :].rearrange("p (h v) -> p h v", h=4), op=ALU.add),
                      reads=[Cst, psU], writes=[Cst])
                kb.op("dve", lambda e: e.tensor_tensor(out=nst[:, hs], in0=nst[:, hs], in1=psD[0:64, 8 + h0:12 + h0], op=ALU.add), reads=[nst, psD], writes=[nst])
                kb.op("act", lambda e: e.activation(out=Cbn[:, hs, :], in_=Cst[:, hs, :], func=AF.Copy), reads=[Cst], writes=[Cbn])
                kb.op("act", lambda e: e.activation(out=nbn[:, hs], in_=nst[:, hs], func=AF.Copy), reads=[nst], writes=[nbn])
                for hh in range(4):
                    h = h0 + hh
                    kb.op("act", lambda e: e.activation(out=hd[:, h, :], in_=psN[0:64, hh * 128:(hh + 1) * 128], func=AF.Copy, scale=s3[:, h:h + 1]),
                          reads=[psN, s3], writes=[hd])
                yield
            kb.dma("pool", hdst.ap()[tsl, :], hd[:].rearrange("p h v -> p (h v)"), reads=[hd], writes=[hdst], waw=False)
            yield

    def build(self, phases=("p1", "attn", "mlstm", "p3a", "p3b")):
        kb = self.kb
        with kb.phase():
            self.consts_phase()
            self.cast_weights(0)
            for L in range(self.n_layers):
                xin = self.xT if L == 0 else self.xbuf[(L - 1) % 2]
                xout = self.outT if L == self.n_layers - 1 else self.xbuf[L % 2]
                if "p1" in phases:
                    self.p1(L, xin, 0)
                    self.p1(L, xin, 1)
                self.pending_casts = []
                if L + 1 < self.n_layers:
                    for n, _, _ in WEIGHTS:
                        kb.wait_tile("pool", self.WB[L % 2][n])
                    self.pending_casts = self.cast_items(L + 1)
                if "attn" in phases:
                    self.attn(L)
                if "mlstm" in phases:
                    self.mlstm(L)
                if "p3a" in phases:
                    self.p3a(L, xin)
                if "p3b" in phases:
                    self.p3b(L, xout)
            kb.barrier()
        return self.nc


_PROG_CACHE = {}


def kernel(**inputs):
    x = np.asarray(inputs["x"], dtype=np.float32)
    B = x.shape[0]
    consts = make_consts()
    params = prep_params(inputs)
    if "nc" not in _PROG_CACHE:
        _PROG_CACHE["nc"] = Prog().build()
    nc = _PROG_CACHE["nc"]
    shared = {}
    for n, _, _ in WEIGHTS:
        shared[n] = np.ascontiguousarray(inputs[n], dtype=np.float32)
    shared.update(params)
    shared.update(consts)
    in_maps = []
    active = [0, 1, 4, 5]
    zero_x = np.zeros((D, S), np.float32)
    for c in range(NCORES):
        m = dict(shared)
        m["xT"] = np.ascontiguousarray(x[active.index(c)].T) if c in active else zero_x
        in_maps.append(m)
    res = run_bass_kernel_spmd(nc, in_maps, core_ids=list(range(NCORES)))
    out = np.stack([np.ascontiguousarray(res.results[active[b]]["outT"].T) for b in range(B)], axis=0)
    return out.astype(np.float32)
```
